# Optimizing a Trainium2 kernel written in Bass

```python
import jax, jax.numpy as jnp
from jax import lax
import numpy as np

D_MODEL = 1024
BATCH = 8
SEQ = 2048
DEPTH = 4
DEC_BATCH = 128
DEC_SEQ = 8
PAST_LEN = 16384
PAGE_SIZE = 128

A_WIDTH = D_MODEL // 2
A_HEAD = 64
A_HEADS = A_WIDTH // A_HEAD
A_W_RANK = 64
A_A_RANK = 64
A_G_RANK = 128
A_COLS = 3 * A_WIDTH + A_W_RANK + A_A_RANK + A_G_RANK
GN_EPS = 64e-5
B_WIDTH = D_MODEL // 2
POOL_WINDOWS = (2, 4, 8, 16)
B_GROUPS = len(POOL_WINDOWS)
B_GROUP = B_WIDTH // B_GROUPS
POOL_BUF = max(POOL_WINDOWS) - 1
B_COLS = B_WIDTH
C_WIDTH = D_MODEL // 2
C_HEAD = 128
C_HEADS = C_WIDTH // C_HEAD
CONV_W = 4
CHUNK = 64
C_COLS = 4 * C_WIDTH + 2 * C_HEADS
N_MEM = 256
M_HEADS = 4
M_HEAD = 64
M_WIDTH = M_HEADS * M_HEAD
M_COLS = M_WIDTH
N_BRANCH = 4
G_COLS = N_BRANCH * D_MODEL
N_IN = A_COLS + B_COLS + C_COLS + M_COLS + G_COLS
BR_WIDTH = A_WIDTH + B_WIDTH + C_WIDTH + M_WIDTH
D_FF = 2 * D_MODEL
ALPHA = (2.0 * DEPTH) ** 0.25
BETA = (8.0 * DEPTH) ** -0.25
LN_EPS = 1e-5
F32 = jnp.float32

kernel_name = 'hybrid_rwkv7_pool_gdn_memory_decoder'


def layer_norm(x, g, b):
    xf = x.astype(F32)
    mu = jnp.mean(xf, axis=-1, keepdims=True)
    var = jnp.mean(jnp.square(xf - mu), axis=-1, keepdims=True)
    return ((xf - mu) * lax.rsqrt(var + LN_EPS) * g.astype(F32) + b.astype(F32)).astype(x.dtype)


def l2_normalize(x):
    xf = x.astype(F32)
    return xf * lax.rsqrt(jnp.sum(xf * xf, axis=-1, keepdims=True) + 1e-12)


def swiglu(x, w_in, w_out):
    gate, up = jnp.split(x @ w_in, 2, axis=-1)
    return (jax.nn.silu(gate) * up) @ w_out


def rwkv7_branch(pa, shift_prev, s0, mu, w0, w_up, a0, a_up, g_up, k_k, k_a, r_k, gn_w, gn_b):
    bsz, seq, _ = pa.shape
    prev = jnp.concatenate([shift_prev[:, None, :], pa[:, :-1]], axis=1)
    xm = pa + (prev - pa) * mu
    r, k, v, xw, xa, xg = jnp.split(
        xm, [A_WIDTH, 2 * A_WIDTH, 3 * A_WIDTH, 3 * A_WIDTH + A_W_RANK, 3 * A_WIDTH + A_W_RANK + A_A_RANK], axis=-1)
    z = (w0 + jnp.tanh(xw) @ w_up).astype(F32)
    decay = jnp.exp(-jnp.exp(-jax.nn.softplus(-z) - 0.5))
    a = jax.nn.sigmoid((a0 + xa @ a_up).astype(F32))
    g = jax.nn.sigmoid(xg) @ g_up
    heads = lambda t: t.astype(F32).reshape(bsz, seq, A_HEADS, A_HEAD)
    kk = l2_normalize(heads(k * k_k))
    k = heads(k * (1.0 + (a - 1.0) * k_a))
    r, v, decay, a = heads(r), heads(v), heads(decay), heads(a)

    def step(s, inp):
        r_t, w_t, k_t, v_t, kk_t, a_t = inp
        sa = jnp.einsum('bhvk,bhk->bhv', s, kk_t)
        s = (s * w_t[:, :, None, :] - sa[..., None] * (kk_t * a_t)[:, :, None, :]
             + v_t[..., None] * k_t[:, :, None, :])
        return s, jnp.einsum('bhvk,bhk->bhv', s, r_t)

    xs = tuple(jnp.moveaxis(t, 1, 0) for t in (r, decay, k, v, kk, a))
    s_new, y = lax.scan(step, s0.astype(F32), xs)
    y = jnp.moveaxis(y, 0, 1)
    mean = jnp.mean(y, axis=-1, keepdims=True)
    var = jnp.mean(jnp.square(y - mean), axis=-1, keepdims=True)
    y = ((y - mean) * lax.rsqrt(var + GN_EPS)).reshape(bsz, seq, A_WIDTH) * gn_w + gn_b
    bonus = jnp.sum(r * k * r_k.astype(F32), axis=-1, keepdims=True) * v
    out = (y + bonus.reshape(bsz, seq, A_WIDTH)) * g
    return out.astype(pa.dtype), pa[:, -1], s_new.astype(s0.dtype)


def pool_branch(pb, buf, pool_w, pool_scale, pos0):
    bsz, seq, _ = pb.shape
    ext = jnp.concatenate([buf, pb], axis=1)
    cs = jnp.pad(jnp.cumsum(ext.astype(F32), axis=1), ((0, 0), (1, 0), (0, 0)))
    t = jnp.arange(seq)
    means = []
    for j, w in enumerate(POOL_WINDOWS):
        sl = slice(j * B_GROUP, (j + 1) * B_GROUP)
        hi = cs[:, POOL_BUF + 1:POOL_BUF + 1 + seq, sl]
        lo = cs[:, POOL_BUF + 1 - w:POOL_BUF + 1 - w + seq, sl]
        count = jnp.minimum(pos0 + t + 1, w).astype(F32)[None, :, None]
        means.append((hi - lo) / count)
    pooled = jnp.concatenate(means, axis=-1) - pb.astype(F32)
    out = jnp.einsum('blgc,gcd->blgd', pooled.reshape(bsz, seq, B_GROUPS, B_GROUP), pool_w.astype(F32))
    out = out.reshape(bsz, seq, B_WIDTH) * pool_scale
    return out.astype(pb.dtype), ext[:, -POOL_BUF:]


def gated_delta_chunked(q, k, v, beta, g, s0):
    bsz, seq, nh, dk = q.shape
    pad = (-seq) % CHUNK
    n = (seq + pad) // CHUNK

    def to_blocks(t):
        t = jnp.pad(t.astype(F32), [(0, 0), (0, pad)] + [(0, 0)] * (t.ndim - 2))
        t = t.reshape((bsz, n, CHUNK) + t.shape[2:])
        return jnp.moveaxis(t, (1, 3), (0, 2))

    qb, kb, vb, bb, gb = to_blocks(q), to_blocks(k), to_blocks(v), to_blocks(beta), to_blocks(g)
    cum = jnp.cumsum(gb, axis=-1)
    incl = jnp.tril(jnp.ones((CHUNK, CHUNK), dtype=bool))
    strict = jnp.tril(jnp.ones((CHUNK, CHUNK), dtype=bool), -1)
    diff = cum[..., :, None] - cum[..., None, :]
    decay = jnp.where(incl, jnp.exp(jnp.where(incl, diff, 0.0)), 0.0)
    k_beta = kb * bb[..., None]
    lmat = jnp.where(strict, jnp.einsum('nbhcd,nbhsd->nbhcs', k_beta, kb) * decay, 0.0)
    eye = jnp.eye(CHUNK, dtype=F32)
    tmat = lax.linalg.triangular_solve(eye + lmat, jnp.broadcast_to(eye, lmat.shape),
                                       left_side=True, lower=True, unit_diagonal=True)
    u = tmat @ (vb * bb[..., None])
    w = tmat @ (k_beta * jnp.exp(cum)[..., None])
    aqk = jnp.einsum('nbhcd,nbhsd->nbhcs', qb, kb) * decay
    g_last = cum[..., -1]
    k_dec = kb * jnp.exp(g_last[..., None] - cum)[..., None]
    q_dec = qb * jnp.exp(cum)[..., None]

    def step(s, inp):
        u_i, w_i, a_i, qd_i, kd_i, gl_i = inp
        v_new = u_i - w_i @ s
        o = qd_i @ s + a_i @ v_new
        s = s * jnp.exp(gl_i)[..., None, None] + jnp.einsum('bhcd,bhce->bhde', kd_i, v_new)
        return s, o

    s_new, o = lax.scan(step, s0, (u, w, aqk, q_dec, k_dec, g_last))
    o = jnp.moveaxis(o, (0, 2), (1, 3)).reshape(bsz, n * CHUNK, nh, vb.shape[-1])[:, :seq]
    return o, s_new


def delta_branch(pc, conv_buf, s0, conv_w, a_log, dt_bias, norm_w):
    bsz, seq, _ = pc.shape
    qkv, zg, b, a = jnp.split(pc, [3 * C_WIDTH, 4 * C_WIDTH, 4 * C_WIDTH + C_HEADS], axis=-1)
    ext = jnp.concatenate([conv_buf, qkv], axis=1)
    conv = ext[:, 0:seq] * conv_w[0]
    for j in range(1, CONV_W):
        conv = conv + ext[:, j:j + seq] * conv_w[j]
    q, k, v = jnp.split(jax.nn.silu(conv), 3, axis=-1)
    heads = lambda t: t.reshape(bsz, seq, C_HEADS, C_HEAD)
    q = l2_normalize(heads(q)) * C_HEAD ** -0.5
    k = l2_normalize(heads(k))
    v = heads(v).astype(F32)
    beta = jax.nn.sigmoid(b.astype(F32))
    g = -jnp.exp(a_log.astype(F32)) * jax.nn.softplus(a.astype(F32) + dt_bias)
    o, s_new = gated_delta_chunked(q, k, v, beta, g, s0.astype(F32))
    o = o * lax.rsqrt(jnp.mean(o * o, axis=-1, keepdims=True) + 1e-6) * norm_w
    o = o * jax.nn.silu(heads(zg).astype(F32))
    return o.reshape(bsz, seq, C_WIDTH).astype(pc.dtype), ext[:, -(CONV_W - 1):], s_new.astype(s0.dtype)


def memory_attention(pm, mem_k, mem_v):
    bsz, seq, _ = pm.shape
    q = pm.reshape(bsz, seq, M_HEADS, M_HEAD)
    s = jnp.einsum('blhd,bmhd->bhlm', q, mem_k).astype(F32) * M_HEAD ** -0.5
    p = jax.nn.softmax(s, axis=-1)
    o = jnp.einsum('bhlm,bmhd->blhd', p.astype(mem_v.dtype), mem_v)
    return o.reshape(bsz, seq, M_WIDTH)


def token_mixing(h, mem_k, mem_v, shift, s_rwkv, pool_buf, conv_buf, s_delta, p, pos0):
    bsz, seq, _ = h.shape
    proj = h @ p['w_in']
    pa, pb, pc, pm, pg = jnp.split(
        proj, [A_COLS, A_COLS + B_COLS, A_COLS + B_COLS + C_COLS, A_COLS + B_COLS + C_COLS + M_COLS], axis=-1)
    oa, shift_new, s_rwkv_new = rwkv7_branch(
        pa, shift, s_rwkv, p['rwkv_mu'], p['rwkv_w0'], p['rwkv_w_up'], p['rwkv_a0'], p['rwkv_a_up'],
        p['rwkv_g_up'], p['rwkv_k_k'], p['rwkv_k_a'], p['rwkv_r_k'], p['rwkv_gn_w'], p['rwkv_gn_b'])
    ob, pool_new = pool_branch(pb, pool_buf, p['pool_w'], p['pool_scale'], pos0)
    oc, conv_new, s_delta_new = delta_branch(
        pc, conv_buf, s_delta, p['delta_conv_w'], p['delta_a_log'], p['delta_dt_bias'], p['delta_norm_w'])
    om = memory_attention(pm, mem_k, mem_v)
    gates = jax.nn.sigmoid(pg.reshape(bsz, seq, N_BRANCH, D_MODEL).astype(F32)).astype(h.dtype)
    wb = p['w_branch']
    bounds = (0, A_WIDTH, A_WIDTH + B_WIDTH, A_WIDTH + B_WIDTH + C_WIDTH, BR_WIDTH)
    merged = None
    for i, o in enumerate((oa, ob, oc, om)):
        term = gates[:, :, i] * (o @ wb[bounds[i]:bounds[i + 1]])
        merged = term if merged is None else merged + term
    return merged @ p['w_out'], (shift_new, s_rwkv_new, pool_new, conv_new, s_delta_new)


def run_trunk(x, mem_k, mem_v, shift, s_rwkv, pool_buf, conv_buf, s_delta, params, pos0):
    new = []
    for l in range(DEPTH):
        p = params[l]
        x = layer_norm(ALPHA * x + 0.5 * swiglu(x, p['ffn1_w_in'], p['ffn1_w_out']), p['ln_g'][0], p['ln_b'][0])
        mix, st = token_mixing(x, mem_k[l], mem_v[l], shift[l], s_rwkv[l], pool_buf[l], conv_buf[l],
                               s_delta[l], p, pos0)
        x = layer_norm(ALPHA * x + mix, p['ln_g'][1], p['ln_b'][1])
        x = layer_norm(ALPHA * x + 0.5 * swiglu(x, p['ffn2_w_in'], p['ffn2_w_out']), p['ln_g'][2], p['ln_b'][2])
        new.append(st)
    shift_n, rwkv_n, pool_n, conv_n, delta_n = [jnp.stack([st[i] for st in new]) for i in range(5)]
    return x, shift_n, rwkv_n, pool_n, conv_n, delta_n


def setup_inputs(seed: int = 0) -> dict:
    key = jax.random.key(seed)
    keys = iter(jax.random.split(key, 64))

    def nrm(shape, scale):
        return scale * jax.random.normal(next(keys), shape, F32)

    def unif(shape, lo, hi):
        return jax.random.uniform(next(keys), shape, F32, lo, hi)

    L = DEPTH
    return {
        'x_prompt': nrm((BATCH, SEQ, D_MODEL), 1.0),
        'x_sample': nrm((DEC_BATCH, DEC_SEQ, D_MODEL), 1.0),
        'mem_prompt': nrm((BATCH, N_MEM, D_MODEL), 1.0),
        'cache_mem_k': nrm((L, DEC_BATCH, N_MEM, M_HEADS, M_HEAD), 1.0),
        'cache_mem_v': nrm((L, DEC_BATCH, N_MEM, M_HEADS, M_HEAD), 1.0),
        'state_rwkv': nrm((L, DEC_BATCH, A_HEADS, A_HEAD, A_HEAD), 0.1),
        'state_rwkv_shift': nrm((L, DEC_BATCH, A_COLS), 1.0),
        'state_pool': nrm((L, DEC_BATCH, POOL_BUF, B_WIDTH), 1.0),
        'state_delta': nrm((L, DEC_BATCH, C_HEADS, C_HEAD, C_HEAD), 0.1),
        'state_delta_conv': nrm((L, DEC_BATCH, CONV_W - 1, 3 * C_WIDTH), 1.0),
        'w_in': nrm((L, D_MODEL, N_IN), D_MODEL ** -0.5),
        'rwkv_mu': unif((L, A_COLS), 0.0, 1.0),
        'rwkv_w0': unif((L, A_WIDTH), -5.0, 1.0),
        'rwkv_w_up': nrm((L, A_W_RANK, A_WIDTH), 0.1),
        'rwkv_a0': nrm((L, A_WIDTH), 0.1),
        'rwkv_a_up': nrm((L, A_A_RANK, A_WIDTH), A_A_RANK ** -0.5),
        'rwkv_g_up': nrm((L, A_G_RANK, A_WIDTH), A_G_RANK ** -0.5),
        'rwkv_k_k': 0.85 + nrm((L, A_WIDTH), 0.02),
        'rwkv_k_a': 1.0 + nrm((L, A_WIDTH), 0.02),
        'rwkv_r_k': nrm((L, A_HEADS, A_HEAD), 0.1),
        'rwkv_gn_w': 1.0 + nrm((L, A_WIDTH), 0.02),
        'rwkv_gn_b': nrm((L, A_WIDTH), 0.02),
        'pool_w': nrm((L, B_GROUPS, B_GROUP, B_GROUP), B_GROUP ** -0.5),
        'pool_scale': 1.0 + nrm((L, B_WIDTH), 0.02),
        'delta_conv_w': nrm((L, CONV_W, 3 * C_WIDTH), 0.5),
        'delta_a_log': jnp.log(unif((L, C_HEADS), 1.0, 16.0)),
        'delta_dt_bias': nrm((L, C_HEADS), 0.1),
        'delta_norm_w': 1.0 + nrm((L, C_HEAD), 0.02),
        'mem_w_kv': nrm((L, D_MODEL, 2 * M_WIDTH), D_MODEL ** -0.5),
        'w_branch': nrm((L, BR_WIDTH, D_MODEL), BETA * A_WIDTH ** -0.5),
        'w_out': nrm((L, D_MODEL, D_MODEL), BETA * D_MODEL ** -0.5),
        'ffn1_w_in': nrm((L, D_MODEL, 2 * D_FF), D_MODEL ** -0.5),
        'ffn1_w_out': nrm((L, D_FF, D_MODEL), BETA * D_FF ** -0.5),
        'ffn2_w_in': nrm((L, D_MODEL, 2 * D_FF), D_MODEL ** -0.5),
        'ffn2_w_out': nrm((L, D_FF, D_MODEL), BETA * D_FF ** -0.5),
        'ln_g': 1.0 + nrm((L, 3, D_MODEL), 0.02),
        'ln_b': nrm((L, 3, D_MODEL), 0.02),
    }


def reference(x_prompt, x_sample, mem_prompt, cache_mem_k, cache_mem_v, state_rwkv, state_rwkv_shift,
              state_pool, state_delta, state_delta_conv, w_in, rwkv_mu, rwkv_w0, rwkv_w_up, rwkv_a0,
              rwkv_a_up, rwkv_g_up, rwkv_k_k, rwkv_k_a, rwkv_r_k, rwkv_gn_w, rwkv_gn_b, pool_w, pool_scale,
              delta_conv_w, delta_a_log, delta_dt_bias, delta_norm_w, mem_w_kv, w_branch, w_out,
              ffn1_w_in, ffn1_w_out, ffn2_w_in, ffn2_w_out, ln_g, ln_b):
    params = [dict(w_in=w_in[l], rwkv_mu=rwkv_mu[l], rwkv_w0=rwkv_w0[l], rwkv_w_up=rwkv_w_up[l],
                   rwkv_a0=rwkv_a0[l], rwkv_a_up=rwkv_a_up[l], rwkv_g_up=rwkv_g_up[l], rwkv_k_k=rwkv_k_k[l],
                   rwkv_k_a=rwkv_k_a[l], rwkv_r_k=rwkv_r_k[l], rwkv_gn_w=rwkv_gn_w[l], rwkv_gn_b=rwkv_gn_b[l],
                   pool_w=pool_w[l], pool_scale=pool_scale[l], delta_conv_w=delta_conv_w[l],
                   delta_a_log=delta_a_log[l], delta_dt_bias=delta_dt_bias[l], delta_norm_w=delta_norm_w[l],
                   w_branch=w_branch[l], w_out=w_out[l], ffn1_w_in=ffn1_w_in[l], ffn1_w_out=ffn1_w_out[l],
                   ffn2_w_in=ffn2_w_in[l], ffn2_w_out=ffn2_w_out[l], ln_g=ln_g[l], ln_b=ln_b[l])
              for l in range(DEPTH)]
    bp = x_prompt.shape[0]
    n_mem = mem_prompt.shape[1]
    dt = x_prompt.dtype
    mem_kv = jnp.einsum('bmd,lde->lbme', mem_prompt, mem_w_kv)
    p_mem_k = mem_kv[..., :M_WIDTH].reshape(DEPTH, bp, n_mem, M_HEADS, M_HEAD)
    p_mem_v = mem_kv[..., M_WIDTH:].reshape(DEPTH, bp, n_mem, M_HEADS, M_HEAD)
    y_prompt, p_rwkv_shift, p_rwkv, p_pool, p_delta_conv, p_delta = run_trunk(
        x_prompt, p_mem_k, p_mem_v,
        jnp.zeros((DEPTH, bp, A_COLS), dt),
        jnp.zeros((DEPTH, bp, A_HEADS, A_HEAD, A_HEAD), dt),
        jnp.zeros((DEPTH, bp, POOL_BUF, B_WIDTH), dt),
        jnp.zeros((DEPTH, bp, CONV_W - 1, 3 * C_WIDTH), dt),
        jnp.zeros((DEPTH, bp, C_HEADS, C_HEAD, C_HEAD), dt),
        params, 0)
    y_sample, s_rwkv_shift, s_rwkv, s_pool, s_delta_conv, s_delta = run_trunk(
        x_sample, cache_mem_k, cache_mem_v, state_rwkv_shift, state_rwkv, state_pool, state_delta_conv,
        state_delta, params, PAST_LEN)
    return (y_prompt, y_sample, p_rwkv, p_rwkv_shift, p_pool, p_delta, p_delta_conv, p_mem_k, p_mem_v,
            s_rwkv, s_rwkv_shift, s_pool, s_delta, s_delta_conv)
```

```python
import os
import numpy as np
from contextlib import ExitStack
import concourse.bass as bass
import concourse.mybir as mybir
from concourse.bass_utils import run_bass_kernel_spmd

F32 = mybir.dt.float32
BF16 = mybir.dt.bfloat16
AF = mybir.ActivationFunctionType
ALU = mybir.AluOpType

ENGS = ("pe", "act", "dve", "pool", "sp")


class _Op:
    __slots__ = ("fn", "raw", "oth", "needs_inc", "count", "dsem")

    def __init__(self, fn, raw, oth, dsem=None):
        self.fn = fn
        self.raw = raw
        self.oth = oth
        self.needs_inc = False
        self.count = 0
        self.dsem = dsem


class Prog:
    def __init__(self, nc):
        self.nc = nc
        self.ops = {e: [] for e in ENGS}
        self.last_w = {}
        self.readers = {}
        self.dcount = {}

    def _deps(self, reads, writes):
        raw = set()
        oth = set()
        lw = self.last_w
        for k in reads:
            t = lw.get(k)
            if t is not None:
                raw.add(t)
        for k in writes:
            t = lw.get(k)
            if t is not None:
                oth.add(t)
            r = self.readers.get(k)
            if r:
                oth.update(r.values())
        return raw, oth

    def _fix(self, deps):
        out = set()
        for t in deps:
            if t[0] == "D":
                out.add(("D", t[1], self.dcount[t[1]]))
            else:
                out.add(t)
        return out

    def _mark(self, tok, rk, reads, writes):
        for k in reads:
            d = self.readers.get(k)
            if d is None:
                d = self.readers[k] = {}
            d[rk] = tok
        for k in writes:
            self.last_w[k] = tok
            self.readers[k] = {}

    def op(self, eng, fn, reads=(), writes=()):
        self.nrec = getattr(self, "nrec", 0) + 1
        if self.nrec > int(os.environ.get("KCUT", "100000000")) or getattr(self, "cut", False):
            return None
        if eng != "pe":
            pk = [k for k in reads if isinstance(k, tuple) and k[0] == "ps"]
            if pk:
                writes = list(writes) + pk
        raw, oth = self._deps(reads, writes)
        o = _Op(fn, self._fix(raw), self._fix(oth))
        lst = self.ops[eng]
        tok = ("E", eng, len(lst))
        lst.append(o)
        self._mark(tok, eng, reads, writes)
        return o

    def dma(self, q, fn, reads=(), writes=(), sem="dma"):
        self.nrec = getattr(self, "nrec", 0) + 1
        if self.nrec > int(os.environ.get("KCUT", "100000000")) or getattr(self, "cut", False):
            return None
        raw, oth = self._deps(reads, writes)
        o = _Op(fn, self._fix(raw), self._fix(oth), dsem=sem)
        self.dcount[sem] = self.dcount.get(sem, 0) + 16
        tok = ("D", sem, self.dcount[sem])
        self.ops[q].append(o)
        self._mark(tok, ("D", sem), reads, writes)
        return o

    def ck(self, name):
        if os.environ.get("KSTOP", "") == name:
            self.cut = True

    def final_wait_all(self, q="sp"):
        deps = set(("D", k, c) for k, c in self.dcount.items())
        self.ops[q].append(_Op(None, deps, set()))

    def emit(self, es):
        nc = self.nc
        eff = {}
        for e in ENGS:
            for i, o in enumerate(self.ops[e]):
                ds = set()
                for t in o.raw:
                    if t[0] == "E" and t[1] == e and e == "pe":
                        continue
                    ds.add(t)
                for t in o.oth:
                    if t[0] == "E" and t[1] == e and e == "pe":
                        continue
                    ds.add(t)
                eff[(e, i)] = ds
                for t in ds:
                    if t[0] == "E":
                        self.ops[t[1]][t[2]].needs_inc = True
        for e in ENGS:
            c = 0
            for o in self.ops[e]:
                if o.dsem is None and o.needs_inc:
                    c += 1
                o.count = c
        esem = {e: es.enter_context(nc.semaphore("s_" + e)) for e in ENGS}
        dsem = {k: es.enter_context(nc.semaphore("d_%d" % i)) for i, k in enumerate(self.dcount)}
        block = es.enter_context(nc.Block())
        engobj = {"pe": block.tensor, "act": block.scalar, "dve": block.vector, "pool": block.gpsimd,
                  "sp": block.sync}
        self.n_wait = 0
        self.n_ins = 0

        def run(e):
            def body(eng):
                have = {}
                for i, o in enumerate(self.ops[e]):
                    need = {}
                    for t in eff[(e, i)]:
                        if t[0] == "E":
                            s = ("E", t[1])
                            v = self.ops[t[1]][t[2]].count
                        else:
                            s = ("D", t[1])
                            v = t[2]
                        if v > need.get(s, 0):
                            need[s] = v
                    for s, v in need.items():
                        if have.get(s, 0) >= v:
                            continue
                        have[s] = v
                        sem = esem[s[1]] if s[0] == "E" else dsem[s[1]]
                        eng.wait_ge(sem, v)
                        self.n_wait += 1
                    if o.fn is None:
                        continue
                    ins = o.fn(eng)
                    self.n_ins += 1
                    if o.dsem is not None:
                        ins.then_inc(dsem[o.dsem], 16)
                    elif o.needs_inc:
                        ins.then_inc(esem[e], 1)
            engobj[e](body)

        for e in ENGS:
            if self.ops[e]:
                run(e)


D = 1024
KC = 8
DEPTH = 4
SEQ = 2048
NST = 4
TP = 512
TS = 32
NSQ = 4
T = TP + TS
DFF = 2048
ALPHA = (2.0 * DEPTH) ** 0.25
LN_EPS = 1e-5
NSLOT = 4
SLOT_ELEMS = 4096
A_COLS = 1792
C0 = float(np.exp(-0.5))
OFF_B = 1792
OFF_C = 2304
OFF_M = 4360
OFF_G = 4616
N_IN = 8712
NPC = 176
NF = 22
NB = 41
FW = 560

DBG = set(x for x in os.environ.get("KDBG", "").split(",") if x)
ACTIVE = os.environ.get("KACTIVE", "abcm")
NLAYERS = int(os.environ.get("KLAYERS", "4"))

C_ID, C_ONE, C_MUS, C_MLS, C_MUI, C_SCAN, C_BLK, C_PFIX, C_SEL4, C_NMUS, C_NMLS = 0, 128, 256, 320, 384, 448, 992, 1120, 1184, 1696, 1760
NCST = 1824
NCSTB = 1184

OUT_SPECS = [
    ("y_prompt", [SEQ, D]), ("y_sample", [128, D]),
    ("p_rwkv", [DEPTH, 8, 64, 64]), ("p_rwkv_shift", [DEPTH, 14, 128]), ("p_pool", [DEPTH, 15, 512]),
    ("p_delta", [DEPTH, 4, 128, 128]), ("p_delta_conv", [DEPTH, 3, 1536]),
    ("p_mem_k", [DEPTH, 256, 256]), ("p_mem_v", [DEPTH, 256, 256]),
    ("s_rwkv", [DEPTH, 16, 8, 64, 64]), ("s_rwkv_shift", [DEPTH, 16, 14, 128]), ("s_pool", [DEPTH, 16, 15, 512]),
    ("s_delta", [DEPTH, 16, 4, 128, 128]), ("s_delta_conv", [DEPTH, 16, 3, 1536]),
]
IN_SPECS = [
    ("x_prompt", [SEQ, D]), ("x_sample", [128, D]), ("mem_prompt", [256, D]),
    ("cache_mem_k", [DEPTH, 16, 256, 256]), ("cache_mem_v", [DEPTH, 16, 256, 256]),
    ("state_rwkv", [DEPTH, 16, 8, 64, 64]), ("state_rwkv_shift", [DEPTH, 16, 14, 128]),
    ("state_pool", [DEPTH, 16, 15, 512]), ("state_delta", [DEPTH, 16, 4, 128, 128]),
    ("state_delta_conv", [DEPTH, 16, 3, 1536]),
    ("w_in", [DEPTH, D, N_IN]), ("rwkv_mu", [DEPTH, A_COLS]), ("rwkv_w0", [DEPTH, 512]),
    ("rwkv_w_up", [DEPTH, 64, 512]), ("rwkv_a0", [DEPTH, 512]), ("rwkv_a_up", [DEPTH, 64, 512]),
    ("rwkv_g_up", [DEPTH, 128, 512]), ("rwkv_k_k", [DEPTH, 512]), ("rwkv_k_a", [DEPTH, 512]),
    ("rwkv_r_k", [DEPTH, 512]), ("rwkv_gn_w", [DEPTH, 512]), ("rwkv_gn_b", [DEPTH, 512]),
    ("pool_w", [DEPTH, 4, 128, 128]), ("pool_scale", [DEPTH, 512]), ("delta_conv_w", [DEPTH, 4, 1536]),
    ("delta_a_log", [DEPTH, 4]), ("delta_dt_bias", [DEPTH, 4]), ("delta_norm_w", [DEPTH, 128]),
    ("mem_w_kv", [DEPTH, D, 512]), ("w_branch", [DEPTH, 1792, D]), ("w_out", [DEPTH, D, D]),
    ("ffn1_w_in", [DEPTH, D, 2 * DFF]), ("ffn1_w_out", [DEPTH, DFF, D]),
    ("ffn2_w_in", [DEPTH, D, 2 * DFF]), ("ffn2_w_out", [DEPTH, DFF, D]),
    ("ln_g", [DEPTH, 3, D]), ("ln_b", [DEPTH, 3, D]), ("consts", [128, NCST]),
]


def make_consts():
    c = np.zeros((128, NCST), np.float32)
    c[:, C_ID:C_ID + 128] = np.eye(128, dtype=np.float32)
    c[:, C_ONE:C_ONE + 128] = 1.0
    p = np.arange(64)[:, None]
    f = np.arange(64)[None, :]
    c[0:64, C_MUS:C_MUS + 64] = (p < f)
    c[0:64, C_MLS:C_MLS + 64] = (f < p)
    c[0:64, C_MUI:C_MUI + 64] = (p <= f)
    c[0:64, C_NMUS:C_NMUS + 64] = -1.0 * (p < f)
    c[0:64, C_NMLS:C_NMLS + 64] = -1.0 * (f < p)
    t = np.arange(T)
    scan = np.ones(T, np.float32)
    scan[:TP][t[:TP] % 64 == 0] = 0.0
    scan[TP:][(t[TP:] - TP) % 8 == 0] = 0.0
    c[:, C_SCAN:C_SCAN + T] = scan[None, :]
    blk = np.zeros((128, 128), np.float32)
    blk[0:64, 0:64] = 1.0
    blk[64:128, 64:128] = 1.0
    c[:, C_BLK:C_BLK + 128] = blk
    for g, w in enumerate((2, 4, 8, 16)):
        tt = np.arange(16)
        c[:, C_PFIX + 16 * g:C_PFIX + 16 * g + 16] = (w / np.minimum(tt + 1, w))[None, :]
    for h in range(4):
        c[h, C_SEL4 + 128 * h:C_SEL4 + 128 * h + 128] = 1.0
    return c


def build():
    nc = bass.Bass("TRN2", target_bir_lowering=False)
    es = ExitStack()
    P = Prog(nc)
    I = {n: nc.dram_tensor(n, list(s), F32, kind="ExternalInput").ap() for n, s in IN_SPECS}
    O = {n: nc.dram_tensor(n, list(s), F32, kind="ExternalOutput").ap() for n, s in OUT_SPECS}

    def sb(name, shape, dt=F32):
        return es.enter_context(nc.sbuf_tensor(name, list(shape), dt))

    xf = sb("xf", [128, KC, T])
    xb = sb("xb", [128, KC, T], BF16)
    mg = sb("mg", [128, KC, T])
    cst = sb("cst", [128, NCST])
    cstb = sb("cstb", [128, NCSTB], BF16)
    wring = sb("wring", [128, NSLOT, SLOT_ELEMS], BF16)
    SFt = [sb("sf%d" % i, [128, FW]) for i in range(NF)]
    SBt = [sb("sb%d" % i, [128, T], BF16) for i in range(NB)]
    stg = [sb("stg%d" % i, [128, 128]) for i in range(2)]

    class _XV:
        def __init__(self, t0):
            self.t0 = t0

        def cols(self, i, a, b, rows=slice(0, 128)):
            h = a // 512
            assert (b - 1) // 512 == h
            return SFt[self.t0 + 2 * i + h][rows, a - 512 * h:b - 512 * h]

        def keys(self, i):
            return [("sf", self.t0 + 2 * i), ("sf", self.t0 + 2 * i + 1)]
    xin = _XV(0)
    xin_s = _XV(4)
    pcol = sb("pcol", [128, DEPTH, NPC])
    carry_pa = sb("carry_pa", [128, DEPTH, 14])
    carry_pb = sb("carry_pb", [128, DEPTH, 4, 15])
    carry_pc = sb("carry_pc", [128, DEPTH, 12, 3])
    Srw = sb("Srw", [128, DEPTH, 4, 64])
    Sgd = sb("Sgd", [128, DEPTH, 4, 128])
    memK = sb("memK", [128, DEPTH, 2, 256], BF16)
    memV = sb("memV", [128, DEPTH, 2, 256], BF16)
    gparm = sb("gparm", [4, DEPTH, 2])
    mpT = sb("mpT", [128, KC, 256], BF16)
    pall = es.enter_context(nc.psum_tensor("pall", [128, 4096], F32))
    pallb = pall.bitcast(BF16)

    ident = cst[:, C_ID:C_ID + 128]
    identb = cstb[:, C_ID:C_ID + 128]
    onesb = cstb[:, C_ONE:C_ONE + 128]
    blkb = cstb[:, C_BLK:C_BLK + 128]

    def SF(i):
        return SFt[i], ("sf", i)

    def SB(i):
        return SBt[i], ("sb", i)

    def bank(b):
        return pall[:, 512 * b:512 * b + 512]

    def dbv(i):
        return pall[:, 1024 * i:1024 * i + T]

    def dbk(i):
        return [("ps", 2 * i), ("ps", 2 * i + 1)]

    OP = P.op

    def mm(out, lhsT, rhs, start, stop, reads, writes):
        P.op("pe", lambda e: e.matmul(out, lhsT, rhs, start=start, stop=stop), reads=reads, writes=writes)

    def tr(out, in_, idn, reads, writes):
        P.op("pe", lambda e: e.transpose(out, in_, idn), reads=reads, writes=writes)

    def act(out, in_, func, reads, writes, bias=None, scale=None, accum=None):
        kw = {}
        if bias is not None:
            kw["bias"] = bias
        if scale is not None:
            kw["scale"] = scale
        if accum is not None:
            kw["accum_out"] = accum
        P.op("act", lambda e: e.activation(out, in_, func, **kw), reads=reads, writes=writes)

    def tt(out, a, b, op, reads, writes, eng="dve"):
        P.op(eng, lambda e: e.tensor_tensor(out, a, b, op), reads=reads, writes=writes)

    def stt(out, a, s, b, op0, op1, reads, writes):
        P.op("dve", lambda e: e.scalar_tensor_tensor(out, a, s, b, op0, op1), reads=reads, writes=writes)

    def ts(out, a, s1, s2, op0, op1, reads, writes, eng="dve"):
        if op1 is None:
            P.op(eng, lambda e: e.tensor_scalar(out, a, s1, None, op0), reads=reads, writes=writes)
        else:
            P.op(eng, lambda e: e.tensor_scalar(out, a, s1, s2, op0, op1), reads=reads, writes=writes)

    def cp(out, in_, reads, writes, eng="dve"):
        if eng == "act":
            P.op("act", lambda e: e.activation(out, in_, AF.Copy), reads=reads, writes=writes)
        else:
            P.op(eng, lambda e: e.tensor_copy(out, in_), reads=reads, writes=writes)

    def memset(ap, v, writes, eng="dve"):
        P.op(eng, lambda e: e.memset(ap, v), writes=writes)

    P.dma("sp", lambda e: e.dma_start(out=cst[:], in_=I["consts"]), writes=["cst"], sem="c0")
    cp(cstb[:], cst[:, 0:NCSTB], ["cst"], ["cstb"])
    memset(carry_pa[:], 0.0, ["carry_pa"])
    memset(carry_pb[:], 0.0, ["carry_pb"])
    memset(carry_pc[:], 0.0, ["carry_pc"])
    memset(Srw[:], 0.0, ["Srw"])
    memset(Sgd[:], 0.0, ["Sgd"])
    for i in range(2):
        memset(stg[i][:], 0.0, [("stg", i)])
    P.dma("sp", lambda e: e.dma_start(out=gparm[:, :, 0:1], in_=I["delta_a_log"].rearrange("l (h o) -> h l o", o=1), allow_slow_non_contiguous=True),
          writes=["gparm"], sem="c1")
    P.dma("sp", lambda e: e.dma_start(out=gparm[:, :, 1:2], in_=I["delta_dt_bias"].rearrange("l (h o) -> h l o", o=1), allow_slow_non_contiguous=True),
          writes=["gparm"], sem="c1")

    PC = {}
    _o = [0]

    def _reg(name, n):
        PC[name] = _o[0]
        _o[0] += n
    for nm, n in (("ln_g", 24), ("ln_b", 24), ("mu", 14), ("w0", 4), ("a0", 4), ("k_k", 4), ("k_a", 4), ("r_k", 4),
                  ("gn_w", 4), ("gn_b", 4), ("pscale", 4), ("conv", 48), ("nrm", 1), ("omka", 4), ("nmu", 14)):
        _reg(nm, n)
    assert _o[0] <= NPC
    stgn = [0]

    def rows_to_cols(l, items):
        groups = []
        cur = []
        cnt = 0
        for it in items:
            if cnt + it[1] > 128:
                groups.append(cur)
                cur = []
                cnt = 0
            cur.append((cnt, it))
            cnt += it[1]
        groups.append(cur)
        for gidx, g in enumerate(groups):
            si = stgn[0] % 2
            stgn[0] += 1
            st_t = stg[si]
            for (r0, (ap, n, name)) in g:
                P.dma("sp", lambda e, ap=ap, r0=r0, n=n, st_t=st_t: e.dma_start(out=st_t[r0:r0 + n, :], in_=ap),
                      writes=[("stg", si)], sem=("stg", si))
            tr(bank(7)[:, 0:128], st_t[:, :], ident, [("stg", si), "cst"], [("ps", 7)])
            for (r0, (ap, n, name)) in g:
                cp(pcol[:, l, PC[name]:PC[name] + n], bank(7)[:, r0:r0 + n], [("ps", 7)], [("pcol", l)])

    for l in range(DEPTH):
        r128 = lambda ap: ap.rearrange("(k p) -> k p", p=128)
        items = [
            (I["ln_g"][l].rearrange("i (k p) -> (i k) p", p=128), 24, "ln_g"),
            (I["ln_b"][l].rearrange("i (k p) -> (i k) p", p=128), 24, "ln_b"),
            (r128(I["rwkv_mu"][l]), 14, "mu"), (r128(I["rwkv_w0"][l]), 4, "w0"), (r128(I["rwkv_a0"][l]), 4, "a0"),
            (r128(I["rwkv_k_k"][l]), 4, "k_k"), (r128(I["rwkv_k_a"][l]), 4, "k_a"), (r128(I["rwkv_r_k"][l]), 4, "r_k"),
            (r128(I["rwkv_gn_w"][l]), 4, "gn_w"), (r128(I["rwkv_gn_b"][l]), 4, "gn_b"),
            (r128(I["pool_scale"][l]), 4, "pscale"),
            (I["delta_conv_w"][l].rearrange("j (k p) -> (j k) p", p=128), 48, "conv"),
            (I["delta_norm_w"][l].rearrange("(k p) -> k p", p=128), 1, "nrm"),
        ]
        rows_to_cols(l, items)
        ts(pcol[:, l, PC["omka"]:PC["omka"] + 4], pcol[:, l, PC["k_a"]:PC["k_a"] + 4], -1.0, 1.0, ALU.mult, ALU.add,
           [("pcol", l)], [("pcol", l)])
        ts(pcol[:, l, PC["nmu"]:PC["nmu"] + 14], pcol[:, l, PC["mu"]:PC["mu"] + 14], -1.0, 1.0, ALU.mult, ALU.add,
           [("pcol", l)], [("pcol", l)])

    def pc_(l, name, j=0):
        return pcol[:, l, PC[name] + j:PC[name] + j + 1]

    wstate = {"n": 0}

    def wslot():
        s = wstate["n"] % NSLOT
        wstate["n"] += 1
        return s

    def wload(src2d, K, ncols):
        kc = K // 128
        assert kc * ncols <= SLOT_ELEMS and ncols <= 2048
        s = wslot()
        view = wring[:, s, 0:kc * ncols].rearrange("p (k n) -> p k n", k=kc)
        P.dma("pool", lambda e: e.dma_start(out=view, in_=src2d.rearrange("(k p) n -> p k n", p=128)),
              writes=[("w", s)], sem=("w", s))
        return view, ("w", s)

    def dense_chunk(db, wv, wk, kcn, col, rhs_fn, rhs_keys):
        for (o0, n, t0, bk) in ((0, TP, 0, 0), (TP, TS, TP, 1)):
            for k in range(kcn):
                mm(pall[:, 1024 * db + o0:1024 * db + o0 + n], wv[:, k, col:col + 128], rhs_fn(k)[:, t0:t0 + n],
                   k == 0, k == kcn - 1, [wk, rhs_keys[k]], [("ps", 2 * db + bk)])

    xb_fn = lambda k: xb[:, k, :]
    xb_keys = [("xb", k) for k in range(KC)]

    def out_proj_ln(l, lni, wsrc, K, rhs_fn, rhs_keys, cres):
        kcn = K // 128
        ncol = SLOT_ELEMS // kcn
        if ncol > 1024:
            ncol = 1024
        per = ncol // 128
        ybf, kyb = SB(NB - 1)
        ysq, kys = SB(NB - 2)
        for j in range(D // ncol):
            wo, ko = wload(wsrc[:, ncol * j:ncol * j + ncol], K, ncol)
            for i in range(per):
                m = per * j + i
                pi = m % 2
                dense_chunk(pi, wo, ko, kcn, 128 * i, rhs_fn, rhs_keys)
                stt(xf[:, m, :], dbv(pi), cres, xf[:, m, :], ALU.mult, ALU.add, dbk(pi) + [("xf", m)], [("xf", m)])
                act(ybf[:], xf[:, m, :], AF.Copy, [("xf", m)], [kyb])
                act(ysq[:], xf[:, m, :], AF.Square, [("xf", m)], [kys])
                for (o0, n, bk) in ((0, TP, 0), (TP, TS, 1)):
                    mm(pall[:, 2048 + o0:2048 + o0 + n], onesb, ybf[:, o0:o0 + n], m == 0, m == 7, ["cstb", kyb],
                       [("ps", 4 + bk)])
                    mm(pall[:, 3072 + o0:3072 + o0 + n], onesb, ysq[:, o0:o0 + n], m == 0, m == 7, ["cstb", kys],
                       [("ps", 6 + bk)])
        eps = LN_EPS / (ALPHA * ALPHA)
        (mean, kmean), (msq, kmsq), (var, kvar), (rstd, krstd), (nmr, knmr) = [SF(NF - 1 - i) for i in range(5)]
        W = slice(0, T)
        act(mean[:, W], dbv(2), AF.Copy, dbk(2), [kmean], scale=1.0 / D)
        tt(msq[:, W], mean[:, W], mean[:, W], ALU.mult, [kmean], [kmsq])
        stt(var[:, W], dbv(3), 1.0 / D, msq[:, W], ALU.mult, ALU.subtract, dbk(3) + [kmsq], [kvar])
        act(var[:, W], var[:, W], AF.Sqrt, [kvar], [kvar], bias=eps)
        OP("dve", lambda e: e.reciprocal(rstd[:, W], var[:, W]), reads=[kvar], writes=[krstd])
        stt(nmr[:, W], mean[:, W], -1.0, rstd[:, W], ALU.mult, ALU.mult, [kmean, krstd], [knmr])
        for m in range(KC):
            ta, kta = SF(NF - 6 - (m % 2))
            tt(ta[:, W], xf[:, m, :], rstd[:, W], ALU.mult, [("xf", m), krstd], [kta])
            tt(ta[:, W], ta[:, W], nmr[:, W], ALU.add, [kta, knmr], [kta])
            gcol = pc_(l, "ln_g", lni * 8 + m)
            bcol = pc_(l, "ln_b", lni * 8 + m)
            act(xf[:, m, :], ta[:, W], AF.Identity, [kta, ("pcol", l)], [("xf", m)], bias=bcol, scale=gcol)
            act(xb[:, m, :], ta[:, W], AF.Identity, [kta, ("pcol", l)], [("xb", m)], bias=bcol, scale=gcol)

    def ffn_ln(l, w_in, w_out, lni):
        for j in range(4):
            wg, kg = wload(w_in[l][:, 512 * j:512 * j + 512], D, 512)
            wu, ku = wload(w_in[l][:, DFF + 512 * j:DFF + 512 * j + 512], D, 512)
            for i in range(4):
                c = 4 * j + i
                dense_chunk(0, wg, kg, KC, 128 * i, xb_fn, xb_keys)
                dense_chunk(1, wu, ku, KC, 128 * i, xb_fn, xb_keys)
                sgt, ksg = SF(c % 2)
                hc, khc = SB(c)
                act(sgt[:, 0:T], dbv(0), AF.Silu, dbk(0), [ksg])
                tt(hc[:], sgt[:, 0:T], dbv(1), ALU.mult, dbk(1) + [ksg], [khc])
        out_proj_ln(l, lni, w_out[l], DFF, lambda c: SBt[c][:], [("sb", c) for c in range(16)], 0.5 / ALPHA)

    def proj_cols(l, col0, ncols, consume):
        done = 0
        ci = 0
        while done < ncols:
            n = min(512, ncols - done)
            wv, wk = wload(I["w_in"][l][:, col0 + done:col0 + done + n], D, n)
            for i in range((n + 127) // 128):
                w = min(128, n - 128 * i)
                db = ci % 2
                for (o0, nn, t0, bk) in ((0, TP, 0, 0), (TP, TS, TP, 1)):
                    for k in range(KC):
                        mm(pall[0:w, 1024 * db + o0:1024 * db + o0 + nn], wv[:, k, 128 * i:128 * i + w],
                           xb[:, k, t0:t0 + nn], k == 0, k == KC - 1, [wk, ("xb", k)], [("ps", 2 * db + bk)])
                consume(ci, db, w)
                ci += 1
            done += n

    def merge_branch(l, bi, o_fn, o_keys, nk, r0):
        ncol = min(1024, SLOT_ELEMS // nk)
        for j in range(D // ncol):
            wb, kb = wload(I["w_branch"][l][r0:r0 + 128 * nk, ncol * j:ncol * j + ncol], 128 * nk, ncol)
            for jj in range(ncol // 512):
                gcol0 = OFF_G + bi * D + ncol * j + 512 * jj
                wg, kg = wload(I["w_in"][l][:, gcol0:gcol0 + 512], D, 512)
                for i in range(4):
                    m = (ncol * j + 512 * jj) // 128 + i
                    dense_chunk(0, wg, kg, KC, 128 * i, xb_fn, xb_keys)
                    dense_chunk(1, wb, kb, nk, 512 * jj + 128 * i, o_fn, o_keys)
                    gt, kgt = SF(NF - 8 - (m % 2))
                    act(gt[:, 0:T], dbv(0), AF.Sigmoid, dbk(0), [kgt])
                    tt(gt[:, 0:T], gt[:, 0:T], dbv(1), ALU.mult, dbk(1) + [kgt], [kgt])
                    tt(mg[:, m, :], mg[:, m, :], gt[:, 0:T], ALU.add, [kgt, ("mg", m)], [("mg", m)])

    def branch_pool(l, st):
        b0 = 4 * st
        pbE = [SF(i) for i in range(4)]
        pbS = [SF(4 + i) for i in range(4)]
        tmp = [SF(8), SF(9)]
        for g in range(4):
            cp(pbE[g][0][:, 0:15], carry_pb[:, l, g, :], ["carry_pb"], [pbE[g][1]])
        for half in range(2):
            si = stgn[0] % 2
            stgn[0] += 1
            for q in range(2):
                s = 2 * half + q
                P.dma("sp", lambda e, s=s, q=q, si=si: e.dma_start(
                    out=stg[si][60 * q:60 * q + 60, :],
                    in_=I["state_pool"][l, b0 + s].rearrange("j (c p) -> (j c) p", p=128)),
                    writes=[("stg", si)], sem=("stg", si))
            tr(bank(7)[:, 0:128], stg[si][:, :], ident, [("stg", si), "cst"], [("ps", 7)])
            for q in range(2):
                s = 2 * half + q
                for g in range(4):
                    src = bank(7)[:, 60 * q:60 * q + 60].rearrange("p (j c) -> p c j", c=4)[:, g, :]
                    cp(pbS[g][0][:, 23 * s:23 * s + 15], src, [("ps", 7)], [pbS[g][1]])

        def consume(ci, db, w):
            g = ci
            cp(pbE[g][0][:, 15:15 + TP], pall[:, 1024 * db:1024 * db + TP], [("ps", 2 * db)], [pbE[g][1]], eng="act")
            cp(pbS[g][0][:, 0:92].rearrange("p (s j) -> p s j", j=23)[:, :, 15:23],
               pall[:, 1024 * db + TP:1024 * db + T].rearrange("p (s j) -> p s j", j=8),
               [("ps", 2 * db + 1)], [pbS[g][1]])
        proj_cols(l, OFF_B, 512, consume)
        pooled = [SB(16 + g) for g in range(4)]
        obT = [SB(20 + g) for g in range(4)]
        pw, kpw = None, None
        s_ = wslot()
        pw = wring[:, s_, 0:512].rearrange("p (g d) -> p g d", g=4)
        kpw = ("w", s_)
        P.dma("pool", lambda e: e.dma_start(out=pw, in_=I["pool_w"][l].rearrange("g c d -> c g d")),
              writes=[kpw], sem=kpw)
        for g in range(4):
            E, kE = pbE[g]
            S_, kS = pbS[g]
            Sv = lambda t_, a, b: t_[:, 0:92].rearrange("p (s j) -> p s j", j=23)[:, :, a:b]
            src, ksrc = E, kE
            srcs, ksrcs = S_, kS
            sh = 1
            for step in range(g + 1):
                dst, kdst = tmp[step % 2]
                lo = 2 * sh - 1
                tt(dst[:, lo:527], src[:, lo:527], src[:, lo - sh:527 - sh], ALU.add, [ksrc], [kdst])
                src, ksrc = dst, kdst
                sh *= 2
            w = 2 ** (g + 1)
            if st == 0:
                tt(src[:, 15:31], src[:, 15:31], cst[:, C_PFIX + 16 * g:C_PFIX + 16 * g + 16], ALU.mult,
                   [ksrc, "cst"], [ksrc])
            stt(pooled[g][0][:, 0:TP], src[:, 15:527], 1.0 / w, E[:, 15:527], ALU.mult, ALU.subtract,
                [ksrc, kE], [pooled[g][1]])
            sh = 1
            src2, ksrc2 = S_, kS
            tmps = [SF(10), SF(11)]
            for step in range(g + 1):
                dst, kdst = tmps[step % 2]
                lo = 2 * sh - 1
                tt(Sv(dst, lo, 23), Sv(src2, lo, 23), Sv(src2, lo - sh, 23 - sh), ALU.add, [ksrc2], [kdst])
                src2, ksrc2 = dst, kdst
                sh *= 2
            stt(pooled[g][0][:, TP:T].rearrange("p (s j) -> p s j", j=8), Sv(src2, 15, 23), 1.0 / w, Sv(S_, 15, 23),
                ALU.mult, ALU.subtract, [ksrc2, kS], [pooled[g][1]])
            for (o0, n, bk) in ((0, TP, 0), (TP, TS, 1)):
                mm(pall[:, 2048 + o0:2048 + o0 + n], pw[:, g, :], pooled[g][0][:, o0:o0 + n], True, True,
                   [kpw, pooled[g][1]], [("ps", 4 + bk)])
            act(obT[g][0][:], dbv(2), AF.Identity, dbk(2) + [("pcol", l)], [obT[g][1]], scale=pc_(l, "pscale", g))
            cp(carry_pb[:, l, g, :], E[:, 512:527], [kE], ["carry_pb"])
        for half in range(2):
            si = stgn[0] % 2
            stgn[0] += 1
            for q in range(2):
                s = 2 * half + q
                for g in range(4):
                    dst = stg[si][:, 60 * q:60 * q + 60].rearrange("p (j c) -> p c j", c=4)[:, g, :]
                    cp(dst, pbS[g][0][:, 23 * s + 8:23 * s + 23], [pbS[g][1]], [("stg", si)])
            tr(bank(7)[:, 0:128], stg[si][:, :], ident, [("stg", si), "cst"], [("ps", 7)])
            rt, krt = SF(12)
            cp(rt[:, 0:128], bank(7)[:, 0:128], [("ps", 7)], [krt])
            for q in range(2):
                s = 2 * half + q
                P.dma("sp", lambda e, s=s, q=q, rt=rt: e.dma_start(
                    out=O["s_pool"][l, b0 + s].rearrange("j (c p) -> (j c) p", p=128), in_=rt[60 * q:60 * q + 60, 0:128]),
                    reads=[krt], sem=("sfd", 12))
        if st == NST - 1:
            si = stgn[0] % 2
            stgn[0] += 1
            for g in range(4):
                dst = stg[si][:, 0:60].rearrange("p (j c) -> p c j", c=4)[:, g, :]
                cp(dst, carry_pb[:, l, g, :], ["carry_pb"], [("stg", si)])
            tr(bank(7)[:, 0:128], stg[si][:, :], ident, [("stg", si), "cst"], [("ps", 7)])
            rt, krt = SF(12)
            cp(rt[:, 0:128], bank(7)[:, 0:128], [("ps", 7)], [krt])
            P.dma("sp", lambda e, rt=rt: e.dma_start(
                out=O["p_pool"][l].rearrange("j (c p) -> (j c) p", p=128), in_=rt[0:60, 0:128]),
                reads=[krt], sem=("sfd", 12))
        if "b" in ACTIVE:
            merge_branch(l, 1, lambda c: obT[c][0][:], [obT[c][1] for c in range(4)], 4, 512)

    def attn_prep():
        for mt in range(2):
            for hh in range(2):
                P.dma("sp", lambda e, mt=mt, hh=hh: e.dma_start(
                    out=xin.cols(mt, 512 * hh, 512 * hh + 512), in_=I["mem_prompt"][128 * mt:128 * mt + 128, 512 * hh:512 * hh + 512]),
                    writes=[xin.keys(mt)[hh]], sem=("xin", mt, hh))
        for k in range(KC):
            b = k % 2
            for mt in range(2):
                tr(bank(b)[:, 128 * mt:128 * mt + 128], xin.cols(mt, 128 * k, 128 * k + 128), ident,
                   xin.keys(mt) + ["cst"], [("ps", b)])
            cp(mpT[:, k, :], bank(b)[:, 0:256], [("ps", b)], ["mpT"], eng="act" if k % 2 else "dve")
        for l in range(NLAYERS):
            wkv, kkv = wload(I["mem_w_kv"][l], D, 512)
            for mt in range(2):
                for k in range(KC):
                    mm(bank(2 + mt), mpT[:, k, 128 * mt:128 * mt + 128], wkv[:, k, :], k == 0, k == KC - 1,
                       ["mpT", kkv], [("ps", 2 + mt)])
                kvf, kkvf = SF(mt)
                cp(kvf[:, 0:512], bank(2 + mt), [("ps", 2 + mt)], [kkvf])
                cp(memV[:, l, mt, :], kvf[:, 256:512], [kkvf], [("memV", l)], eng="act")
                P.dma("sp", lambda e, l=l, mt=mt, kvf=kvf: e.dma_start(
                    out=O["p_mem_k"][l, 128 * mt:128 * mt + 128, :], in_=kvf[:, 0:256]), reads=[kkvf], sem=("sfd", mt))
                P.dma("sp", lambda e, l=l, mt=mt, kvf=kvf: e.dma_start(
                    out=O["p_mem_v"][l, 128 * mt:128 * mt + 128, :], in_=kvf[:, 256:512]), reads=[kkvf], sem=("sfd", mt))
            for c in range(2):
                for k in range(KC):
                    mm(bank(4 + c)[:, 0:256], wkv[:, k, 128 * c:128 * c + 128], mpT[:, k, :], k == 0, k == KC - 1,
                       ["mpT", kkv], [("ps", 4 + c)])
                cp(memK[:, l, c, :], bank(4 + c)[:, 0:256], [("ps", 4 + c)], [("memK", l)], eng="act" if c else "dve")

    def softmax_pv(l, np_, q_fn, kT_fn, kT_keys, v_fn, v_keys, out_fn, out_keys):
        sc = pall[0:np_, 2048:3072].rearrange("p (h m) -> p h m", h=4)
        hb = lambda h: 2048 + 512 * (h % 2) + 256 * (h // 2)
        ps_ = lambda h: 2 * (h % 2) + h // 2
        for h in range(4):
            mm(pall[0:np_, hb(h):hb(h) + 256], q_fn(h), kT_fn(h), True, True,
               ["qT%d" % (h // 2)] + kT_keys, [("ps", 4 + h % 2)])
        P.ck("sm_scores")
        (mx, kmx), (nb, knb), (ss, kss), (rs, krs) = SF(2), SF(3), SF(4), SF(5)
        OP("dve", lambda e: e.tensor_reduce(mx[0:np_, 0:4], sc, mybir.AxisListType.X, ALU.max),
           reads=[("ps", 4), ("ps", 5)], writes=[kmx])
        ts(nb[0:np_, 0:4], mx[0:np_, 0:4], -0.125, None, ALU.mult, None, [kmx], [knb])
        pf = [SF(6), SF(7)]
        for h in range(4):
            pt, kpt = pf[h // 2]
            act(pt[0:np_, 256 * (h % 2):256 * (h % 2) + 256], pall[0:np_, hb(h):hb(h) + 256], AF.Exp,
                [("ps", 4 + h % 2), knb], [kpt, kss], bias=nb[0:np_, ps_(h):ps_(h) + 1], scale=0.125, accum=ss[0:np_, h:h + 1])
        P.ck("sm_exp")
        OP("dve", lambda e: e.reciprocal(rs[0:np_, 0:4], ss[0:np_, 0:4]), reads=[kss], writes=[krs])
        pn = [SB(0), SB(1)]
        for h in range(4):
            pt, kpt = pf[h // 2]
            pnt, kpn = pn[h // 2]
            ts(pnt[0:np_, 256 * (h % 2):256 * (h % 2) + 256], pt[0:np_, 256 * (h % 2):256 * (h % 2) + 256],
               rs[0:np_, h:h + 1], None, ALU.mult, None, [kpt, krs], [kpn])
        P.ck("sm_norm")
        pT = [SB(2), SB(3)]
        for h in range(4):
            pnt, kpn = pn[h // 2]
            for mt in range(2):
                j = 2 * h + mt
                tr(pallb[:, 6 * 1024 + np_ * j:6 * 1024 + np_ * j + np_],
                   pnt[0:np_, 256 * (h % 2) + 128 * mt:256 * (h % 2) + 128 * mt + 128], identb[0:np_, 0:np_],
                   [kpn, "cstb"], [("ps", 6)])
        for half in range(2):
            cp(pT[half][0][:, 0:4 * np_], pallb[:, 6 * 1024 + 4 * np_ * half:6 * 1024 + 4 * np_ * half + 4 * np_],
               [("ps", 6)], [pT[half][1]], eng="act" if half else "dve")
        P.ck("sm_tr")
        for h in range(4):
            for mt in range(2):
                j = 2 * h + mt
                mm(out_fn(h), v_fn(h, mt), pT[j // 4][0][:, np_ * (j % 4):np_ * (j % 4) + np_], mt == 0, mt == 1,
                   v_keys + [pT[j // 4][1]], out_keys)

    def branch_attn(l, st):
        b0 = 4 * st
        qT = [SB(4), SB(5)]
        omT = [SB(20), SB(21)]

        def consume(ci, db, w):
            cp(qT[ci][0][:], dbv(db), dbk(db), [qT[ci][1], "qT%d" % ci], eng="act" if ci else "dve")
        proj_cols(l, OFF_M, 256, consume)
        P.ck("m_proj")
        for tt_ in range(4):
            softmax_pv(
                l, 128,
                lambda h: qT[h // 2][0][64 * (h % 2):64 * (h % 2) + 64, 128 * tt_:128 * tt_ + 128],
                lambda h: memK[64 * (h % 2):64 * (h % 2) + 64, l, h // 2, :], [("memK", l)],
                lambda h, mt: memV[:, l, mt, 64 * h:64 * h + 64], [("memV", l)],
                lambda h: pall[64 * (h % 2):64 * (h % 2) + 64, 3584 + 128 * (h // 2):3584 + 128 * (h // 2) + 128],
                [("ps", 7)])
            for c in range(2):
                cp(omT[c][0][:, 128 * tt_:128 * tt_ + 128], pall[:, 3584 + 128 * c:3584 + 128 * c + 128], [("ps", 7)],
                   [omT[c][1]], eng="act" if c else "dve")
        P.ck("m_prompt")
        for s_ in range(NSQ):
            kf, kkf = SF(8)
            vf, kvf = SF(9)
            kTs, kkTs = SB(6)
            vS, kvS = SB(7)
            P.dma("sp", lambda e, s_=s_, kf=kf: e.dma_start(
                out=kf[:, 0:512].rearrange("p (t n) -> p t n", t=2),
                in_=I["cache_mem_k"][l, b0 + s_].rearrange("(t p) n -> p t n", p=128)), writes=[kkf], sem=("sfd", 8))
            P.dma("sp", lambda e, s_=s_, vf=vf: e.dma_start(
                out=vf[:, 0:512].rearrange("p (t n) -> p t n", t=2),
                in_=I["cache_mem_v"][l, b0 + s_].rearrange("(t p) n -> p t n", p=128)), writes=[kvf], sem=("sfd", 9))
            cp(vS[:, 0:512], vf[:, 0:512], [kvf], [kvS], eng="act")
            for c in range(2):
                for mt in range(2):
                    tr(bank(4 + c)[:, 128 * mt:128 * mt + 128], kf[:, 256 * mt + 128 * c:256 * mt + 128 * c + 128], ident,
                       [kkf, "cst"], [("ps", 4 + c)])
                cp(kTs[:, 256 * c:256 * c + 256], bank(4 + c)[:, 0:256], [("ps", 4 + c)], [kkTs], eng="act" if c else "dve")
            softmax_pv(
                l, 8,
                lambda h: qT[h // 2][0][64 * (h % 2):64 * (h % 2) + 64, TP + 8 * s_:TP + 8 * s_ + 8],
                lambda h: kTs[64 * (h % 2):64 * (h % 2) + 64, 256 * (h // 2):256 * (h // 2) + 256], [kkTs],
                lambda h, mt: vS[:, 256 * mt + 64 * h:256 * mt + 64 * h + 64], [kvS],
                lambda h: pall[64 * (h % 2):64 * (h % 2) + 64, 3584 + 256 + 32 * (h // 2) + 8 * s_:3584 + 256 + 32 * (h // 2) + 8 * s_ + 8],
                [("ps", 7)])
        for c in range(2):
            cp(omT[c][0][:, TP:T], pall[:, 3584 + 256 + 32 * c:3584 + 256 + 32 * c + 32], [("ps", 7)], [omT[c][1]],
               eng="act" if c else "dve")
        P.ck("m_sample")
        merge_branch(l, 3, lambda c: omT[c][0][:], [omT[c][1] for c in range(2)], 2, 1536)

    def inv_chain(C, H, P1, Q1, X, Pn, Qn):
        W = H * C
        v3 = lambda t_: t_[0:C, 0:W].rearrange("p (h c) -> p h c", h=H)
        tt(v3(X[0]), v3(P1[0]), identb[0:C, 0:C].unsqueeze(1).to_broadcast([C, H, C]), ALU.add, [P1[1], "cstb"], [X[1]])
        nsteps = {64: 5, 8: 2}[C]
        Pc, Qc = P1, Q1
        Pd, Qd = Pn, Qn
        for k in range(nsteps):
            last = k == nsteps - 1
            for h in range(H):
                mm(pall[0:C, 0 + C * h:0 + C * h + C], Pc[0][0:C, C * h:C * h + C], Qc[0][0:C, C * h:C * h + C], True, True,
                   [Pc[1], Qc[1]], [("ps", 0)])
            if not last:
                for h in range(H):
                    mm(pall[0:C, 512 + C * h:512 + C * h + C], Qc[0][0:C, C * h:C * h + C], Pc[0][0:C, C * h:C * h + C],
                       True, True, [Pc[1], Qc[1]], [("ps", 1)])
            cp(Qd[0][0:C, 0:W], pall[0:C, 0:W], [("ps", 0)], [Qd[1]], eng="act")
            if not last:
                cp(Pd[0][0:C, 0:W], pall[0:C, 512:512 + W], [("ps", 1)], [Pd[1]], eng="act")
            for h in range(H):
                mm(pall[0:C, 1024 + C * h:1024 + C * h + C], Qd[0][0:C, C * h:C * h + C], X[0][0:C, C * h:C * h + C],
                   True, True, [Qd[1], X[1]], [("ps", 2)])
            tt(X[0][0:C, 0:W], X[0][0:C, 0:W], pall[0:C, 1024:1024 + W], ALU.add, [X[1], ("ps", 2)], [X[1]])
            Pc, Qc, Pd, Qd = Pd, Qd, Pc, Qc

    def branch_gdn(l, st):
        b0 = 4 * st
        convh, kconvh = SF(0)
        EE, kEE = SF(1)
        acc, kacc = SF(2)
        tmpf, ktmpf = SF(3)
        rows = {n: SF(4 + i) for i, n in enumerate(("cum", "beta", "ecum", "becum"))}
        o_f = [SF(8 + h) for h in range(4)]
        misc, kmisc = SF(12)
        qT = [SB(h) for h in range(4)]
        kT = [SB(4 + h) for h in range(4)]
        vT = [SB(8 + h) for h in range(4)]
        zs = [SB(12 + h) for h in range(4)]
        CT = [SB(16 + h) for h in range(4)]
        QT = [SB(20 + h) for h in range(4)]
        for half in range(2):
            si = stgn[0] % 2
            stgn[0] += 1
            for q in range(2):
                s_ = 2 * half + q
                P.dma("sp", lambda e, s_=s_, q=q, si=si: e.dma_start(
                    out=stg[si][36 * q:36 * q + 36, :],
                    in_=I["state_delta_conv"][l, b0 + s_].rearrange("j (c p) -> (j c) p", p=128)),
                    writes=[("stg", si)], sem=("stg", si))
            tr(bank(7)[:, 0:128], stg[si][:, :], ident, [("stg", si), "cst"], [("ps", 7)])
            cp(convh[:, 72 * half:72 * half + 72], bank(7)[:, 0:72], [("ps", 7)], [kconvh])
        so_t = [stg[0], stg[1]]
        outst, koutst = SF(13)

        def consume(ci, db, w):
            E = EE[:, 0:515]
            Es = EE[:, 515:559].rearrange("p (s j) -> p s j", j=11)
            if ci < 12:
                cp(E[:, 0:3], carry_pc[:, l, ci, :], ["carry_pc"], [kEE])
                cp(Es[:, :, 0:3], convh[:, 0:144].rearrange("p (s j c) -> p c s j", j=3, c=12)[:, ci, :, :], [kconvh], [kEE])
                cp(E[:, 3:515], pall[:, 1024 * db:1024 * db + TP], [("ps", 2 * db)], [kEE], eng="act")
                cp(Es[:, :, 3:11], pall[:, 1024 * db + TP:1024 * db + T].rearrange("p (s j) -> p s j", j=8),
                   [("ps", 2 * db + 1)], [kEE])
                A = acc[:, 0:TP]
                As = acc[:, TP:T].rearrange("p (s j) -> p s j", j=8)
                for j in range(4):
                    wc = pc_(l, "conv", 12 * j + ci)
                    if j == 0:
                        ts(A, E[:, 0:512], wc, None, ALU.mult, None, [kEE, ("pcol", l)], [kacc])
                        ts(As, Es[:, :, 0:8], wc, None, ALU.mult, None, [kEE, ("pcol", l)], [kacc])
                    else:
                        stt(A, E[:, j:j + 512], wc, A, ALU.mult, ALU.add, [kEE, ("pcol", l), kacc], [kacc])
                        stt(As, Es[:, :, j:j + 8], wc, As, ALU.mult, ALU.add, [kEE, ("pcol", l), kacc], [kacc])
                cp(carry_pc[:, l, ci, :], E[:, 512:515], [kEE], ["carry_pc"])
                cp(outst[:, 0:144].rearrange("p (s j c) -> p c s j", j=3, c=12)[:, ci, :, :], Es[:, :, 8:11], [kEE], [koutst])
                h = ci % 4
                if ci >= 8:
                    act(vT[h][0][:], acc[:, 0:T], AF.Silu, [kacc], [vT[h][1]])
                else:
                    act(tmpf[:, 0:T], acc[:, 0:T], AF.Silu, [kacc], [ktmpf])
                    sq, ksq = SB(24)
                    act(sq[:], tmpf[:, 0:T], AF.Square, [ktmpf], [ksq])
                    for (o0, n, bk) in ((0, TP, 0), (TP, TS, 1)):
                        mm(pall[:, 2048 + o0:2048 + o0 + n], onesb, sq[:, o0:o0 + n], True, True, ["cstb", ksq],
                           [("ps", 4 + bk)])
                    rn, krn = SF(12 if False else 3), None
                    act(acc[:, 0:T], dbv(2), AF.Sqrt, dbk(2), [kacc], bias=1e-12)
                    OP("dve", lambda e: e.reciprocal(acc[:, 0:T], acc[:, 0:T]), reads=[kacc], writes=[kacc])
                    dst = qT[h] if ci < 4 else kT[h]
                    sc_ = (128.0 ** -0.5) if ci < 4 else 1.0
                    stt(dst[0][:], tmpf[:, 0:T], sc_, acc[:, 0:T], ALU.mult, ALU.mult, [ktmpf, kacc], [dst[1]])
            else:
                h = ci - 12
                act(zs[h][0][:], dbv(db), AF.Silu, dbk(db), [zs[h][1]])
        proj_cols(l, OFF_C, 2048, consume)
        bet, kbet = rows["beta"]
        cum, kcum = rows["cum"]
        ecum, kecum = rows["ecum"]
        becum, kbecum = rows["becum"]

        def cons_b(ci, db, w):
            act(bet[0:4, 0:T], pall[0:4, 1024 * db:1024 * db + T], AF.Sigmoid, dbk(db), [kbet])
        proj_cols(l, OFF_C + 2048, 4, cons_b)

        def cons_a(ci, db, w):
            x_ = tmpf[0:4, 0:T]
            act(x_, pall[0:4, 1024 * db:1024 * db + T], AF.Identity, dbk(db) + ["gparm"], [ktmpf], bias=gparm[:, l, 1:2])
            ax = acc[0:4, 0:T]
            act(ax, x_, AF.Abs, [ktmpf], [kacc])
            act(ax, ax, AF.Exp, [kacc], [kacc], scale=-1.0)
            act(ax, ax, AF.Ln, [kacc], [kacc], bias=1.0)
            stt(x_, x_, 0.0, ax, ALU.max, ALU.add, [ktmpf, kacc], [ktmpf])
            act(misc[0:4, 0:1], gparm[:, l, 0:1], AF.Exp, ["gparm"], [kmisc])
            ts(x_, x_, misc[0:4, 0:1], -1.0, ALU.mult, ALU.mult, [ktmpf, kmisc], [ktmpf])
            OP("dve", lambda e: e.tensor_tensor_scan(cum[0:4, 0:T], cst[0:4, C_SCAN:C_SCAN + T], x_, 0.0, ALU.mult, ALU.add),
               reads=[ktmpf, "cst"], writes=[kcum])
            act(ecum[0:4, 0:T], cum[0:4, 0:T], AF.Exp, [kcum], [kecum])
            tt(becum[0:4, 0:T], ecum[0:4, 0:T], bet[0:4, 0:T], ALU.mult, [kecum, kbet], [kbecum])
        proj_cols(l, OFF_C + 2052, 4, cons_a)
        wCt = misc[:, 16:16 + 48].rearrange("p (h u) -> p h u", h=4)
        for h in range(4):
            sel = cst[0:4, C_SEL4 + 128 * h:C_SEL4 + 128 * h + 128]
            for (src, ks, dstp, srcT) in ((becum, kbecum, CT[h], kT[h]), (ecum, kecum, QT[h], qT[h])):
                for (o0, n, bk) in ((0, TP, 0), (TP, TS, 1)):
                    mm(pall[:, 3072 + o0:3072 + o0 + n], sel, src[0:4, o0:o0 + n], True, True, ["cst", ks], [("ps", 6 + bk)])
                tt(dstp[0][:], srcT[0][:], dbv(3), ALU.mult, [srcT[1]] + dbk(3), [dstp[1]])
                if src is ecum:
                    cp(wCt[:, h, 0:8], pall[:, 3072:3072 + TP].rearrange("p (u c) -> p u c", c=64)[:, :, 63], [("ps", 6)], [kmisc])
                    cp(wCt[:, h, 8:12], pall[:, 3072 + TP:3072 + T].rearrange("p (u c) -> p u c", c=8)[:, :, 7], [("ps", 7)], [kmisc])
        U1, kU1 = SF(1)
        U2, kU2 = SF(2)
        U3, kU3 = SF(3)
        P1a, Q1a, P1b, Q1b, Xt, A1T, K1, Rb, Xn, Sb = [SB(24 + i) for i in range(10)]
        sS, ksS = SF(12)
        units = [(64 * u, 64, u, None) for u in range(8)] + [(TP + 8 * s_, 8, 8 + s_, s_) for s_ in range(NSQ)]
        for (t0, C, ui, sidx) in units:
            if sidx is None:
                S_ap = Sgd[:, l, :, :]
                kS = ("Sgd", l)
            else:
                S_ap = o_f[0][0]
                stile, kS = SF(13 + sidx % 2) if False else (None, None)
            if sidx is not None:
                S_ap = SFt[14][:, 0:512].rearrange("p (h v) -> p h v", h=4)
                kS = ("sf", 14)
                P.dma("sp", lambda e, S_ap=S_ap, sidx=sidx: e.dma_start(
                    out=S_ap, in_=I["state_delta"][l, b0 + sidx].rearrange("h k v -> k h v")),
                    writes=[kS], sem=("sfd", 14))
            Sflat = S_ap.rearrange("p h v -> p (h v)")
            cp(Sb[0][:, 0:512], Sflat, [kS], [Sb[1]], eng="act")
            W4 = 4 * C
            v3 = lambda ap: ap.rearrange("p (h c) -> p h c", h=4)
            mm(pall[0:C, 1536:1540], cum[0:4, t0:t0 + C], ident[0:4, 0:4], True, True, [kcum, "cst"], [("ps", 3)])
            mm(pall[0:C, 1540:1544], bet[0:4, t0:t0 + C], ident[0:4, 0:4], True, True, [kbet, "cst"], [("ps", 3)])
            ts(misc[0:4, 4:8], ident[0:4, 0:4], cum[0:4, t0 + C - 1:t0 + C], None, ALU.mult, None, [kcum, "cst"], [kmisc])
            mm(pall[0:C, 1544:1548], cst[0:4, C_ONE:C_ONE + C], misc[0:4, 4:8], True, True, ["cst", kmisc], [("ps", 3)])
            colv = U1[0:C, 0:16]
            cp(colv[:, 0:12], pall[0:C, 1536:1548], [("ps", 3)], [kU1])
            tt(colv[:, 12:16], colv[:, 8:12], colv[:, 0:4], ALU.subtract, [kU1], [kU1])
            act(colv[:, 12:16], colv[:, 12:16], AF.Exp, [kU1], [kU1])
            for h in range(4):
                sel = cst[0:4, C_SEL4 + 128 * h:C_SEL4 + 128 * h + C]
                mm(pall[0:C, 2048 + C * h:2048 + C * h + C], sel, cum[0:4, t0:t0 + C], True, True, ["cst", kcum], [("ps", 4)])
                mm(pall[0:C, 2560 + C * h:2560 + C * h + C], sel, bet[0:4, t0:t0 + C], True, True, ["cst", kbet], [("ps", 5)])
            Ev = U1[0:C, 16:16 + W4]
            e1 = U1[0:C, 272:272 + W4]
            e2 = U2[0:C, 0:W4]
            tG = U2[0:C, 256:256 + W4]
            tt(v3(Ev), v3(pall[0:C, 2048:2048 + W4]), colv[:, 0:4].unsqueeze(2).to_broadcast([C, 4, C]), ALU.subtract,
               [("ps", 4), kU1], [kU1])
            ts(e1, Ev, 0.0, None, ALU.min, None, [kU1], [kU1])
            ts(e2, Ev, -1.0, 0.0, ALU.mult, ALU.min, [kU1], [kU2])
            act(e1, e1, AF.Exp, [kU1], [kU1])
            act(e2, e2, AF.Exp, [kU2], [kU2])
            mus = cst[0:C, C_MUS:C_MUS + C].unsqueeze(1).to_broadcast([C, 4, C])
            mls = cst[0:C, C_MLS:C_MLS + C].unsqueeze(1).to_broadcast([C, 4, C])
            mui = cst[0:C, C_MUI:C_MUI + C].unsqueeze(1).to_broadcast([C, 4, C])
            for h in range(4):
                mm(pall[0:C, 0 + C * h:0 + C * h + C], kT[h][0][:, t0:t0 + C], kT[h][0][:, t0:t0 + C], True, True,
                   [kT[h][1]], [("ps", 0)])
                mm(pall[0:C, 512 + C * h:512 + C * h + C], kT[h][0][:, t0:t0 + C], qT[h][0][:, t0:t0 + C], True, True,
                   [kT[h][1], qT[h][1]], [("ps", 1)])
            tt(v3(tG), v3(pall[0:C, 0:W4]), v3(e1), ALU.mult, [("ps", 0), kU1], [kU2])
            tt(v3(tG), v3(tG), mus, ALU.mult, [kU2, "cst"], [kU2])
            stt(P1a[0][0:C, 0:W4], tG, -1.0, pall[0:C, 2560:2560 + W4], ALU.mult, ALU.mult, [kU2, ("ps", 5)], [P1a[1]])
            tt(v3(tG), v3(pall[0:C, 0:W4]), v3(e2), ALU.mult, [("ps", 0), kU2], [kU2])
            tt(v3(tG), v3(tG), mls, ALU.mult, [kU2, "cst"], [kU2])
            stt(v3(Q1a[0][0:C, 0:W4]), v3(tG), -1.0, colv[:, 4:8].unsqueeze(2).to_broadcast([C, 4, C]), ALU.mult, ALU.mult,
                [kU2, kU1], [Q1a[1]])
            tt(v3(tG), v3(pall[0:C, 512:512 + W4]), v3(e1), ALU.mult, [("ps", 1), kU1], [kU2])
            tt(v3(A1T[0][0:C, 0:W4]), v3(tG), mui, ALU.mult, [kU2, "cst"], [A1T[1]])
            inv_chain(C, 4, P1a, Q1a, Xt, P1b, Q1b)
            for h in range(4):
                tr(pallb[0:C, 6 * 1024 + 128 * h:6 * 1024 + 128 * h + 128], kT[h][0][:, t0:t0 + C], identb, [kT[h][1], "cstb"],
                   [("ps", 6)])
                tr(pallb[0:C, 7 * 1024 + 128 * h:7 * 1024 + 128 * h + 128], vT[h][0][:, t0:t0 + C], identb, [vT[h][1], "cstb"],
                   [("ps", 7)])
            v3d = lambda ap: ap.rearrange("p (h d) -> p h d", h=4)
            tt(v3d(K1[0][0:C, 0:512]), v3d(pallb[0:C, 6 * 1024:6 * 1024 + 512]),
               colv[:, 12:16].unsqueeze(2).to_broadcast([C, 4, 128]), ALU.mult, [("ps", 6), kU1], [K1[1]])
            Rc = U3[0:C, 0:512]
            tt(v3d(Rc), v3d(pallb[0:C, 7 * 1024:7 * 1024 + 512]), colv[:, 4:8].unsqueeze(2).to_broadcast([C, 4, 128]),
               ALU.mult, [("ps", 7), kU1], [kU3])
            for h in range(4):
                mm(pall[0:C, 1536 + 0:1536 + 0] if False else pall[0:C, 2048 + 128 * h:2048 + 128 * h + 128],
                   CT[h][0][:, t0:t0 + C], Sb[0][:, 128 * h:128 * h + 128], True, True, [CT[h][1], Sb[1]], [("ps", 4)])
            tt(Rb[0][0:C, 0:512], Rc, pall[0:C, 2048:2560], ALU.subtract, [kU3, ("ps", 4)], [Rb[1]])
            for h in range(4):
                mm(pall[0:C, 2560 + 128 * h:2560 + 128 * h + 128], Xt[0][0:C, C * h:C * h + C], Rb[0][0:C, 128 * h:128 * h + 128],
                   True, True, [Xt[1], Rb[1]], [("ps", 5)])
            cp(Xn[0][0:C, 0:512], pall[0:C, 2560:3072], [("ps", 5)], [Xn[1]], eng="act")
            for h in range(4):
                mm(pall[:, 0 + C * h:0 + C * h + C], Sb[0][:, 128 * h:128 * h + 128], QT[h][0][:, t0:t0 + C], True, False,
                   [Sb[1], QT[h][1]], [("ps", 0)])
                mm(pall[:, 0 + C * h:0 + C * h + C], Xn[0][0:C, 128 * h:128 * h + 128], A1T[0][0:C, C * h:C * h + C], False, True,
                   [Xn[1], A1T[1]], [("ps", 0)])
            for h in range(4):
                cp(o_f[h][0][:, t0:t0 + C], pall[:, C * h:C * h + C], [("ps", 0)], [o_f[h][1]], eng="act")
            for h in range(4):
                mm(pall[:, 512 + 128 * h:512 + 128 * h + 128], K1[0][0:C, 128 * h:128 * h + 128], Xn[0][0:C, 128 * h:128 * h + 128],
                   True, True, [K1[1], Xn[1]], [("ps", 1)])
            for h in range(4):
                stt(S_ap[:, h, :], S_ap[:, h, :], wCt[:, h, ui:ui + 1], pall[:, 512 + 128 * h:512 + 128 * h + 128], ALU.mult, ALU.add,
                    [kS, kmisc, ("ps", 1)], [kS])
            if sidx is not None:
                P.dma("sp", lambda e, S_ap=S_ap, sidx=sidx: e.dma_start(
                    out=O["s_delta"][l, b0 + sidx].rearrange("h k v -> k h v"), in_=S_ap),
                    reads=[kS], sem=("sfd", 14))
        if st == NST - 1:
            P.dma("sp", lambda e: e.dma_start(out=O["p_delta"][l].rearrange("h k v -> k h v"), in_=Sgd[:, l, :, :]),
                  reads=[("Sgd", l)], sem=("pdel", l))
            si = stgn[0] % 2
            stgn[0] += 1
            cp(stg[si][:, 0:36].rearrange("p (j c) -> p c j", c=12), carry_pc[:, l, :, :], ["carry_pc"], [("stg", si)])
            tr(bank(7)[:, 0:128], stg[si][:, :], ident, [("stg", si), "cst"], [("ps", 7)])
            cp(tmpf[:, 0:128], bank(7)[:, 0:128], [("ps", 7)], [ktmpf])
            P.dma("sp", lambda e: e.dma_start(out=O["p_delta_conv"][l].rearrange("j (c p) -> (j c) p", p=128), in_=tmpf[0:36, 0:128]),
                  reads=[ktmpf], sem=("sfd", 3))
        for half in range(2):
            si = stgn[0] % 2
            stgn[0] += 1
            cp(stg[si][:, 0:72], outst[:, 72 * half:72 * half + 72], [koutst], [("stg", si)])
            tr(bank(7)[:, 0:128], stg[si][:, :], ident, [("stg", si), "cst"], [("ps", 7)])
            cp(acc[:, 0:128], bank(7)[:, 0:128], [("ps", 7)], [kacc])
            for q in range(2):
                s_ = 2 * half + q
                P.dma("sp", lambda e, s_=s_, q=q: e.dma_start(
                    out=O["s_delta_conv"][l, b0 + s_].rearrange("j (c p) -> (j c) p", p=128), in_=acc[36 * q:36 * q + 36, 0:128]),
                    reads=[kacc], sem=("sfd", 2))
        ocT = [SB(20 + h) for h in range(4)]
        for h in range(4):
            sq, ksq = SB(24)
            act(sq[:], o_f[h][0][:, 0:T], AF.Square, [o_f[h][1]], [ksq])
            for (o0, n, bk) in ((0, TP, 0), (TP, TS, 1)):
                mm(pall[:, 2048 + o0:2048 + o0 + n], onesb, sq[:, o0:o0 + n], True, True, ["cstb", ksq], [("ps", 4 + bk)])
            act(acc[:, 0:T], dbv(2), AF.Sqrt, dbk(2), [kacc], bias=1e-6, scale=1.0 / 128)
            OP("dve", lambda e: e.reciprocal(acc[:, 0:T], acc[:, 0:T]), reads=[kacc], writes=[kacc])
            stt(tmpf[:, 0:T], o_f[h][0][:, 0:T], pc_(l, "nrm", 0), acc[:, 0:T], ALU.mult, ALU.mult,
                [o_f[h][1], ("pcol", l), kacc], [ktmpf])
            tt(ocT[h][0][:], tmpf[:, 0:T], zs[h][0][:], ALU.mult, [ktmpf, zs[h][1]], [ocT[h][1]])
        merge_branch(l, 2, lambda c: ocT[c][0][:], [ocT[c][1] for c in range(4)], 4, 1024)

    def branch_rwkv(l, st):
        b0 = 4 * st
        hist, khist = SF(0)
        paE, kpaE = SF(1)
        tA, ktA = SF(2)
        tB, ktB = SF(3)
        cs, kcs = SF(4)
        coef = [SF(5), SF(6), SF(7), SF(13)]
        yT = [SF(8 + hp) for hp in range(4)]
        misc, kmisc = SF(12)
        tC, ktC = SF(14)
        QT = [SB(hp) for hp in range(4)]
        CT = [SB(4 + hp) for hp in range(4)]
        BT = [SB(8 + hp) for hp in range(4)]
        KT = [SB(12 + hp) for hp in range(4)]
        vT = [SB(16 + hp) for hp in range(4)]
        x12b, kx12 = SB(20)
        sgb, ksgb = SB(21)
        waup, kwaup = SB(22)
        gup, kgup = SB(23)
        WC = misc[:, 0:48].rearrange("p (h u) -> p h u", h=4)
        P.dma("pool", lambda e: e.dma_start(out=waup[0:64, 0:512], in_=I["rwkv_w_up"][l]), writes=[kwaup], sem=("sbd", 22))
        P.dma("pool", lambda e: e.dma_start(out=waup[64:128, 0:512], in_=I["rwkv_a_up"][l]), writes=[kwaup], sem=("sbd", 22))
        P.dma("pool", lambda e: e.dma_start(out=gup[:, 0:512], in_=I["rwkv_g_up"][l]), writes=[kgup], sem=("sbd", 23))
        si = stgn[0] % 2
        stgn[0] += 1
        for s_ in range(NSQ):
            P.dma("sp", lambda e, s_=s_, si=si: e.dma_start(out=stg[si][14 * s_:14 * s_ + 14, :], in_=I["state_rwkv_shift"][l, b0 + s_]),
                  writes=[("stg", si)], sem=("stg", si))
        tr(bank(7)[:, 0:128], stg[si][:, :], ident, [("stg", si), "cst"], [("ps", 7)])
        cp(hist[:, 0:56], bank(7)[:, 0:56], [("ps", 7)], [khist])

        def shifted(c, db):
            cp(paE[:, 0:1], carry_pa[:, l, c:c + 1], ["carry_pa"], [kpaE])
            cp(paE[:, 1:1 + T], dbv(db), dbk(db), [kpaE], eng="act")
            Sx = paE[:, 513:545].rearrange("p (s j) -> p s j", j=8)
            d = tB[:, 0:T]
            ds_ = tB[:, TP:T].rearrange("p (s j) -> p s j", j=8)
            tt(tB[:, 0:TP], paE[:, 0:512], paE[:, 1:513], ALU.subtract, [kpaE], [ktB])
            tt(ds_[:, :, 1:8], Sx[:, :, 0:7], Sx[:, :, 1:8], ALU.subtract, [kpaE], [ktB])
            hv = hist[:, 0:56].rearrange("p (s c) -> p c s", c=14)[:, c, :]
            tt(ds_[:, :, 0], hv, Sx[:, :, 0], ALU.subtract, [kpaE, khist], [ktB])
            stt(tA[:, 0:T], d, pc_(l, "mu", c), paE[:, 1:1 + T], ALU.mult, ALU.add, [ktB, kpaE, ("pcol", l)], [ktA])
            cp(carry_pa[:, l, c:c + 1], paE[:, 512:513], [kpaE], ["carry_pa"])
            cp(hist[:, 64:120].rearrange("p (s c) -> p c s", c=14)[:, c, :], Sx[:, :, 7], [kpaE], [khist])

        def cons_w(ci, db, w):
            shifted(12 + ci, db)
            if ci == 0:
                act(x12b[0:64, :], tA[0:64, 0:T], AF.Tanh, [ktA], [kx12])
                cp(x12b[64:128, :], tA[64:128, 0:T], [ktA], [kx12])
            else:
                act(sgb[:], tA[:, 0:T], AF.Sigmoid, [ktA], [ksgb])
        proj_cols(l, 1536, 256, cons_w)

        for hp in range(4):
            for (o0, n, bk) in ((0, TP, 0), (TP, TS, 1)):
                mm(pall[:, 2048 + o0:2048 + o0 + n], waup[0:64, 128 * hp:128 * hp + 128], x12b[0:64, o0:o0 + n], True, True,
                   [kwaup, kx12], [("ps", 4 + bk)])
                mm(pall[:, 3072 + o0:3072 + o0 + n], waup[64:128, 128 * hp:128 * hp + 128], x12b[64:128, o0:o0 + n], True, True,
                   [kwaup, kx12], [("ps", 6 + bk)])
            sig = tC[:, 0:T]
            act(sig, dbv(2), AF.Sigmoid, dbk(2) + [("pcol", l)], [ktC], bias=pc_(l, "w0", hp))
            OP("dve", lambda e: e.tensor_tensor_scan(cs[:, 0:T], cst[:, C_SCAN:C_SCAN + T], sig, 0.0, ALU.mult, ALU.add),
               reads=[ktC, "cst"], writes=[kcs])
            e1, ke1 = SB(24)
            e2, ke2 = SB(25)
            e3, ke3 = SB(26)
            a_t, ka = SB(27)
            e1f, ke1f = SF(3)
            tt(sig, cs[:, 0:T], sig, ALU.subtract, [kcs, ktC], [ktC])
            act(e3[:], sig, AF.Exp, [ktC], [ke3], scale=-C0)
            act(e1f[:, 0:T], cs[:, 0:T], AF.Exp, [kcs], [ke1f], scale=-C0)
            cp(e1[:], e1f[:, 0:T], [ke1f], [ke1])
            act(e2[:], cs[:, 0:T], AF.Exp, [kcs], [ke2], scale=C0)
            cp(WC[:, hp, 0:8], e1f[:, 0:TP].rearrange("p (u c) -> p u c", c=64)[:, :, 63], [ke1f], [kmisc])
            cp(WC[:, hp, 8:12], e1f[:, TP:T].rearrange("p (u c) -> p u c", c=8)[:, :, 7], [ke1f], [kmisc])
            act(a_t[:], dbv(3), AF.Sigmoid, dbk(3) + [("pcol", l)], [ka], bias=pc_(l, "a0", hp))
            rb_, krb = SB(28)

            def cons_r(ci, db, w, hp=hp, e1=e1, ke1=ke1, rb_=rb_, krb=krb):
                shifted(hp, db)
                cp(rb_[:], tA[:, 0:T], [ktA], [krb], eng="act")
                tt(QT[hp][0][:], tA[:, 0:T], e1[:], ALU.mult, [ktA, ke1], [QT[hp][1]])
            proj_cols(l, 128 * hp, 128, cons_r)

            def cons_k(ci, db, w, hp=hp, e2=e2, ke2=ke2, e3=e3, ke3=ke3, a_t=a_t, ka=ka, rb_=rb_, krb=krb):
                shifted(4 + hp, db)
                k_ = tA[:, 0:T]
                kkr, kkkr = SF(3)
                ts(kkr[:, 0:T], k_, pc_(l, "k_k", hp), None, ALU.mult, None, [ktA, ("pcol", l)], [kkkr])
                sq, ksq = SB(29)
                act(sq[:], kkr[:, 0:T], AF.Square, [kkkr], [ksq])
                for (o0, n, bk) in ((0, TP, 0), (TP, TS, 1)):
                    mm(pall[:, 2048 + o0:2048 + o0 + n], blkb, sq[:, o0:o0 + n], True, True, ["cstb", ksq], [("ps", 4 + bk)])
                act(tC[:, 0:T], dbv(2), AF.Sqrt, dbk(2), [ktC], bias=1e-12)
                OP("dve", lambda e: e.reciprocal(tC[:, 0:T], tC[:, 0:T]), reads=[ktC], writes=[ktC])
                tt(kkr[:, 0:T], kkr[:, 0:T], tC[:, 0:T], ALU.mult, [kkkr, ktC], [kkkr])
                tt(CT[hp][0][:], kkr[:, 0:T], e3[:], ALU.mult, [kkkr, ke3], [CT[hp][1]])
                tt(kkr[:, 0:T], kkr[:, 0:T], a_t[:], ALU.mult, [kkkr, ka], [kkkr])
                tt(BT[hp][0][:], kkr[:, 0:T], e2[:], ALU.mult, [kkkr, ke2], [BT[hp][1]])
                ts(tC[:, 0:T], a_t[:], pc_(l, "k_a", hp), pc_(l, "omka", hp), ALU.mult, ALU.add, [ka, ("pcol", l)], [ktC])
                tt(k_, k_, tC[:, 0:T], ALU.mult, [ktA, ktC], [ktA])
                tt(KT[hp][0][:], k_, e2[:], ALU.mult, [ktA, ke2], [KT[hp][1]])
                stt(sq[:], k_, pc_(l, "r_k", hp), rb_[:], ALU.mult, ALU.mult, [ktA, ("pcol", l), krb], [ksq])
                for (o0, n, bk) in ((0, TP, 0), (TP, TS, 1)):
                    mm(pall[:, 2048 + o0:2048 + o0 + n], blkb, sq[:, o0:o0 + n], True, True, ["cstb", ksq], [("ps", 4 + bk)])
                cp(coef[hp][0][:, 0:T], dbv(2), dbk(2), [coef[hp][1]], eng="act")
            proj_cols(l, 512 + 128 * hp, 128, cons_k)

            def cons_v(ci, db, w, hp=hp):
                shifted(8 + hp, db)
                cp(vT[hp][0][:], tA[:, 0:T], [ktA], [vT[hp][1]], eng="act")
                tt(coef[hp][0][:, 0:T], coef[hp][0][:, 0:T], tA[:, 0:T], ALU.mult, [coef[hp][1], ktA], [coef[hp][1]])
            proj_cols(l, 1024 + 128 * hp, 128, cons_v)
        si = stgn[0] % 2
        stgn[0] += 1
        cp(stg[si][:, 0:56], hist[:, 64:120], [khist], [("stg", si)])
        if st == NST - 1:
            cp(stg[si][:, 56:70], carry_pa[:, l, :], ["carry_pa"], [("stg", si)])
        tr(bank(7)[:, 0:128], stg[si][:, :], ident, [("stg", si), "cst"], [("ps", 7)])
        cp(tB[:, 0:128], bank(7)[:, 0:128], [("ps", 7)], [ktB])
        for s_ in range(NSQ):
            P.dma("sp", lambda e, s_=s_: e.dma_start(out=O["s_rwkv_shift"][l, b0 + s_], in_=tB[14 * s_:14 * s_ + 14, 0:128]),
                  reads=[ktB], sem=("sfd", 3))
        if st == NST - 1:
            P.dma("sp", lambda e: e.dma_start(out=O["p_rwkv_shift"][l], in_=tB[56:70, 0:128]), reads=[ktB], sem=("sfd", 3))
        P1a, Q1a, P1b, Q1b, Xt, AqbT, AqkT, BTm, K12, K1t, K2t, Vt, Rb, Un, Sb = [SB(24 + i) for i in range(15)]
        Sst, kSst = SF(3)
        units = [(64 * u, 64, u, None) for u in range(8)] + [(TP + 8 * s_, 8, 8 + s_, s_) for s_ in range(NSQ)]
        st_map = "(a b c) v k -> b v a c k"

        def state_out(S_src, kS_src, dst4):
            for a in range(2):
                tr(bank(3)[:, 128 * a:128 * a + 128], S_src[:, 2 * a:2 * a + 2, :].rearrange("p h v -> p (h v)"), ident,
                   [kS_src, "cst"], [("ps", 3)])
            cp(tC[:, 0:256], bank(3)[:, 0:256], [("ps", 3)], [ktC])
            for b_ in range(2):
                for a_ in range(2):
                    P.dma("sp", lambda e, b_=b_, a_=a_: e.dma_start(
                        out=dst4.rearrange(st_map, a=2, b=2, c=2)[b_][:, a_],
                        in_=tC[64 * b_:64 * b_ + 64, 128 * a_:128 * a_ + 128].rearrange("p (c k) -> p c k", c=2)),
                        reads=[ktC], sem=("sfd", 14))

        for (t0, C, ui, sidx) in units:
            if sidx is None:
                S_ap = Srw[:, l, :, :]
                kS = ("Srw", l)
            else:
                S_ap = Sst[:, 256:512].rearrange("p (h v) -> p h v", h=4)
                kS = kSst
                for b_ in range(2):
                    for a_ in range(2):
                        P.dma("sp", lambda e, b_=b_, a_=a_, sidx=sidx: e.dma_start(
                            out=Sst[64 * b_:64 * b_ + 64, 128 * a_:128 * a_ + 128].rearrange("p (c k) -> p c k", c=2),
                            in_=I["state_rwkv"][l, b0 + sidx].rearrange(st_map, a=2, b=2, c=2)[b_][:, a_]),
                            writes=[kSst], sem=("sfd", 3))
                for a in range(2):
                    tr(bank(3)[:, 128 * a:128 * a + 128], Sst[:, 128 * a:128 * a + 128], ident, [kSst, "cst"], [("ps", 3)])
                cp(Sst[:, 256:512], bank(3)[:, 0:256], [("ps", 3)], [kSst])
            cp(Sb[0][:, 0:256], S_ap.rearrange("p h v -> p (h v)"), [kS], [Sb[1]], eng="act")
            W8 = 8 * C
            U = slice(t0, t0 + C)
            hs = lambda h: 4 * (h % 2) + h // 2
            pv4 = lambda k: pall[0:C, 1024 * k:1024 * k + 1024].rearrange("p (a x) -> p a x", a=2)[:, :, 0:4 * C].rearrange("p a (h c) -> p a h c", h=4)
            sv4 = lambda t_: t_[0:C, 0:W8].rearrange("p (a h c) -> p a h c", a=2, h=4)
            mk4 = lambda c0: cst[0:C, c0:c0 + C].unsqueeze(1).unsqueeze(1).to_broadcast([C, 2, 4, C])

            def gram(k, lt, rt):
                for h in range(8):
                    hp, hl = h // 2, h % 2
                    pr = slice(64 * hl, 64 * hl + 64)
                    mm(pall[0:C, 1024 * k + 512 * hl + C * hp:1024 * k + 512 * hl + C * hp + C], lt[hp][0][pr, U], rt[hp][0][pr, U],
                       True, True, [lt[hp][1], rt[hp][1]], [("ps", 2 * k + hl)])
            gram(0, BT, CT)
            gram(1, CT, BT)
            gram(2, KT, CT)
            gram(3, BT, QT)
            tt(sv4(P1a[0]), pv4(0), mk4(C_NMUS), ALU.mult, [("ps", 0), ("ps", 1), "cst"], [P1a[1]])
            tt(sv4(Q1a[0]), pv4(1), mk4(C_NMLS), ALU.mult, [("ps", 2), ("ps", 3), "cst"], [Q1a[1]])
            tt(sv4(BTm[0]), pv4(2), mk4(C_MUS), ALU.mult, [("ps", 4), ("ps", 5), "cst"], [BTm[1]])
            tt(sv4(AqbT[0]), pv4(3), mk4(C_MUI), ALU.mult, [("ps", 6), ("ps", 7), "cst"], [AqbT[1]])
            gram(0, KT, QT)
            tt(sv4(AqkT[0]), pv4(0), mk4(C_MUI), ALU.mult, [("ps", 0), ("ps", 1), "cst"], [AqkT[1]])
            inv_chain(C, 8, P1a, Q1a, Xt, P1b, Q1b)
            K12v = K12[0][:, 0:8 * C].rearrange("p (h x c) -> p h x c", h=4, x=2)
            for hp in range(4):
                act(K12v[:, hp, 0, :], BT[hp][0][:, U], AF.Copy, [BT[hp][1], kmisc], [K12[1]], scale=WC[:, hp, ui:ui + 1])
                act(K12v[:, hp, 1, :], KT[hp][0][:, U], AF.Copy, [KT[hp][1], kmisc], [K12[1]], scale=WC[:, hp, ui:ui + 1])
            for hp in range(4):
                tr(pallb[0:C, 5 * 1024 + 128 * hp:5 * 1024 + 128 * hp + 128], K12v[:, hp, 0, :], identb, [K12[1], "cstb"], [("ps", 5)])
                tr(pallb[0:C, 6 * 1024 + 128 * hp:6 * 1024 + 128 * hp + 128], K12v[:, hp, 1, :], identb, [K12[1], "cstb"], [("ps", 6)])
                tr(pallb[0:C, 7 * 1024 + 128 * hp:7 * 1024 + 128 * hp + 128], vT[hp][0][:, U], identb, [vT[hp][1], "cstb"], [("ps", 7)])
            cp(K1t[0][0:C, 0:512], pallb[0:C, 5 * 1024:5 * 1024 + 512], [("ps", 5)], [K1t[1]], eng="act")
            cp(K2t[0][0:C, 0:512], pallb[0:C, 6 * 1024:6 * 1024 + 512], [("ps", 6)], [K2t[1]], eng="act")
            cp(Vt[0][0:C, 0:512], pallb[0:C, 7 * 1024:7 * 1024 + 512], [("ps", 7)], [Vt[1]], eng="act")
            for h in range(8):
                hp, hl = h // 2, h % 2
                pr = slice(64 * hl, 64 * hl + 64)
                mm(pall[0:C, 64 * hs(h):64 * hs(h) + 64], BTm[0][0:C, C * hs(h):C * hs(h) + C], Vt[0][0:C, 64 * h:64 * h + 64], True, True,
                   [BTm[1], Vt[1]], [("ps", 0)])
                mm(pall[0:C, 512 + 512 * hl + 64 * hp:512 + 512 * hl + 64 * hp + 64], CT[hp][0][pr, U], Sb[0][pr, 64 * hp:64 * hp + 64], True, True,
                   [CT[hp][1], Sb[1]], [("ps", 1 + hl)])
            act(tC[0:C, 0:512], pall[0:C, 0:512], AF.Copy, [("ps", 0)], [ktC], scale=-1.0)
            tt(Rb[0][0:C, 0:512].rearrange("p (a x) -> p a x", a=2), tC[0:C, 0:512].rearrange("p (a x) -> p a x", a=2),
               pall[0:C, 512:1536].rearrange("p (a x) -> p a x", a=2)[:, :, 0:256], ALU.subtract, [ktC, ("ps", 1), ("ps", 2)], [Rb[1]])
            for h in range(8):
                mm(pall[0:C, 1536 + 64 * hs(h):1536 + 64 * hs(h) + 64], Xt[0][0:C, C * hs(h):C * hs(h) + C],
                   Rb[0][0:C, 64 * hs(h):64 * hs(h) + 64], True, True, [Xt[1], Rb[1]], [("ps", 3)])
            cp(Un[0][0:C, 0:512], pall[0:C, 1536:2048], [("ps", 3)], [Un[1]], eng="act")
            for h in range(8):
                hp, hl = h // 2, h % 2
                pr = slice(64 * hl, 64 * hl + 64)
                mm(pall[pr, 2048 + 512 * hl + C * hp:2048 + 512 * hl + C * hp + C], Sb[0][pr, 64 * hp:64 * hp + 64], QT[hp][0][pr, U],
                   True, True, [Sb[1], QT[hp][1]], [("ps", 4 + hl)])
                oo = pall[pr, 3072 + C * hp:3072 + C * hp + C]
                mm(oo, Un[0][0:C, 64 * hs(h):64 * hs(h) + 64], AqbT[0][0:C, C * hs(h):C * hs(h) + C], True, False, [Un[1], AqbT[1]], [("ps", 6)])
                mm(oo, Vt[0][0:C, 64 * h:64 * h + 64], AqkT[0][0:C, C * hs(h):C * hs(h) + C], False, True, [Vt[1], AqkT[1]], [("ps", 6)])
                so = pall[pr, 3584 + 64 * hp:3584 + 64 * hp + 64]
                mm(so, K1t[0][0:C, 64 * h:64 * h + 64], Un[0][0:C, 64 * hs(h):64 * hs(h) + 64], True, False, [K1t[1], Un[1]], [("ps", 7)])
                mm(so, K2t[0][0:C, 64 * h:64 * h + 64], Vt[0][0:C, 64 * h:64 * h + 64], False, True, [K2t[1], Vt[1]], [("ps", 7)])
            cp(tC[:, 0:4 * C], pall[:, 3072:3072 + 4 * C], [("ps", 6)], [ktC], eng="act")
            for hp in range(4):
                for hl in range(2):
                    pr = slice(64 * hl, 64 * hl + 64)
                    tt(yT[hp][0][pr, U], tC[pr, C * hp:C * hp + C], pall[pr, 2048 + 512 * hl + C * hp:2048 + 512 * hl + C * hp + C], ALU.add,
                       [ktC, ("ps", 4 + hl)], [yT[hp][1]])
                stt(S_ap[:, hp, :], S_ap[:, hp, :], WC[:, hp, ui:ui + 1], pall[:, 3584 + 64 * hp:3584 + 64 * hp + 64], ALU.mult, ALU.add,
                    [kS, kmisc, ("ps", 7)], [kS])
            if sidx is not None:
                state_out(S_ap, kS, O["s_rwkv"][l, b0 + sidx])
        if st == NST - 1:
            state_out(Srw[:, l, :, :], ("Srw", l), O["p_rwkv"][l])
        oaT = [SB(24 + hp) for hp in range(4)]
        for hp in range(4):
            yb, kyb = SB(28)
            ysq_, kysq = SB(29)
            cp(yb[:], yT[hp][0][:, 0:T], [yT[hp][1]], [kyb], eng="act")
            act(ysq_[:], yT[hp][0][:, 0:T], AF.Square, [yT[hp][1]], [kysq])
            for (o0, n, bk) in ((0, TP, 0), (TP, TS, 1)):
                mm(pall[:, 2048 + o0:2048 + o0 + n], blkb, yb[:, o0:o0 + n], True, True, ["cstb", kyb], [("ps", 4 + bk)])
                mm(pall[:, 3072 + o0:3072 + o0 + n], blkb, ysq_[:, o0:o0 + n], True, True, ["cstb", kysq], [("ps", 6 + bk)])
                mm(pall[:, 0 + o0:0 + o0 + n], gup[:, 128 * hp:128 * hp + 128], sgb[:, o0:o0 + n], True, True, [kgup, ksgb],
                   [("ps", 0 + bk)])
            mean = tA[:, 0:T]
            var = tB[:, 0:T]
            act(mean, dbv(2), AF.Copy, dbk(2), [ktA], scale=1.0 / 64)
            tt(var, mean, mean, ALU.mult, [ktA], [ktB])
            stt(var, dbv(3), 1.0 / 64, var, ALU.mult, ALU.subtract, dbk(3) + [ktB], [ktB])
            act(var, var, AF.Sqrt, [ktB], [ktB], bias=64e-5)
            OP("dve", lambda e, var=var: e.reciprocal(var, var), reads=[ktB], writes=[ktB])
            tt(mean, yT[hp][0][:, 0:T], mean, ALU.subtract, [yT[hp][1], ktA], [ktA])
            tt(mean, mean, var, ALU.mult, [ktA, ktB], [ktA])
            act(mean, mean, AF.Identity, [ktA, ("pcol", l)], [ktA], bias=pc_(l, "gn_b", hp), scale=pc_(l, "gn_w", hp))
            tt(mean, mean, coef[hp][0][:, 0:T], ALU.add, [ktA, coef[hp][1]], [ktA])
            tt(oaT[hp][0][:], mean, dbv(0), ALU.mult, [ktA] + dbk(0), [oaT[hp][1]])
        merge_branch(l, 0, lambda c: oaT[c][0][:], [oaT[c][1] for c in range(4)], 4, 0)

    def mixing(l, st):
        for m in range(KC):
            memset(mg[:, m, :], 0.0, [("mg", m)], eng="pool" if False else "dve")
        branch_pool(l, st)
        if "m" in ACTIVE:
            branch_attn(l, st)
        if "c" in ACTIVE:
            branch_gdn(l, st)
        if "a" in ACTIVE:
            branch_rwkv(l, st)
        mgb = [SB(m) for m in range(KC)]
        for m in range(KC):
            cp(mgb[m][0][:], mg[:, m, :], [("mg", m)], [mgb[m][1]], eng="act")
        out_proj_ln(l, 1, I["w_out"][l], D, lambda k: mgb[k][0][:], [mgb[k][1] for k in range(KC)], 1.0 / ALPHA)

    attn_prep()
    nst = 1 if "one" in DBG else NST
    for st in range(nst):
        for half in range(2):
            for i in range(2):
                ti = 2 * half + i
                for hh in range(2):
                    P.dma("sp", lambda e, i=i, ti=ti, st=st, hh=hh: e.dma_start(
                        out=xin.cols(i, 512 * hh, 512 * hh + 512),
                        in_=I["x_prompt"][st * TP + 128 * ti:st * TP + 128 * ti + 128, 512 * hh:512 * hh + 512]),
                        writes=[xin.keys(i)[hh]], sem=("xin", i, hh))
            for k in range(KC):
                b = k % 2
                for i in range(2):
                    tr(bank(b)[:, 128 * i:128 * i + 128], xin.cols(i, 128 * k, 128 * k + 128), ident,
                       xin.keys(i) + ["cst"], [("ps", b)])
                cp(xf[:, k, 256 * half:256 * half + 256], bank(b)[:, 0:256], [("ps", b)], [("xf", k)])
                cp(xb[:, k, 256 * half:256 * half + 256], xf[:, k, 256 * half:256 * half + 256], [("xf", k)], [("xb", k)],
                   eng="act")
        for hh in range(2):
            P.dma("sp", lambda e, st=st, hh=hh: e.dma_start(
                out=xin_s.cols(0, 512 * hh, 512 * hh + 512, slice(0, 96)),
                in_=I["x_prompt"][st * TP + 416:st * TP + 512, 512 * hh:512 * hh + 512]),
                writes=[xin_s.keys(0)[hh]], sem=("xin_s", hh))
            P.dma("sp", lambda e, st=st, hh=hh: e.dma_start(
                out=xin_s.cols(0, 512 * hh, 512 * hh + 512, slice(96, 128)),
                in_=I["x_sample"][st * TS:st * TS + TS, 512 * hh:512 * hh + 512]),
                writes=[xin_s.keys(0)[hh]], sem=("xin_s", hh))
        for k in range(KC):
            b = 2 + k % 2
            tr(bank(b)[:, 0:128], xin_s.cols(0, 128 * k, 128 * k + 128), ident, xin_s.keys(0) + ["cst"], [("ps", b)])
            cp(xf[:, k, TP:T], bank(b)[:, 96:128], [("ps", b)], [("xf", k)])
            cp(xb[:, k, TP:T], xf[:, k, TP:T], [("xf", k)], [("xb", k)], eng="act")
        for l in range(NLAYERS):
            ffn_ln(l, I["ffn1_w_in"], I["ffn1_w_out"], 0)
            if "nomix" not in DBG:
                mixing(l, st)
            if "noffn2" not in DBG:
                ffn_ln(l, I["ffn2_w_in"], I["ffn2_w_out"], 2)
        for half in range(2):
            for i in range(2):
                ti = 2 * half + i
                for hh in range(2):
                    b = hh
                    for kk in range(4):
                        k = 4 * hh + kk
                        tr(bank(b)[:, 128 * kk:128 * kk + 128], xf[:, k, 128 * ti:128 * ti + 128], ident,
                           [("xf", k), "cst"], [("ps", b)])
                    cp(xin.cols(i, 512 * hh, 512 * hh + 512), bank(b), [("ps", b)], [xin.keys(i)[hh]], eng="dve" if hh == 0 else "act")
                    P.dma("sp", lambda e, i=i, ti=ti, st=st, hh=hh: e.dma_start(
                        out=O["y_prompt"][st * TP + 128 * ti:st * TP + 128 * ti + 128, 512 * hh:512 * hh + 512],
                        in_=xin.cols(i, 512 * hh, 512 * hh + 512)),
                        reads=[xin.keys(i)[hh]], sem=("xin", i, hh))
        for hh in range(2):
            b = 2 + hh
            for kk in range(4):
                k = 4 * hh + kk
                tr(bank(b)[:, 128 * kk:128 * kk + 128], xf[:, k, T - 128:T], ident, [("xf", k), "cst"], [("ps", b)])
            cp(xin_s.cols(0, 512 * hh, 512 * hh + 512, slice(96, 128)), bank(b)[96:128, :], [("ps", b)], [xin_s.keys(0)[hh]])
            P.dma("sp", lambda e, st=st, hh=hh: e.dma_start(
                out=O["y_sample"][st * TS:st * TS + TS, 512 * hh:512 * hh + 512],
                in_=xin_s.cols(0, 512 * hh, 512 * hh + 512, slice(96, 128))),
                reads=[xin_s.keys(0)[hh]], sem=("xin_s", hh))

    P.final_wait_all("sp")
    P.emit(es)
    es.close()
    return nc, P


_CACHE = {}


def make_in_maps(inputs, cores):
    consts = make_consts()
    f = lambda a: np.ascontiguousarray(a, dtype=np.float32)
    shared = {}
    for name, shp in IN_SPECS:
        if name in ("x_prompt", "x_sample", "mem_prompt", "consts") or name.startswith("cache_") or name.startswith("state_"):
            continue
        shared[name] = f(np.asarray(inputs[name]).reshape(shp))
    in_maps = []
    for c in cores:
        d = dict(shared)
        sl = slice(16 * c, 16 * c + 16)
        d["x_prompt"] = f(inputs["x_prompt"][c])
        d["x_sample"] = f(np.asarray(inputs["x_sample"][sl]).reshape(128, D))
        d["mem_prompt"] = f(inputs["mem_prompt"][c])
        d["cache_mem_k"] = f(np.asarray(inputs["cache_mem_k"][:, sl]).reshape(DEPTH, 16, 256, 256))
        d["cache_mem_v"] = f(np.asarray(inputs["cache_mem_v"][:, sl]).reshape(DEPTH, 16, 256, 256))
        d["state_rwkv"] = f(inputs["state_rwkv"][:, sl])
        d["state_rwkv_shift"] = f(np.asarray(inputs["state_rwkv_shift"][:, sl]).reshape(DEPTH, 16, 14, 128))
        d["state_pool"] = f(inputs["state_pool"][:, sl])
        d["state_delta"] = f(inputs["state_delta"][:, sl])
        d["state_delta_conv"] = f(inputs["state_delta_conv"][:, sl])
        d["consts"] = consts
        in_maps.append(d)
    return in_maps


def kernel(**inputs):
    n = 8
    if "nc" not in _CACHE:
        _CACHE["nc"] = build()[0]
    nc = _CACHE["nc"]
    in_maps = make_in_maps(inputs, range(n))
    res = run_bass_kernel_spmd(nc, in_maps, core_ids=list(range(n)))
    R_ = res.results
    cat = lambda name, shp: np.concatenate([R_[c][name].reshape(shp) for c in range(n)], axis=1)
    out = (
        np.stack([R_[c]["y_prompt"] for c in range(n)], 0),
        np.concatenate([R_[c]["y_sample"].reshape(16, 8, D) for c in range(n)], 0),
        cat("p_rwkv", (DEPTH, 1, 8, 64, 64)), cat("p_rwkv_shift", (DEPTH, 1, 1792)), cat("p_pool", (DEPTH, 1, 15, 512)),
        cat("p_delta", (DEPTH, 1, 4, 128, 128)), cat("p_delta_conv", (DEPTH, 1, 3, 1536)),
        cat("p_mem_k", (DEPTH, 1, 256, 4, 64)), cat("p_mem_v", (DEPTH, 1, 256, 4, 64)),
        cat("s_rwkv", (DEPTH, 16, 8, 64, 64)), cat("s_rwkv_shift", (DEPTH, 16, 1792)), cat("s_pool", (DEPTH, 16, 15, 512)),
        cat("s_delta", (DEPTH, 16, 4, 128, 128)), cat("s_delta_conv", (DEPTH, 16, 3, 1536)),
    )
    return out
```

```python
import os
import numpy as np
from contextlib import ExitStack
import concourse.bass as bass
import concourse.mybir as mybir
from concourse.bass_utils import run_bass_kernel_spmd

F32 = mybir.dt.float32
BF16 = mybir.dt.bfloat16
AF = mybir.ActivationFunctionType
ALU = mybir.AluOpType

ENGS = ("pe", "act", "dve", "pool", "sp")


class _Op:
    __slots__ = ("fn", "raw", "oth", "needs_inc", "count", "dsem")

    def __init__(self, fn, raw, oth, dsem=None):
        self.fn = fn
        self.raw = raw
        self.oth = oth
        self.needs_inc = False
        self.count = 0
        self.dsem = dsem


class Prog:
    def __init__(self, nc):
        self.nc = nc
        self.ops = {e: [] for e in ENGS}
        self.last_w = {}
        self.readers = {}
        self.dcount = {}

    def _deps(self, reads, writes):
        raw = set()
        oth = set()
        lw = self.last_w
        for k in reads:
            t = lw.get(k)
            if t is not None:
                raw.add(t)
        for k in writes:
            t = lw.get(k)
            if t is not None:
                oth.add(t)
            r = self.readers.get(k)
            if r:
                oth.update(r.values())
        return raw, oth

    def _fix(self, deps):
        out = set()
        for t in deps:
            if t[0] == "D":
                out.add(("D", t[1], self.dcount[t[1]]))
            else:
                out.add(t)
        return out

    def _mark(self, tok, rk, reads, writes):
        for k in reads:
            d = self.readers.get(k)
            if d is None:
                d = self.readers[k] = {}
            d[rk] = tok
        for k in writes:
            self.last_w[k] = tok
            self.readers[k] = {}

    def op(self, eng, fn, reads=(), writes=()):
        self.nrec = getattr(self, "nrec", 0) + 1
        if self.nrec > int(os.environ.get("KCUT", "100000000")) or getattr(self, "cut", False):
            return None
        if eng != "pe":
            pk = [k for k in reads if isinstance(k, tuple) and k[0] == "ps"]
            if pk:
                writes = list(writes) + pk
        raw, oth = self._deps(reads, writes)
        o = _Op(fn, self._fix(raw), self._fix(oth))
        lst = self.ops[eng]
        tok = ("E", eng, len(lst))
        lst.append(o)
        self._mark(tok, eng, reads, writes)
        return o

    def dma(self, q, fn, reads=(), writes=(), sem="dma"):
        self.nrec = getattr(self, "nrec", 0) + 1
        if self.nrec > int(os.environ.get("KCUT", "100000000")) or getattr(self, "cut", False):
            return None
        raw, oth = self._deps(reads, writes)
        o = _Op(fn, self._fix(raw), self._fix(oth), dsem=sem)
        self.dcount[sem] = self.dcount.get(sem, 0) + 16
        tok = ("D", sem, self.dcount[sem])
        self.ops[q].append(o)
        self._mark(tok, ("D", sem), reads, writes)
        return o

    def ck(self, name):
        if os.environ.get("KSTOP", "") == name:
            self.cut = True

    def final_wait_all(self, q="sp"):
        deps = set(("D", k, c) for k, c in self.dcount.items())
        self.ops[q].append(_Op(None, deps, set()))

    def emit(self, es):
        nc = self.nc
        eff = {}
        for e in ENGS:
            for i, o in enumerate(self.ops[e]):
                ds = set()
                for t in o.raw:
                    if t[0] == "E" and t[1] == e and e == "pe":
                        continue
                    ds.add(t)
                for t in o.oth:
                    if t[0] == "E" and t[1] == e and e == "pe":
                        continue
                    ds.add(t)
                eff[(e, i)] = ds
                for t in ds:
                    if t[0] == "E":
                        self.ops[t[1]][t[2]].needs_inc = True
        for e in ENGS:
            c = 0
            for o in self.ops[e]:
                if o.dsem is None and o.needs_inc:
                    c += 1
                o.count = c
        esem = {e: es.enter_context(nc.semaphore("s_" + e)) for e in ENGS}
        dsem = {k: es.enter_context(nc.semaphore("d_%d" % i)) for i, k in enumerate(self.dcount)}
        block = es.enter_context(nc.Block())
        engobj = {"pe": block.tensor, "act": block.scalar, "dve": block.vector, "pool": block.gpsimd,
                  "sp": block.sync}
        self.n_wait = 0
        self.n_ins = 0

        def run(e):
            def body(eng):
                have = {}
                for i, o in enumerate(self.ops[e]):
                    need = {}
                    for t in eff[(e, i)]:
                        if t[0] == "E":
                            s = ("E", t[1])
                            v = self.ops[t[1]][t[2]].count
                        else:
                            s = ("D", t[1])
                            v = t[2]
                        if v > need.get(s, 0):
                            need[s] = v
                    for s, v in need.items():
                        if have.get(s, 0) >= v:
                            continue
                        have[s] = v
                        sem = esem[s[1]] if s[0] == "E" else dsem[s[1]]
                        eng.wait_ge(sem, v)
                        self.n_wait += 1
                    if o.fn is None:
                        continue
                    ins = o.fn(eng)
                    self.n_ins += 1
                    if o.dsem is not None:
                        ins.then_inc(dsem[o.dsem], 16)
                    elif o.needs_inc:
                        ins.then_inc(esem[e], 1)
            engobj[e](body)

        for e in ENGS:
            if self.ops[e]:
                run(e)


D = 1024
KC = 8
DEPTH = 4
SEQ = 2048
NST = 4
TP = 512
TS = 32
NSQ = 4
T = TP + TS
DFF = 2048
ALPHA = (2.0 * DEPTH) ** 0.25
LN_EPS = 1e-5
NSLOT = 4
SLOT_ELEMS = 4096
A_COLS = 1792
C0 = float(np.exp(-0.5))
OFF_B = 1792
OFF_C = 2304
OFF_M = 4360
OFF_G = 4616
N_IN = 8712
NPC = 176
NF = 22
NB = 41
FW = 560

DBG = set(x for x in os.environ.get("KDBG", "").split(",") if x)
ACTIVE = os.environ.get("KACTIVE", "abcm")
NLAYERS = int(os.environ.get("KLAYERS", "4"))

C_ID, C_ONE, C_MUS, C_MLS, C_MUI, C_SCAN, C_BLK, C_PFIX, C_SEL4, C_NMUS, C_NMLS = 0, 128, 256, 320, 384, 448, 992, 1120, 1184, 1696, 1760
NCST = 1824
NCSTB = 1184

OUT_SPECS = [
    ("y_prompt", [SEQ, D]), ("y_sample", [128, D]),
    ("p_rwkv", [DEPTH, 8, 64, 64]), ("p_rwkv_shift", [DEPTH, 14, 128]), ("p_pool", [DEPTH, 15, 512]),
    ("p_delta", [DEPTH, 4, 128, 128]), ("p_delta_conv", [DEPTH, 3, 1536]),
    ("p_mem_k", [DEPTH, 256, 256]), ("p_mem_v", [DEPTH, 256, 256]),
    ("s_rwkv", [DEPTH, 16, 8, 64, 64]), ("s_rwkv_shift", [DEPTH, 16, 14, 128]), ("s_pool", [DEPTH, 16, 15, 512]),
    ("s_delta", [DEPTH, 16, 4, 128, 128]), ("s_delta_conv", [DEPTH, 16, 3, 1536]),
]
IN_SPECS = [
    ("x_prompt", [SEQ, D]), ("x_sample", [128, D]), ("mem_prompt", [256, D]),
    ("cache_mem_k", [DEPTH, 16, 256, 256]), ("cache_mem_v", [DEPTH, 16, 256, 256]),
    ("state_rwkv", [DEPTH, 16, 8, 64, 64]), ("state_rwkv_shift", [DEPTH, 16, 14, 128]),
    ("state_pool", [DEPTH, 16, 15, 512]), ("state_delta", [DEPTH, 16, 4, 128, 128]),
    ("state_delta_conv", [DEPTH, 16, 3, 1536]),
    ("w_in", [DEPTH, D, N_IN]), ("rwkv_mu", [DEPTH, A_COLS]), ("rwkv_w0", [DEPTH, 512]),
    ("rwkv_w_up", [DEPTH, 64, 512]), ("rwkv_a0", [DEPTH, 512]), ("rwkv_a_up", [DEPTH, 64, 512]),
    ("rwkv_g_up", [DEPTH, 128, 512]), ("rwkv_k_k", [DEPTH, 512]), ("rwkv_k_a", [DEPTH, 512]),
    ("rwkv_r_k", [DEPTH, 512]), ("rwkv_gn_w", [DEPTH, 512]), ("rwkv_gn_b", [DEPTH, 512]),
    ("pool_w", [DEPTH, 4, 128, 128]), ("pool_scale", [DEPTH, 512]), ("delta_conv_w", [DEPTH, 4, 1536]),
    ("delta_a_log", [DEPTH, 4]), ("delta_dt_bias", [DEPTH, 4]), ("delta_norm_w", [DEPTH, 128]),
    ("mem_w_kv", [DEPTH, D, 512]), ("w_branch", [DEPTH, 1792, D]), ("w_out", [DEPTH, D, D]),
    ("ffn1_w_in", [DEPTH, D, 2 * DFF]), ("ffn1_w_out", [DEPTH, DFF, D]),
    ("ffn2_w_in", [DEPTH, D, 2 * DFF]), ("ffn2_w_out", [DEPTH, DFF, D]),
    ("ln_g", [DEPTH, 3, D]), ("ln_b", [DEPTH, 3, D]), ("consts", [128, NCST]),
]


def make_consts():
    c = np.zeros((128, NCST), np.float32)
    c[:, C_ID:C_ID + 128] = np.eye(128, dtype=np.float32)
    c[:, C_ONE:C_ONE + 128] = 1.0
    p = np.arange(64)[:, None]
    f = np.arange(64)[None, :]
    c[0:64, C_MUS:C_MUS + 64] = (p < f)
    c[0:64, C_MLS:C_MLS + 64] = (f < p)
    c[0:64, C_MUI:C_MUI + 64] = (p <= f)
    c[0:64, C_NMUS:C_NMUS + 64] = -1.0 * (p < f)
    c[0:64, C_NMLS:C_NMLS + 64] = -1.0 * (f < p)
    t = np.arange(T)
    scan = np.ones(T, np.float32)
    scan[:TP][t[:TP] % 64 == 0] = 0.0
    scan[TP:][(t[TP:] - TP) % 8 == 0] = 0.0
    c[:, C_SCAN:C_SCAN + T] = scan[None, :]
    blk = np.zeros((128, 128), np.float32)
    blk[0:64, 0:64] = 1.0
    blk[64:128, 64:128] = 1.0
    c[:, C_BLK:C_BLK + 128] = blk
    for g, w in enumerate((2, 4, 8, 16)):
        tt = np.arange(16)
        c[:, C_PFIX + 16 * g:C_PFIX + 16 * g + 16] = (w / np.minimum(tt + 1, w))[None, :]
    for h in range(4):
        c[h, C_SEL4 + 128 * h:C_SEL4 + 128 * h + 128] = 1.0
    return c


def build():
    nc = bass.Bass("TRN2", target_bir_lowering=False)
    es = ExitStack()
    P = Prog(nc)
    I = {n: nc.dram_tensor(n, list(s), F32, kind="ExternalInput").ap() for n, s in IN_SPECS}
    O = {n: nc.dram_tensor(n, list(s), F32, kind="ExternalOutput").ap() for n, s in OUT_SPECS}

    def sb(name, shape, dt=F32):
        return es.enter_context(nc.sbuf_tensor(name, list(shape), dt))

    xf = sb("xf", [128, KC, T])
    xb = sb("xb", [128, KC, T], BF16)
    mg = sb("mg", [128, KC, T])
    cst = sb("cst", [128, NCST])
    cstb = sb("cstb", [128, NCSTB], BF16)
    wring = sb("wring", [128, NSLOT, SLOT_ELEMS], BF16)
    SFt = [sb("sf%d" % i, [128, FW]) for i in range(NF)]
    SBt = [sb("sb%d" % i, [128, T], BF16) for i in range(NB)]
    stg = [sb("stg%d" % i, [128, 128]) for i in range(2)]

    class _XV:
        def __init__(self, t0):
            self.t0 = t0

        def cols(self, i, a, b, rows=slice(0, 128)):
            h = a // 512
            assert (b - 1) // 512 == h
            return SFt[self.t0 + 2 * i + h][rows, a - 512 * h:b - 512 * h]

        def keys(self, i):
            return [("sf", self.t0 + 2 * i), ("sf", self.t0 + 2 * i + 1)]
    xin = _XV(0)
    xin_s = _XV(4)
    pcol = sb("pcol", [128, DEPTH, NPC])
    carry_pa = sb("carry_pa", [128, DEPTH, 14])
    carry_pb = sb("carry_pb", [128, DEPTH, 4, 15])
    carry_pc = sb("carry_pc", [128, DEPTH, 12, 3])
    Srw = sb("Srw", [128, DEPTH, 4, 64])
    Sgd = sb("Sgd", [128, DEPTH, 4, 128])
    memK = sb("memK", [128, DEPTH, 2, 256], BF16)
    memV = sb("memV", [128, DEPTH, 2, 256], BF16)
    gparm = sb("gparm", [4, DEPTH, 2])
    mpT = sb("mpT", [128, KC, 256], BF16)
    pall = es.enter_context(nc.psum_tensor("pall", [128, 4096], F32))
    pallb = pall.bitcast(BF16)

    ident = cst[:, C_ID:C_ID + 128]
    identb = cstb[:, C_ID:C_ID + 128]
    onesb = cstb[:, C_ONE:C_ONE + 128]
    blkb = cstb[:, C_BLK:C_BLK + 128]

    def SF(i):
        return SFt[i], ("sf", i)

    def SB(i):
        return SBt[i], ("sb", i)

    def bank(b):
        return pall[:, 512 * b:512 * b + 512]

    def dbv(i):
        return pall[:, 1024 * i:1024 * i + T]

    def dbk(i):
        return [("ps", 2 * i), ("ps", 2 * i + 1)]

    OP = P.op

    def mm(out, lhsT, rhs, start, stop, reads, writes):
        P.op("pe", lambda e: e.matmul(out, lhsT, rhs, start=start, stop=stop), reads=reads, writes=writes)

    def tr(out, in_, idn, reads, writes):
        P.op("pe", lambda e: e.transpose(out, in_, idn), reads=reads, writes=writes)

    def act(out, in_, func, reads, writes, bias=None, scale=None, accum=None):
        kw = {}
        if bias is not None:
            kw["bias"] = bias
        if scale is not None:
            kw["scale"] = scale
        if accum is not None:
            kw["accum_out"] = accum
        P.op("act", lambda e: e.activation(out, in_, func, **kw), reads=reads, writes=writes)

    def tt(out, a, b, op, reads, writes, eng="dve"):
        P.op(eng, lambda e: e.tensor_tensor(out, a, b, op), reads=reads, writes=writes)

    def stt(out, a, s, b, op0, op1, reads, writes):
        P.op("dve", lambda e: e.scalar_tensor_tensor(out, a, s, b, op0, op1), reads=reads, writes=writes)

    def ts(out, a, s1, s2, op0, op1, reads, writes, eng="dve"):
        if op1 is None:
            P.op(eng, lambda e: e.tensor_scalar(out, a, s1, None, op0), reads=reads, writes=writes)
        else:
            P.op(eng, lambda e: e.tensor_scalar(out, a, s1, s2, op0, op1), reads=reads, writes=writes)

    def cp(out, in_, reads, writes, eng="dve"):
        if eng == "act":
            P.op("act", lambda e: e.activation(out, in_, AF.Copy), reads=reads, writes=writes)
        else:
            P.op(eng, lambda e: e.tensor_copy(out, in_), reads=reads, writes=writes)

    def memset(ap, v, writes, eng="dve"):
        P.op(eng, lambda e: e.memset(ap, v), writes=writes)

    P.dma("sp", lambda e: e.dma_start(out=cst[:], in_=I["consts"]), writes=["cst"], sem="c0")
    cp(cstb[:], cst[:, 0:NCSTB], ["cst"], ["cstb"])
    memset(carry_pa[:], 0.0, ["carry_pa"])
    memset(carry_pb[:], 0.0, ["carry_pb"])
    memset(carry_pc[:], 0.0, ["carry_pc"])
    memset(Srw[:], 0.0, ["Srw"])
    memset(Sgd[:], 0.0, ["Sgd"])
    for i in range(2):
        memset(stg[i][:], 0.0, [("stg", i)])
    P.dma("sp", lambda e: e.dma_start(out=gparm[:, :, 0:1], in_=I["delta_a_log"].rearrange("l (h o) -> h l o", o=1), allow_slow_non_contiguous=True),
          writes=["gparm"], sem="c1")
    P.dma("sp", lambda e: e.dma_start(out=gparm[:, :, 1:2], in_=I["delta_dt_bias"].rearrange("l (h o) -> h l o", o=1), allow_slow_non_contiguous=True),
          writes=["gparm"], sem="c1")

    PC = {}
    _o = [0]

    def _reg(name, n):
        PC[name] = _o[0]
        _o[0] += n
    for nm, n in (("ln_g", 24), ("ln_b", 24), ("mu", 14), ("w0", 4), ("a0", 4), ("k_k", 4), ("k_a", 4), ("r_k", 4),
                  ("gn_w", 4), ("gn_b", 4), ("pscale", 4), ("conv", 48), ("nrm", 1), ("omka", 4), ("nmu", 14)):
        _reg(nm, n)
    assert _o[0] <= NPC
    stgn = [0]

    def rows_to_cols(l, items):
        groups = []
        cur = []
        cnt = 0
        for it in items:
            if cnt + it[1] > 128:
                groups.append(cur)
                cur = []
                cnt = 0
            cur.append((cnt, it))
            cnt += it[1]
        groups.append(cur)
        for gidx, g in enumerate(groups):
            si = stgn[0] % 2
            stgn[0] += 1
            st_t = stg[si]
            for (r0, (ap, n, name)) in g:
                P.dma("sp", lambda e, ap=ap, r0=r0, n=n, st_t=st_t: e.dma_start(out=st_t[r0:r0 + n, :], in_=ap),
                      writes=[("stg", si)], sem=("stg", si))
            tr(bank(7)[:, 0:128], st_t[:, :], ident, [("stg", si), "cst"], [("ps", 7)])
            for (r0, (ap, n, name)) in g:
                cp(pcol[:, l, PC[name]:PC[name] + n], bank(7)[:, r0:r0 + n], [("ps", 7)], [("pcol", l)])

    for l in range(DEPTH):
        r128 = lambda ap: ap.rearrange("(k p) -> k p", p=128)
        items = [
            (I["ln_g"][l].rearrange("i (k p) -> (i k) p", p=128), 24, "ln_g"),
            (I["ln_b"][l].rearrange("i (k p) -> (i k) p", p=128), 24, "ln_b"),
            (r128(I["rwkv_mu"][l]), 14, "mu"), (r128(I["rwkv_w0"][l]), 4, "w0"), (r128(I["rwkv_a0"][l]), 4, "a0"),
            (r128(I["rwkv_k_k"][l]), 4, "k_k"), (r128(I["rwkv_k_a"][l]), 4, "k_a"), (r128(I["rwkv_r_k"][l]), 4, "r_k"),
            (r128(I["rwkv_gn_w"][l]), 4, "gn_w"), (r128(I["rwkv_gn_b"][l]), 4, "gn_b"),
            (r128(I["pool_scale"][l]), 4, "pscale"),
            (I["delta_conv_w"][l].rearrange("j (k p) -> (j k) p", p=128), 48, "conv"),
            (I["delta_norm_w"][l].rearrange("(k p) -> k p", p=128), 1, "nrm"),
        ]
        rows_to_cols(l, items)
        ts(pcol[:, l, PC["omka"]:PC["omka"] + 4], pcol[:, l, PC["k_a"]:PC["k_a"] + 4], -1.0, 1.0, ALU.mult, ALU.add,
           [("pcol", l)], [("pcol", l)])
        ts(pcol[:, l, PC["nmu"]:PC["nmu"] + 14], pcol[:, l, PC["mu"]:PC["mu"] + 14], -1.0, 1.0, ALU.mult, ALU.add,
           [("pcol", l)], [("pcol", l)])

    def pc_(l, name, j=0):
        return pcol[:, l, PC[name] + j:PC[name] + j + 1]

    wstate = {"n": 0}

    def wslot():
        s = wstate["n"] % NSLOT
        wstate["n"] += 1
        return s

    def wload(src2d, K, ncols):
        kc = K // 128
        assert kc * ncols <= SLOT_ELEMS and ncols <= 2048
        s = wslot()
        view = wring[:, s, 0:kc * ncols].rearrange("p (k n) -> p k n", k=kc)
        P.dma("pool", lambda e: e.dma_start(out=view, in_=src2d.rearrange("(k p) n -> p k n", p=128)),
              writes=[("w", s)], sem=("w", s))
        return view, ("w", s)

    def dense_chunk(db, wv, wk, kcn, col, rhs_fn, rhs_keys):
        for (o0, n, t0, bk) in ((0, TP, 0, 0), (TP, TS, TP, 1)):
            for k in range(kcn):
                mm(pall[:, 1024 * db + o0:1024 * db + o0 + n], wv[:, k, col:col + 128], rhs_fn(k)[:, t0:t0 + n],
                   k == 0, k == kcn - 1, [wk, rhs_keys[k]], [("ps", 2 * db + bk)])

    xb_fn = lambda k: xb[:, k, :]
    xb_keys = [("xb", k) for k in range(KC)]

    def out_proj_ln(l, lni, wsrc, K, rhs_fn, rhs_keys, cres):
        kcn = K // 128
        ncol = SLOT_ELEMS // kcn
        if ncol > 1024:
            ncol = 1024
        per = ncol // 128
        ybf, kyb = SB(NB - 1)
        ysq, kys = SB(NB - 2)
        for j in range(D // ncol):
            wo, ko = wload(wsrc[:, ncol * j:ncol * j + ncol], K, ncol)
            for i in range(per):
                m = per * j + i
                pi = m % 2
                dense_chunk(pi, wo, ko, kcn, 128 * i, rhs_fn, rhs_keys)
                stt(xf[:, m, :], dbv(pi), cres, xf[:, m, :], ALU.mult, ALU.add, dbk(pi) + [("xf", m)], [("xf", m)])
                act(ybf[:], xf[:, m, :], AF.Copy, [("xf", m)], [kyb])
                act(ysq[:], xf[:, m, :], AF.Square, [("xf", m)], [kys])
                for (o0, n, bk) in ((0, TP, 0), (TP, TS, 1)):
                    mm(pall[:, 2048 + o0:2048 + o0 + n], onesb, ybf[:, o0:o0 + n], m == 0, m == 7, ["cstb", kyb],
                       [("ps", 4 + bk)])
                    mm(pall[:, 3072 + o0:3072 + o0 + n], onesb, ysq[:, o0:o0 + n], m == 0, m == 7, ["cstb", kys],
                       [("ps", 6 + bk)])
        eps = LN_EPS / (ALPHA * ALPHA)
        (mean, kmean), (msq, kmsq), (var, kvar), (rstd, krstd), (nmr, knmr) = [SF(NF - 1 - i) for i in range(5)]
        W = slice(0, T)
        act(mean[:, W], dbv(2), AF.Copy, dbk(2), [kmean], scale=1.0 / D)
        tt(msq[:, W], mean[:, W], mean[:, W], ALU.mult, [kmean], [kmsq])
        stt(var[:, W], dbv(3), 1.0 / D, msq[:, W], ALU.mult, ALU.subtract, dbk(3) + [kmsq], [kvar])
        act(var[:, W], var[:, W], AF.Ln, [kvar], [kvar], bias=eps)
        act(rstd[:, W], var[:, W], AF.Exp, [kvar], [krstd], scale=-0.5)
        stt(nmr[:, W], mean[:, W], -1.0, rstd[:, W], ALU.mult, ALU.mult, [kmean, krstd], [knmr])
        for m in range(KC):
            ta, kta = SF(NF - 6 - (m % 2))
            tt(ta[:, W], xf[:, m, :], rstd[:, W], ALU.mult, [("xf", m), krstd], [kta])
            tt(ta[:, W], ta[:, W], nmr[:, W], ALU.add, [kta, knmr], [kta])
            gcol = pc_(l, "ln_g", lni * 8 + m)
            bcol = pc_(l, "ln_b", lni * 8 + m)
            act(xf[:, m, :], ta[:, W], AF.Identity, [kta, ("pcol", l)], [("xf", m)], bias=bcol, scale=gcol)
            act(xb[:, m, :], ta[:, W], AF.Identity, [kta, ("pcol", l)], [("xb", m)], bias=bcol, scale=gcol)

    def ffn_ln(l, w_in, w_out, lni):
        for j in range(4):
            wg, kg = wload(w_in[l][:, 512 * j:512 * j + 512], D, 512)
            wu, ku = wload(w_in[l][:, DFF + 512 * j:DFF + 512 * j + 512], D, 512)
            for i in range(4):
                c = 4 * j + i
                dense_chunk(0, wg, kg, KC, 128 * i, xb_fn, xb_keys)
                dense_chunk(1, wu, ku, KC, 128 * i, xb_fn, xb_keys)
                sgt, ksg = SF(c % 2)
                hc, khc = SB(c)
                act(sgt[:, 0:T], dbv(0), AF.Silu, dbk(0), [ksg])
                tt(hc[:], sgt[:, 0:T], dbv(1), ALU.mult, dbk(1) + [ksg], [khc])
        out_proj_ln(l, lni, w_out[l], DFF, lambda c: SBt[c][:], [("sb", c) for c in range(16)], 0.5 / ALPHA)

    def proj_cols(l, col0, ncols, consume):
        done = 0
        ci = 0
        while done < ncols:
            n = min(512, ncols - done)
            wv, wk = wload(I["w_in"][l][:, col0 + done:col0 + done + n], D, n)
            for i in range((n + 127) // 128):
                w = min(128, n - 128 * i)
                db = ci % 2
                for (o0, nn, t0, bk) in ((0, TP, 0, 0), (TP, TS, TP, 1)):
                    for k in range(KC):
                        mm(pall[0:w, 1024 * db + o0:1024 * db + o0 + nn], wv[:, k, 128 * i:128 * i + w],
                           xb[:, k, t0:t0 + nn], k == 0, k == KC - 1, [wk, ("xb", k)], [("ps", 2 * db + bk)])
                consume(ci, db, w)
                ci += 1
            done += n

    def merge_branch(l, bi, o_fn, o_keys, nk, r0):
        ncol = min(1024, SLOT_ELEMS // nk)
        for j in range(D // ncol):
            wb, kb = wload(I["w_branch"][l][r0:r0 + 128 * nk, ncol * j:ncol * j + ncol], 128 * nk, ncol)
            for jj in range(ncol // 512):
                gcol0 = OFF_G + bi * D + ncol * j + 512 * jj
                wg, kg = wload(I["w_in"][l][:, gcol0:gcol0 + 512], D, 512)
                for i in range(4):
                    m = (ncol * j + 512 * jj) // 128 + i
                    dense_chunk(0, wg, kg, KC, 128 * i, xb_fn, xb_keys)
                    dense_chunk(1, wb, kb, nk, 512 * jj + 128 * i, o_fn, o_keys)
                    gt, kgt = SF(NF - 8 - (m % 2))
                    act(gt[:, 0:T], dbv(0), AF.Sigmoid, dbk(0), [kgt])
                    tt(gt[:, 0:T], gt[:, 0:T], dbv(1), ALU.mult, dbk(1) + [kgt], [kgt])
                    tt(mg[:, m, :], mg[:, m, :], gt[:, 0:T], ALU.add, [kgt, ("mg", m)], [("mg", m)])

    def branch_pool(l, st):
        b0 = 4 * st
        pbE = [SF(i) for i in range(4)]
        pbS = [SF(4 + i) for i in range(4)]
        tmp = [SF(8), SF(9)]
        for g in range(4):
            cp(pbE[g][0][:, 0:15], carry_pb[:, l, g, :], ["carry_pb"], [pbE[g][1]])
        for half in range(2):
            si = stgn[0] % 2
            stgn[0] += 1
            for q in range(2):
                s = 2 * half + q
                P.dma("sp", lambda e, s=s, q=q, si=si: e.dma_start(
                    out=stg[si][60 * q:60 * q + 60, :],
                    in_=I["state_pool"][l, b0 + s].rearrange("j (c p) -> (j c) p", p=128)),
                    writes=[("stg", si)], sem=("stg", si))
            tr(bank(7)[:, 0:128], stg[si][:, :], ident, [("stg", si), "cst"], [("ps", 7)])
            for q in range(2):
                s = 2 * half + q
                for g in range(4):
                    src = bank(7)[:, 60 * q:60 * q + 60].rearrange("p (j c) -> p c j", c=4)[:, g, :]
                    cp(pbS[g][0][:, 23 * s:23 * s + 15], src, [("ps", 7)], [pbS[g][1]])

        def consume(ci, db, w):
            g = ci
            cp(pbE[g][0][:, 15:15 + TP], pall[:, 1024 * db:1024 * db + TP], [("ps", 2 * db)], [pbE[g][1]], eng="act")
            cp(pbS[g][0][:, 0:92].rearrange("p (s j) -> p s j", j=23)[:, :, 15:23],
               pall[:, 1024 * db + TP:1024 * db + T].rearrange("p (s j) -> p s j", j=8),
               [("ps", 2 * db + 1)], [pbS[g][1]])
        proj_cols(l, OFF_B, 512, consume)
        pooled = [SB(16 + g) for g in range(4)]
        obT = [SB(20 + g) for g in range(4)]
        pw, kpw = None, None
        s_ = wslot()
        pw = wring[:, s_, 0:512].rearrange("p (g d) -> p g d", g=4)
        kpw = ("w", s_)
        P.dma("pool", lambda e: e.dma_start(out=pw, in_=I["pool_w"][l].rearrange("g c d -> c g d")),
              writes=[kpw], sem=kpw)
        for g in range(4):
            E, kE = pbE[g]
            S_, kS = pbS[g]
            Sv = lambda t_, a, b: t_[:, 0:92].rearrange("p (s j) -> p s j", j=23)[:, :, a:b]
            src, ksrc = E, kE
            srcs, ksrcs = S_, kS
            sh = 1
            for step in range(g + 1):
                dst, kdst = tmp[step % 2]
                lo = 2 * sh - 1
                tt(dst[:, lo:527], src[:, lo:527], src[:, lo - sh:527 - sh], ALU.add, [ksrc], [kdst])
                src, ksrc = dst, kdst
                sh *= 2
            w = 2 ** (g + 1)
            if st == 0:
                tt(src[:, 15:31], src[:, 15:31], cst[:, C_PFIX + 16 * g:C_PFIX + 16 * g + 16], ALU.mult,
                   [ksrc, "cst"], [ksrc])
            stt(pooled[g][0][:, 0:TP], src[:, 15:527], 1.0 / w, E[:, 15:527], ALU.mult, ALU.subtract,
                [ksrc, kE], [pooled[g][1]])
            sh = 1
            src2, ksrc2 = S_, kS
            tmps = [SF(10), SF(11)]
            for step in range(g + 1):
                dst, kdst = tmps[step % 2]
                lo = 2 * sh - 1
                tt(Sv(dst, lo, 23), Sv(src2, lo, 23), Sv(src2, lo - sh, 23 - sh), ALU.add, [ksrc2], [kdst])
                src2, ksrc2 = dst, kdst
                sh *= 2
            stt(pooled[g][0][:, TP:T].rearrange("p (s j) -> p s j", j=8), Sv(src2, 15, 23), 1.0 / w, Sv(S_, 15, 23),
                ALU.mult, ALU.subtract, [ksrc2, kS], [pooled[g][1]])
            for (o0, n, bk) in ((0, TP, 0), (TP, TS, 1)):
                mm(pall[:, 2048 + o0:2048 + o0 + n], pw[:, g, :], pooled[g][0][:, o0:o0 + n], True, True,
                   [kpw, pooled[g][1]], [("ps", 4 + bk)])
            act(obT[g][0][:], dbv(2), AF.Identity, dbk(2) + [("pcol", l)], [obT[g][1]], scale=pc_(l, "pscale", g))
            cp(carry_pb[:, l, g, :], E[:, 512:527], [kE], ["carry_pb"])
        for half in range(2):
            si = stgn[0] % 2
            stgn[0] += 1
            for q in range(2):
                s = 2 * half + q
                for g in range(4):
                    dst = stg[si][:, 60 * q:60 * q + 60].rearrange("p (j c) -> p c j", c=4)[:, g, :]
                    cp(dst, pbS[g][0][:, 23 * s + 8:23 * s + 23], [pbS[g][1]], [("stg", si)])
            tr(bank(7)[:, 0:128], stg[si][:, :], ident, [("stg", si), "cst"], [("ps", 7)])
            rt, krt = SF(12)
            cp(rt[:, 0:128], bank(7)[:, 0:128], [("ps", 7)], [krt])
            for q in range(2):
                s = 2 * half + q
                P.dma("sp", lambda e, s=s, q=q, rt=rt: e.dma_start(
                    out=O["s_pool"][l, b0 + s].rearrange("j (c p) -> (j c) p", p=128), in_=rt[60 * q:60 * q + 60, 0:128]),
                    reads=[krt], sem=("sfd", 12))
        if st == NST - 1:
            si = stgn[0] % 2
            stgn[0] += 1
            for g in range(4):
                dst = stg[si][:, 0:60].rearrange("p (j c) -> p c j", c=4)[:, g, :]
                cp(dst, carry_pb[:, l, g, :], ["carry_pb"], [("stg", si)])
            tr(bank(7)[:, 0:128], stg[si][:, :], ident, [("stg", si), "cst"], [("ps", 7)])
            rt, krt = SF(12)
            cp(rt[:, 0:128], bank(7)[:, 0:128], [("ps", 7)], [krt])
            P.dma("sp", lambda e, rt=rt: e.dma_start(
                out=O["p_pool"][l].rearrange("j (c p) -> (j c) p", p=128), in_=rt[0:60, 0:128]),
                reads=[krt], sem=("sfd", 12))
        if "b" in ACTIVE:
            merge_branch(l, 1, lambda c: obT[c][0][:], [obT[c][1] for c in range(4)], 4, 512)

    def attn_prep():
        for mt in range(2):
            for hh in range(2):
                P.dma("sp", lambda e, mt=mt, hh=hh: e.dma_start(
                    out=xin.cols(mt, 512 * hh, 512 * hh + 512), in_=I["mem_prompt"][128 * mt:128 * mt + 128, 512 * hh:512 * hh + 512]),
                    writes=[xin.keys(mt)[hh]], sem=("xin", mt, hh))
        for k in range(KC):
            b = k % 2
            for mt in range(2):
                tr(bank(b)[:, 128 * mt:128 * mt + 128], xin.cols(mt, 128 * k, 128 * k + 128), ident,
                   xin.keys(mt) + ["cst"], [("ps", b)])
            cp(mpT[:, k, :], bank(b)[:, 0:256], [("ps", b)], ["mpT"], eng="act" if k % 2 else "dve")
        for l in range(NLAYERS):
            wkv, kkv = wload(I["mem_w_kv"][l], D, 512)
            for mt in range(2):
                for k in range(KC):
                    mm(bank(2 + mt), mpT[:, k, 128 * mt:128 * mt + 128], wkv[:, k, :], k == 0, k == KC - 1,
                       ["mpT", kkv], [("ps", 2 + mt)])
                kvf, kkvf = SF(mt)
                cp(kvf[:, 0:512], bank(2 + mt), [("ps", 2 + mt)], [kkvf])
                cp(memV[:, l, mt, :], kvf[:, 256:512], [kkvf], [("memV", l)], eng="act")
                P.dma("sp", lambda e, l=l, mt=mt, kvf=kvf: e.dma_start(
                    out=O["p_mem_k"][l, 128 * mt:128 * mt + 128, :], in_=kvf[:, 0:256]), reads=[kkvf], sem=("sfd", mt))
                P.dma("sp", lambda e, l=l, mt=mt, kvf=kvf: e.dma_start(
                    out=O["p_mem_v"][l, 128 * mt:128 * mt + 128, :], in_=kvf[:, 256:512]), reads=[kkvf], sem=("sfd", mt))
            for c in range(2):
                for k in range(KC):
                    mm(bank(4 + c)[:, 0:256], wkv[:, k, 128 * c:128 * c + 128], mpT[:, k, :], k == 0, k == KC - 1,
                       ["mpT", kkv], [("ps", 4 + c)])
                cp(memK[:, l, c, :], bank(4 + c)[:, 0:256], [("ps", 4 + c)], [("memK", l)], eng="act" if c else "dve")

    def softmax_pv(l, np_, q_fn, kT_fn, kT_keys, v_fn, v_keys, out_fn, out_keys):
        sc = pall[0:np_, 2048:3072].rearrange("p (h m) -> p h m", h=4)
        hb = lambda h: 2048 + 512 * (h % 2) + 256 * (h // 2)
        ps_ = lambda h: 2 * (h % 2) + h // 2
        for h in range(4):
            mm(pall[0:np_, hb(h):hb(h) + 256], q_fn(h), kT_fn(h), True, True,
               ["qT%d" % (h // 2)] + kT_keys, [("ps", 4 + h % 2)])
        P.ck("sm_scores")
        (mx, kmx), (nb, knb), (ss, kss), (rs, krs) = SF(2), SF(3), SF(4), SF(5)
        OP("dve", lambda e: e.tensor_reduce(mx[0:np_, 0:4], sc, mybir.AxisListType.X, ALU.max),
           reads=[("ps", 4), ("ps", 5)], writes=[kmx])
        ts(nb[0:np_, 0:4], mx[0:np_, 0:4], -0.125, None, ALU.mult, None, [kmx], [knb])
        pf = [SF(6), SF(7)]
        for h in range(4):
            pt, kpt = pf[h // 2]
            act(pt[0:np_, 256 * (h % 2):256 * (h % 2) + 256], pall[0:np_, hb(h):hb(h) + 256], AF.Exp,
                [("ps", 4 + h % 2), knb], [kpt, kss], bias=nb[0:np_, ps_(h):ps_(h) + 1], scale=0.125, accum=ss[0:np_, h:h + 1])
        P.ck("sm_exp")
        OP("dve", lambda e: e.reciprocal(rs[0:np_, 0:4], ss[0:np_, 0:4]), reads=[kss], writes=[krs])
        pn = [SB(0), SB(1)]
        for h in range(4):
            pt, kpt = pf[h // 2]
            pnt, kpn = pn[h // 2]
            ts(pnt[0:np_, 256 * (h % 2):256 * (h % 2) + 256], pt[0:np_, 256 * (h % 2):256 * (h % 2) + 256],
               rs[0:np_, h:h + 1], None, ALU.mult, None, [kpt, krs], [kpn])
        P.ck("sm_norm")
        pT = [SB(2), SB(3)]
        for h in range(4):
            pnt, kpn = pn[h // 2]
            for mt in range(2):
                j = 2 * h + mt
                tr(pallb[:, 6 * 1024 + np_ * j:6 * 1024 + np_ * j + np_],
                   pnt[0:np_, 256 * (h % 2) + 128 * mt:256 * (h % 2) + 128 * mt + 128], identb[0:np_, 0:np_],
                   [kpn, "cstb"], [("ps", 6)])
        for half in range(2):
            cp(pT[half][0][:, 0:4 * np_], pallb[:, 6 * 1024 + 4 * np_ * half:6 * 1024 + 4 * np_ * half + 4 * np_],
               [("ps", 6)], [pT[half][1]], eng="act" if half else "dve")
        P.ck("sm_tr")
        for h in range(4):
            for mt in range(2):
                j = 2 * h + mt
                mm(out_fn(h), v_fn(h, mt), pT[j // 4][0][:, np_ * (j % 4):np_ * (j % 4) + np_], mt == 0, mt == 1,
                   v_keys + [pT[j // 4][1]], out_keys)

    def branch_attn(l, st):
        b0 = 4 * st
        qT = [SB(4), SB(5)]
        omT = [SB(20), SB(21)]

        def consume(ci, db, w):
            cp(qT[ci][0][:], dbv(db), dbk(db), [qT[ci][1], "qT%d" % ci], eng="act" if ci else "dve")
        proj_cols(l, OFF_M, 256, consume)
        P.ck("m_proj")
        for tt_ in range(4):
            softmax_pv(
                l, 128,
                lambda h: qT[h // 2][0][64 * (h % 2):64 * (h % 2) + 64, 128 * tt_:128 * tt_ + 128],
                lambda h: memK[64 * (h % 2):64 * (h % 2) + 64, l, h // 2, :], [("memK", l)],
                lambda h, mt: memV[:, l, mt, 64 * h:64 * h + 64], [("memV", l)],
                lambda h: pall[64 * (h % 2):64 * (h % 2) + 64, 3584 + 128 * (h // 2):3584 + 128 * (h // 2) + 128],
                [("ps", 7)])
            for c in range(2):
                cp(omT[c][0][:, 128 * tt_:128 * tt_ + 128], pall[:, 3584 + 128 * c:3584 + 128 * c + 128], [("ps", 7)],
                   [omT[c][1]], eng="act" if c else "dve")
        P.ck("m_prompt")
        for s_ in range(NSQ):
            kf, kkf = SF(8)
            vf, kvf = SF(9)
            kTs, kkTs = SB(6)
            vS, kvS = SB(7)
            P.dma("sp", lambda e, s_=s_, kf=kf: e.dma_start(
                out=kf[:, 0:512].rearrange("p (t n) -> p t n", t=2),
                in_=I["cache_mem_k"][l, b0 + s_].rearrange("(t p) n -> p t n", p=128)), writes=[kkf], sem=("sfd", 8))
            P.dma("sp", lambda e, s_=s_, vf=vf: e.dma_start(
                out=vf[:, 0:512].rearrange("p (t n) -> p t n", t=2),
                in_=I["cache_mem_v"][l, b0 + s_].rearrange("(t p) n -> p t n", p=128)), writes=[kvf], sem=("sfd", 9))
            cp(vS[:, 0:512], vf[:, 0:512], [kvf], [kvS], eng="act")
            for c in range(2):
                for mt in range(2):
                    tr(bank(4 + c)[:, 128 * mt:128 * mt + 128], kf[:, 256 * mt + 128 * c:256 * mt + 128 * c + 128], ident,
                       [kkf, "cst"], [("ps", 4 + c)])
                cp(kTs[:, 256 * c:256 * c + 256], bank(4 + c)[:, 0:256], [("ps", 4 + c)], [kkTs], eng="act" if c else "dve")
            softmax_pv(
                l, 8,
                lambda h: qT[h // 2][0][64 * (h % 2):64 * (h % 2) + 64, TP + 8 * s_:TP + 8 * s_ + 8],
                lambda h: kTs[64 * (h % 2):64 * (h % 2) + 64, 256 * (h // 2):256 * (h // 2) + 256], [kkTs],
                lambda h, mt: vS[:, 256 * mt + 64 * h:256 * mt + 64 * h + 64], [kvS],
                lambda h: pall[64 * (h % 2):64 * (h % 2) + 64, 3584 + 256 + 32 * (h // 2) + 8 * s_:3584 + 256 + 32 * (h // 2) + 8 * s_ + 8],
                [("ps", 7)])
        for c in range(2):
            cp(omT[c][0][:, TP:T], pall[:, 3584 + 256 + 32 * c:3584 + 256 + 32 * c + 32], [("ps", 7)], [omT[c][1]],
               eng="act" if c else "dve")
        P.ck("m_sample")
        merge_branch(l, 3, lambda c: omT[c][0][:], [omT[c][1] for c in range(2)], 2, 1536)

    def inv_chain(C, H, P1, Q1, X, Pn, Qn):
        W = H * C
        v3 = lambda t_: t_[0:C, 0:W].rearrange("p (h c) -> p h c", h=H)
        tt(v3(X[0]), v3(P1[0]), identb[0:C, 0:C].unsqueeze(1).to_broadcast([C, H, C]), ALU.add, [P1[1], "cstb"], [X[1]])
        nsteps = {64: 5, 8: 2}[C]
        Pc, Qc = P1, Q1
        Pd, Qd = Pn, Qn
        for k in range(nsteps):
            last = k == nsteps - 1
            for h in range(H):
                mm(pall[0:C, 0 + C * h:0 + C * h + C], Pc[0][0:C, C * h:C * h + C], Qc[0][0:C, C * h:C * h + C], True, True,
                   [Pc[1], Qc[1]], [("ps", 0)])
            if not last:
                for h in range(H):
                    mm(pall[0:C, 512 + C * h:512 + C * h + C], Qc[0][0:C, C * h:C * h + C], Pc[0][0:C, C * h:C * h + C],
                       True, True, [Pc[1], Qc[1]], [("ps", 1)])
            cp(Qd[0][0:C, 0:W], pall[0:C, 0:W], [("ps", 0)], [Qd[1]], eng="act")
            if not last:
                cp(Pd[0][0:C, 0:W], pall[0:C, 512:512 + W], [("ps", 1)], [Pd[1]], eng="act")
            for h in range(H):
                mm(pall[0:C, 1024 + C * h:1024 + C * h + C], Qd[0][0:C, C * h:C * h + C], X[0][0:C, C * h:C * h + C],
                   True, True, [Qd[1], X[1]], [("ps", 2)])
            tt(X[0][0:C, 0:W], X[0][0:C, 0:W], pall[0:C, 1024:1024 + W], ALU.add, [X[1], ("ps", 2)], [X[1]])
            Pc, Qc, Pd, Qd = Pd, Qd, Pc, Qc

    def branch_gdn(l, st):
        b0 = 4 * st
        convh, kconvh = SF(0)
        EE, kEE = SF(1)
        acc, kacc = SF(2)
        tmpf, ktmpf = SF(3)
        rows = {n: SF(4 + i) for i, n in enumerate(("cum", "beta", "ecum", "becum"))}
        o_f = [SF(8 + h) for h in range(4)]
        misc, kmisc = SF(12)
        qT = [SB(h) for h in range(4)]
        kT = [SB(4 + h) for h in range(4)]
        vT = [SB(8 + h) for h in range(4)]
        zs = [SB(12 + h) for h in range(4)]
        CT = [SB(16 + h) for h in range(4)]
        QT = [SB(20 + h) for h in range(4)]
        for half in range(2):
            si = stgn[0] % 2
            stgn[0] += 1
            for q in range(2):
                s_ = 2 * half + q
                P.dma("sp", lambda e, s_=s_, q=q, si=si: e.dma_start(
                    out=stg[si][36 * q:36 * q + 36, :],
                    in_=I["state_delta_conv"][l, b0 + s_].rearrange("j (c p) -> (j c) p", p=128)),
                    writes=[("stg", si)], sem=("stg", si))
            tr(bank(7)[:, 0:128], stg[si][:, :], ident, [("stg", si), "cst"], [("ps", 7)])
            cp(convh[:, 72 * half:72 * half + 72], bank(7)[:, 0:72], [("ps", 7)], [kconvh])
        so_t = [stg[0], stg[1]]
        outst, koutst = SF(13)

        def consume(ci, db, w):
            E = EE[:, 0:515]
            Es = EE[:, 515:559].rearrange("p (s j) -> p s j", j=11)
            if ci < 12:
                cp(E[:, 0:3], carry_pc[:, l, ci, :], ["carry_pc"], [kEE])
                cp(Es[:, :, 0:3], convh[:, 0:144].rearrange("p (s j c) -> p c s j", j=3, c=12)[:, ci, :, :], [kconvh], [kEE])
                cp(E[:, 3:515], pall[:, 1024 * db:1024 * db + TP], [("ps", 2 * db)], [kEE], eng="act")
                cp(Es[:, :, 3:11], pall[:, 1024 * db + TP:1024 * db + T].rearrange("p (s j) -> p s j", j=8),
                   [("ps", 2 * db + 1)], [kEE])
                A = acc[:, 0:TP]
                As = acc[:, TP:T].rearrange("p (s j) -> p s j", j=8)
                for j in range(4):
                    wc = pc_(l, "conv", 12 * j + ci)
                    if j == 0:
                        ts(A, E[:, 0:512], wc, None, ALU.mult, None, [kEE, ("pcol", l)], [kacc])
                        ts(As, Es[:, :, 0:8], wc, None, ALU.mult, None, [kEE, ("pcol", l)], [kacc])
                    else:
                        stt(A, E[:, j:j + 512], wc, A, ALU.mult, ALU.add, [kEE, ("pcol", l), kacc], [kacc])
                        stt(As, Es[:, :, j:j + 8], wc, As, ALU.mult, ALU.add, [kEE, ("pcol", l), kacc], [kacc])
                cp(carry_pc[:, l, ci, :], E[:, 512:515], [kEE], ["carry_pc"])
                cp(outst[:, 0:144].rearrange("p (s j c) -> p c s j", j=3, c=12)[:, ci, :, :], Es[:, :, 8:11], [kEE], [koutst])
                h = ci % 4
                if ci >= 8:
                    act(vT[h][0][:], acc[:, 0:T], AF.Silu, [kacc], [vT[h][1]])
                else:
                    act(tmpf[:, 0:T], acc[:, 0:T], AF.Silu, [kacc], [ktmpf])
                    sq, ksq = SB(24)
                    act(sq[:], tmpf[:, 0:T], AF.Square, [ktmpf], [ksq])
                    for (o0, n, bk) in ((0, TP, 0), (TP, TS, 1)):
                        mm(pall[:, 2048 + o0:2048 + o0 + n], onesb, sq[:, o0:o0 + n], True, True, ["cstb", ksq],
                           [("ps", 4 + bk)])
                    rn, krn = SF(12 if False else 3), None
                    act(acc[:, 0:T], dbv(2), AF.Ln, dbk(2), [kacc], bias=1e-12)
                    act(acc[:, 0:T], acc[:, 0:T], AF.Exp, [kacc], [kacc], scale=-0.5)
                    dst = qT[h] if ci < 4 else kT[h]
                    sc_ = (128.0 ** -0.5) if ci < 4 else 1.0
                    stt(dst[0][:], tmpf[:, 0:T], sc_, acc[:, 0:T], ALU.mult, ALU.mult, [ktmpf, kacc], [dst[1]])
            else:
                h = ci - 12
                act(zs[h][0][:], dbv(db), AF.Silu, dbk(db), [zs[h][1]])
        proj_cols(l, OFF_C, 2048, consume)
        bet, kbet = rows["beta"]
        cum, kcum = rows["cum"]
        ecum, kecum = rows["ecum"]
        becum, kbecum = rows["becum"]

        def cons_b(ci, db, w):
            act(bet[0:4, 0:T], pall[0:4, 1024 * db:1024 * db + T], AF.Sigmoid, dbk(db), [kbet])
        proj_cols(l, OFF_C + 2048, 4, cons_b)

        def cons_a(ci, db, w):
            x_ = tmpf[0:4, 0:T]
            act(x_, pall[0:4, 1024 * db:1024 * db + T], AF.Identity, dbk(db) + ["gparm"], [ktmpf], bias=gparm[:, l, 1:2])
            ax = acc[0:4, 0:T]
            act(ax, x_, AF.Abs, [ktmpf], [kacc])
            act(ax, ax, AF.Exp, [kacc], [kacc], scale=-1.0)
            act(ax, ax, AF.Ln, [kacc], [kacc], bias=1.0)
            stt(x_, x_, 0.0, ax, ALU.max, ALU.add, [ktmpf, kacc], [ktmpf])
            act(misc[0:4, 0:1], gparm[:, l, 0:1], AF.Exp, ["gparm"], [kmisc])
            ts(x_, x_, misc[0:4, 0:1], -1.0, ALU.mult, ALU.mult, [ktmpf, kmisc], [ktmpf])
            OP("dve", lambda e: e.tensor_tensor_scan(cum[0:4, 0:T], cst[0:4, C_SCAN:C_SCAN + T], x_, 0.0, ALU.mult, ALU.add),
               reads=[ktmpf, "cst"], writes=[kcum])
            act(ecum[0:4, 0:T], cum[0:4, 0:T], AF.Exp, [kcum], [kecum])
            tt(becum[0:4, 0:T], ecum[0:4, 0:T], bet[0:4, 0:T], ALU.mult, [kecum, kbet], [kbecum])
        proj_cols(l, OFF_C + 2052, 4, cons_a)
        wCt = misc[:, 16:16 + 48].rearrange("p (h u) -> p h u", h=4)
        for h in range(4):
            sel = cst[0:4, C_SEL4 + 128 * h:C_SEL4 + 128 * h + 128]
            for (src, ks, dstp, srcT) in ((becum, kbecum, CT[h], kT[h]), (ecum, kecum, QT[h], qT[h])):
                for (o0, n, bk) in ((0, TP, 0), (TP, TS, 1)):
                    mm(pall[:, 3072 + o0:3072 + o0 + n], sel, src[0:4, o0:o0 + n], True, True, ["cst", ks], [("ps", 6 + bk)])
                tt(dstp[0][:], srcT[0][:], dbv(3), ALU.mult, [srcT[1]] + dbk(3), [dstp[1]])
                if src is ecum:
                    cp(wCt[:, h, 0:8], pall[:, 3072:3072 + TP].rearrange("p (u c) -> p u c", c=64)[:, :, 63], [("ps", 6)], [kmisc])
                    cp(wCt[:, h, 8:12], pall[:, 3072 + TP:3072 + T].rearrange("p (u c) -> p u c", c=8)[:, :, 7], [("ps", 7)], [kmisc])
        U1, kU1 = SF(1)
        U2, kU2 = SF(2)
        U3, kU3 = SF(3)
        P1a, Q1a, P1b, Q1b, Xt, A1T, K1, Rb, Xn, Sb = [SB(24 + i) for i in range(10)]
        sS, ksS = SF(12)
        units = [(64 * u, 64, u, None) for u in range(8)] + [(TP + 8 * s_, 8, 8 + s_, s_) for s_ in range(NSQ)]
        for (t0, C, ui, sidx) in units:
            if sidx is None:
                S_ap = Sgd[:, l, :, :]
                kS = ("Sgd", l)
            else:
                S_ap = o_f[0][0]
                stile, kS = SF(13 + sidx % 2) if False else (None, None)
            if sidx is not None:
                S_ap = SFt[14][:, 0:512].rearrange("p (h v) -> p h v", h=4)
                kS = ("sf", 14)
                P.dma("sp", lambda e, S_ap=S_ap, sidx=sidx: e.dma_start(
                    out=S_ap, in_=I["state_delta"][l, b0 + sidx].rearrange("h k v -> k h v")),
                    writes=[kS], sem=("sfd", 14))
            Sflat = S_ap.rearrange("p h v -> p (h v)")
            cp(Sb[0][:, 0:512], Sflat, [kS], [Sb[1]], eng="act")
            W4 = 4 * C
            v3 = lambda ap: ap.rearrange("p (h c) -> p h c", h=4)
            mm(pall[0:C, 1536:1540], cum[0:4, t0:t0 + C], ident[0:4, 0:4], True, True, [kcum, "cst"], [("ps", 3)])
            mm(pall[0:C, 1540:1544], bet[0:4, t0:t0 + C], ident[0:4, 0:4], True, True, [kbet, "cst"], [("ps", 3)])
            ts(misc[0:4, 4:8], ident[0:4, 0:4], cum[0:4, t0 + C - 1:t0 + C], None, ALU.mult, None, [kcum, "cst"], [kmisc])
            mm(pall[0:C, 1544:1548], cst[0:4, C_ONE:C_ONE + C], misc[0:4, 4:8], True, True, ["cst", kmisc], [("ps", 3)])
            colv = U1[0:C, 0:16]
            cp(colv[:, 0:12], pall[0:C, 1536:1548], [("ps", 3)], [kU1])
            tt(colv[:, 12:16], colv[:, 8:12], colv[:, 0:4], ALU.subtract, [kU1], [kU1])
            act(colv[:, 12:16], colv[:, 12:16], AF.Exp, [kU1], [kU1])
            for h in range(4):
                sel = cst[0:4, C_SEL4 + 128 * h:C_SEL4 + 128 * h + C]
                mm(pall[0:C, 2048 + C * h:2048 + C * h + C], sel, cum[0:4, t0:t0 + C], True, True, ["cst", kcum], [("ps", 4)])
                mm(pall[0:C, 2560 + C * h:2560 + C * h + C], sel, bet[0:4, t0:t0 + C], True, True, ["cst", kbet], [("ps", 5)])
            Ev = U1[0:C, 16:16 + W4]
            e1 = U1[0:C, 272:272 + W4]
            e2 = U2[0:C, 0:W4]
            tG = U2[0:C, 256:256 + W4]
            tt(v3(Ev), v3(pall[0:C, 2048:2048 + W4]), colv[:, 0:4].unsqueeze(2).to_broadcast([C, 4, C]), ALU.subtract,
               [("ps", 4), kU1], [kU1])
            ts(e1, Ev, 0.0, None, ALU.min, None, [kU1], [kU1])
            ts(e2, Ev, -1.0, 0.0, ALU.mult, ALU.min, [kU1], [kU2])
            act(e1, e1, AF.Exp, [kU1], [kU1])
            act(e2, e2, AF.Exp, [kU2], [kU2])
            mus = cst[0:C, C_MUS:C_MUS + C].unsqueeze(1).to_broadcast([C, 4, C])
            mls = cst[0:C, C_MLS:C_MLS + C].unsqueeze(1).to_broadcast([C, 4, C])
            mui = cst[0:C, C_MUI:C_MUI + C].unsqueeze(1).to_broadcast([C, 4, C])
            for h in range(4):
                mm(pall[0:C, 0 + C * h:0 + C * h + C], kT[h][0][:, t0:t0 + C], kT[h][0][:, t0:t0 + C], True, True,
                   [kT[h][1]], [("ps", 0)])
                mm(pall[0:C, 512 + C * h:512 + C * h + C], kT[h][0][:, t0:t0 + C], qT[h][0][:, t0:t0 + C], True, True,
                   [kT[h][1], qT[h][1]], [("ps", 1)])
            tt(v3(tG), v3(pall[0:C, 0:W4]), v3(e1), ALU.mult, [("ps", 0), kU1], [kU2])
            tt(v3(tG), v3(tG), mus, ALU.mult, [kU2, "cst"], [kU2])
            stt(P1a[0][0:C, 0:W4], tG, -1.0, pall[0:C, 2560:2560 + W4], ALU.mult, ALU.mult, [kU2, ("ps", 5)], [P1a[1]])
            tt(v3(tG), v3(pall[0:C, 0:W4]), v3(e2), ALU.mult, [("ps", 0), kU2], [kU2])
            tt(v3(tG), v3(tG), mls, ALU.mult, [kU2, "cst"], [kU2])
            stt(v3(Q1a[0][0:C, 0:W4]), v3(tG), -1.0, colv[:, 4:8].unsqueeze(2).to_broadcast([C, 4, C]), ALU.mult, ALU.mult,
                [kU2, kU1], [Q1a[1]])
            tt(v3(tG), v3(pall[0:C, 512:512 + W4]), v3(e1), ALU.mult, [("ps", 1), kU1], [kU2])
            tt(v3(A1T[0][0:C, 0:W4]), v3(tG), mui, ALU.mult, [kU2, "cst"], [A1T[1]])
            inv_chain(C, 4, P1a, Q1a, Xt, P1b, Q1b)
            for h in range(4):
                tr(pallb[0:C, 6 * 1024 + 128 * h:6 * 1024 + 128 * h + 128], kT[h][0][:, t0:t0 + C], identb, [kT[h][1], "cstb"],
                   [("ps", 6)])
                tr(pallb[0:C, 7 * 1024 + 128 * h:7 * 1024 + 128 * h + 128], vT[h][0][:, t0:t0 + C], identb, [vT[h][1], "cstb"],
                   [("ps", 7)])
            v3d = lambda ap: ap.rearrange("p (h d) -> p h d", h=4)
            tt(v3d(K1[0][0:C, 0:512]), v3d(pallb[0:C, 6 * 1024:6 * 1024 + 512]),
               colv[:, 12:16].unsqueeze(2).to_broadcast([C, 4, 128]), ALU.mult, [("ps", 6), kU1], [K1[1]])
            Rc = U3[0:C, 0:512]
            tt(v3d(Rc), v3d(pallb[0:C, 7 * 1024:7 * 1024 + 512]), colv[:, 4:8].unsqueeze(2).to_broadcast([C, 4, 128]),
               ALU.mult, [("ps", 7), kU1], [kU3])
            for h in range(4):
                mm(pall[0:C, 1536 + 0:1536 + 0] if False else pall[0:C, 2048 + 128 * h:2048 + 128 * h + 128],
                   CT[h][0][:, t0:t0 + C], Sb[0][:, 128 * h:128 * h + 128], True, True, [CT[h][1], Sb[1]], [("ps", 4)])
            tt(Rb[0][0:C, 0:512], Rc, pall[0:C, 2048:2560], ALU.subtract, [kU3, ("ps", 4)], [Rb[1]])
            for h in range(4):
                mm(pall[0:C, 2560 + 128 * h:2560 + 128 * h + 128], Xt[0][0:C, C * h:C * h + C], Rb[0][0:C, 128 * h:128 * h + 128],
                   True, True, [Xt[1], Rb[1]], [("ps", 5)])
            cp(Xn[0][0:C, 0:512], pall[0:C, 2560:3072], [("ps", 5)], [Xn[1]], eng="act")
            for h in range(4):
                mm(pall[:, 0 + C * h:0 + C * h + C], Sb[0][:, 128 * h:128 * h + 128], QT[h][0][:, t0:t0 + C], True, False,
                   [Sb[1], QT[h][1]], [("ps", 0)])
                mm(pall[:, 0 + C * h:0 + C * h + C], Xn[0][0:C, 128 * h:128 * h + 128], A1T[0][0:C, C * h:C * h + C], False, True,
                   [Xn[1], A1T[1]], [("ps", 0)])
            for h in range(4):
                cp(o_f[h][0][:, t0:t0 + C], pall[:, C * h:C * h + C], [("ps", 0)], [o_f[h][1]], eng="act")
            for h in range(4):
                mm(pall[:, 512 + 128 * h:512 + 128 * h + 128], K1[0][0:C, 128 * h:128 * h + 128], Xn[0][0:C, 128 * h:128 * h + 128],
                   True, True, [K1[1], Xn[1]], [("ps", 1)])
            for h in range(4):
                stt(S_ap[:, h, :], S_ap[:, h, :], wCt[:, h, ui:ui + 1], pall[:, 512 + 128 * h:512 + 128 * h + 128], ALU.mult, ALU.add,
                    [kS, kmisc, ("ps", 1)], [kS])
            if sidx is not None:
                P.dma("sp", lambda e, S_ap=S_ap, sidx=sidx: e.dma_start(
                    out=O["s_delta"][l, b0 + sidx].rearrange("h k v -> k h v"), in_=S_ap),
                    reads=[kS], sem=("sfd", 14))
        if st == NST - 1:
            P.dma("sp", lambda e: e.dma_start(out=O["p_delta"][l].rearrange("h k v -> k h v"), in_=Sgd[:, l, :, :]),
                  reads=[("Sgd", l)], sem=("pdel", l))
            si = stgn[0] % 2
            stgn[0] += 1
            cp(stg[si][:, 0:36].rearrange("p (j c) -> p c j", c=12), carry_pc[:, l, :, :], ["carry_pc"], [("stg", si)])
            tr(bank(7)[:, 0:128], stg[si][:, :], ident, [("stg", si), "cst"], [("ps", 7)])
            cp(tmpf[:, 0:128], bank(7)[:, 0:128], [("ps", 7)], [ktmpf])
            P.dma("sp", lambda e: e.dma_start(out=O["p_delta_conv"][l].rearrange("j (c p) -> (j c) p", p=128), in_=tmpf[0:36, 0:128]),
                  reads=[ktmpf], sem=("sfd", 3))
        for half in range(2):
            si = stgn[0] % 2
            stgn[0] += 1
            cp(stg[si][:, 0:72], outst[:, 72 * half:72 * half + 72], [koutst], [("stg", si)])
            tr(bank(7)[:, 0:128], stg[si][:, :], ident, [("stg", si), "cst"], [("ps", 7)])
            cp(acc[:, 0:128], bank(7)[:, 0:128], [("ps", 7)], [kacc])
            for q in range(2):
                s_ = 2 * half + q
                P.dma("sp", lambda e, s_=s_, q=q: e.dma_start(
                    out=O["s_delta_conv"][l, b0 + s_].rearrange("j (c p) -> (j c) p", p=128), in_=acc[36 * q:36 * q + 36, 0:128]),
                    reads=[kacc], sem=("sfd", 2))
        ocT = [SB(20 + h) for h in range(4)]
        for h in range(4):
            sq, ksq = SB(24)
            act(sq[:], o_f[h][0][:, 0:T], AF.Square, [o_f[h][1]], [ksq])
            for (o0, n, bk) in ((0, TP, 0), (TP, TS, 1)):
                mm(pall[:, 2048 + o0:2048 + o0 + n], onesb, sq[:, o0:o0 + n], True, True, ["cstb", ksq], [("ps", 4 + bk)])
            act(acc[:, 0:T], dbv(2), AF.Ln, dbk(2), [kacc], bias=1e-6, scale=1.0 / 128)
            act(acc[:, 0:T], acc[:, 0:T], AF.Exp, [kacc], [kacc], scale=-0.5)
            stt(tmpf[:, 0:T], o_f[h][0][:, 0:T], pc_(l, "nrm", 0), acc[:, 0:T], ALU.mult, ALU.mult,
                [o_f[h][1], ("pcol", l), kacc], [ktmpf])
            tt(ocT[h][0][:], tmpf[:, 0:T], zs[h][0][:], ALU.mult, [ktmpf, zs[h][1]], [ocT[h][1]])
        merge_branch(l, 2, lambda c: ocT[c][0][:], [ocT[c][1] for c in range(4)], 4, 1024)

    def branch_rwkv(l, st):
        b0 = 4 * st
        hist, khist = SF(0)
        paE, kpaE = SF(1)
        tA, ktA = SF(2)
        tB, ktB = SF(3)
        cs, kcs = SF(4)
        coef = [SF(5), SF(6), SF(7), SF(13)]
        yT = [SF(8 + hp) for hp in range(4)]
        misc, kmisc = SF(12)
        tC, ktC = SF(14)
        QT = [SB(hp) for hp in range(4)]
        CT = [SB(4 + hp) for hp in range(4)]
        BT = [SB(8 + hp) for hp in range(4)]
        KT = [SB(12 + hp) for hp in range(4)]
        vT = [SB(16 + hp) for hp in range(4)]
        x12b, kx12 = SB(20)
        sgb, ksgb = SB(21)
        waup, kwaup = SB(22)
        gup, kgup = SB(23)
        WC = misc[:, 0:48].rearrange("p (h u) -> p h u", h=4)
        P.dma("pool", lambda e: e.dma_start(out=waup[0:64, 0:512], in_=I["rwkv_w_up"][l]), writes=[kwaup], sem=("sbd", 22))
        P.dma("pool", lambda e: e.dma_start(out=waup[64:128, 0:512], in_=I["rwkv_a_up"][l]), writes=[kwaup], sem=("sbd", 22))
        P.dma("pool", lambda e: e.dma_start(out=gup[:, 0:512], in_=I["rwkv_g_up"][l]), writes=[kgup], sem=("sbd", 23))
        si = stgn[0] % 2
        stgn[0] += 1
        for s_ in range(NSQ):
            P.dma("sp", lambda e, s_=s_, si=si: e.dma_start(out=stg[si][14 * s_:14 * s_ + 14, :], in_=I["state_rwkv_shift"][l, b0 + s_]),
                  writes=[("stg", si)], sem=("stg", si))
        tr(bank(7)[:, 0:128], stg[si][:, :], ident, [("stg", si), "cst"], [("ps", 7)])
        cp(hist[:, 0:56], bank(7)[:, 0:56], [("ps", 7)], [khist])

        def shifted(c, db):
            cp(paE[:, 0:1], carry_pa[:, l, c:c + 1], ["carry_pa"], [kpaE])
            cp(paE[:, 1:1 + T], dbv(db), dbk(db), [kpaE], eng="act")
            Sx = paE[:, 513:545].rearrange("p (s j) -> p s j", j=8)
            d = tB[:, 0:T]
            ds_ = tB[:, TP:T].rearrange("p (s j) -> p s j", j=8)
            tt(tB[:, 0:TP], paE[:, 0:512], paE[:, 1:513], ALU.subtract, [kpaE], [ktB])
            tt(ds_[:, :, 1:8], Sx[:, :, 0:7], Sx[:, :, 1:8], ALU.subtract, [kpaE], [ktB])
            hv = hist[:, 0:56].rearrange("p (s c) -> p c s", c=14)[:, c, :]
            tt(ds_[:, :, 0], hv, Sx[:, :, 0], ALU.subtract, [kpaE, khist], [ktB])
            stt(tA[:, 0:T], d, pc_(l, "mu", c), paE[:, 1:1 + T], ALU.mult, ALU.add, [ktB, kpaE, ("pcol", l)], [ktA])
            cp(carry_pa[:, l, c:c + 1], paE[:, 512:513], [kpaE], ["carry_pa"])
            cp(hist[:, 64:120].rearrange("p (s c) -> p c s", c=14)[:, c, :], Sx[:, :, 7], [kpaE], [khist])

        def cons_w(ci, db, w):
            shifted(12 + ci, db)
            if ci == 0:
                act(x12b[0:64, :], tA[0:64, 0:T], AF.Tanh, [ktA], [kx12])
                cp(x12b[64:128, :], tA[64:128, 0:T], [ktA], [kx12])
            else:
                act(sgb[:], tA[:, 0:T], AF.Sigmoid, [ktA], [ksgb])
        proj_cols(l, 1536, 256, cons_w)

        for hp in range(4):
            for (o0, n, bk) in ((0, TP, 0), (TP, TS, 1)):
                mm(pall[:, 2048 + o0:2048 + o0 + n], waup[0:64, 128 * hp:128 * hp + 128], x12b[0:64, o0:o0 + n], True, True,
                   [kwaup, kx12], [("ps", 4 + bk)])
                mm(pall[:, 3072 + o0:3072 + o0 + n], waup[64:128, 128 * hp:128 * hp + 128], x12b[64:128, o0:o0 + n], True, True,
                   [kwaup, kx12], [("ps", 6 + bk)])
            sig = tC[:, 0:T]
            act(sig, dbv(2), AF.Sigmoid, dbk(2) + [("pcol", l)], [ktC], bias=pc_(l, "w0", hp))
            OP("dve", lambda e: e.tensor_tensor_scan(cs[:, 0:T], cst[:, C_SCAN:C_SCAN + T], sig, 0.0, ALU.mult, ALU.add),
               reads=[ktC, "cst"], writes=[kcs])
            e1, ke1 = SB(24)
            e2, ke2 = SB(25)
            e3, ke3 = SB(26)
            a_t, ka = SB(27)
            e1f, ke1f = SF(3)
            tt(sig, cs[:, 0:T], sig, ALU.subtract, [kcs, ktC], [ktC])
            act(e3[:], sig, AF.Exp, [ktC], [ke3], scale=-C0)
            act(e1f[:, 0:T], cs[:, 0:T], AF.Exp, [kcs], [ke1f], scale=-C0)
            cp(e1[:], e1f[:, 0:T], [ke1f], [ke1])
            act(e2[:], cs[:, 0:T], AF.Exp, [kcs], [ke2], scale=C0)
            cp(WC[:, hp, 0:8], e1f[:, 0:TP].rearrange("p (u c) -> p u c", c=64)[:, :, 63], [ke1f], [kmisc])
            cp(WC[:, hp, 8:12], e1f[:, TP:T].rearrange("p (u c) -> p u c", c=8)[:, :, 7], [ke1f], [kmisc])
            act(a_t[:], dbv(3), AF.Sigmoid, dbk(3) + [("pcol", l)], [ka], bias=pc_(l, "a0", hp))
            rb_, krb = SB(28)

            def cons_r(ci, db, w, hp=hp, e1=e1, ke1=ke1, rb_=rb_, krb=krb):
                shifted(hp, db)
                cp(rb_[:], tA[:, 0:T], [ktA], [krb], eng="act")
                tt(QT[hp][0][:], tA[:, 0:T], e1[:], ALU.mult, [ktA, ke1], [QT[hp][1]])
            proj_cols(l, 128 * hp, 128, cons_r)

            def cons_k(ci, db, w, hp=hp, e2=e2, ke2=ke2, e3=e3, ke3=ke3, a_t=a_t, ka=ka, rb_=rb_, krb=krb):
                shifted(4 + hp, db)
                k_ = tA[:, 0:T]
                kkr, kkkr = SF(3)
                ts(kkr[:, 0:T], k_, pc_(l, "k_k", hp), None, ALU.mult, None, [ktA, ("pcol", l)], [kkkr])
                sq, ksq = SB(29)
                act(sq[:], kkr[:, 0:T], AF.Square, [kkkr], [ksq])
                for (o0, n, bk) in ((0, TP, 0), (TP, TS, 1)):
                    mm(pall[:, 2048 + o0:2048 + o0 + n], blkb, sq[:, o0:o0 + n], True, True, ["cstb", ksq], [("ps", 4 + bk)])
                act(tC[:, 0:T], dbv(2), AF.Ln, dbk(2), [ktC], bias=1e-12)
                act(tC[:, 0:T], tC[:, 0:T], AF.Exp, [ktC], [ktC], scale=-0.5)
                tt(kkr[:, 0:T], kkr[:, 0:T], tC[:, 0:T], ALU.mult, [kkkr, ktC], [kkkr])
                tt(CT[hp][0][:], kkr[:, 0:T], e3[:], ALU.mult, [kkkr, ke3], [CT[hp][1]])
                tt(kkr[:, 0:T], kkr[:, 0:T], a_t[:], ALU.mult, [kkkr, ka], [kkkr])
                tt(BT[hp][0][:], kkr[:, 0:T], e2[:], ALU.mult, [kkkr, ke2], [BT[hp][1]])
                ts(tC[:, 0:T], a_t[:], pc_(l, "k_a", hp), pc_(l, "omka", hp), ALU.mult, ALU.add, [ka, ("pcol", l)], [ktC])
                tt(k_, k_, tC[:, 0:T], ALU.mult, [ktA, ktC], [ktA])
                tt(KT[hp][0][:], k_, e2[:], ALU.mult, [ktA, ke2], [KT[hp][1]])
                stt(sq[:], k_, pc_(l, "r_k", hp), rb_[:], ALU.mult, ALU.mult, [ktA, ("pcol", l), krb], [ksq])
                for (o0, n, bk) in ((0, TP, 0), (TP, TS, 1)):
                    mm(pall[:, 2048 + o0:2048 + o0 + n], blkb, sq[:, o0:o0 + n], True, True, ["cstb", ksq], [("ps", 4 + bk)])
                cp(coef[hp][0][:, 0:T], dbv(2), dbk(2), [coef[hp][1]], eng="act")
            proj_cols(l, 512 + 128 * hp, 128, cons_k)

            def cons_v(ci, db, w, hp=hp):
                shifted(8 + hp, db)
                cp(vT[hp][0][:], tA[:, 0:T], [ktA], [vT[hp][1]], eng="act")
                tt(coef[hp][0][:, 0:T], coef[hp][0][:, 0:T], tA[:, 0:T], ALU.mult, [coef[hp][1], ktA], [coef[hp][1]])
            proj_cols(l, 1024 + 128 * hp, 128, cons_v)
        si = stgn[0] % 2
        stgn[0] += 1
        cp(stg[si][:, 0:56], hist[:, 64:120], [khist], [("stg", si)])
        if st == NST - 1:
            cp(stg[si][:, 56:70], carry_pa[:, l, :], ["carry_pa"], [("stg", si)])
        tr(bank(7)[:, 0:128], stg[si][:, :], ident, [("stg", si), "cst"], [("ps", 7)])
        cp(tB[:, 0:128], bank(7)[:, 0:128], [("ps", 7)], [ktB])
        for s_ in range(NSQ):
            P.dma("sp", lambda e, s_=s_: e.dma_start(out=O["s_rwkv_shift"][l, b0 + s_], in_=tB[14 * s_:14 * s_ + 14, 0:128]),
                  reads=[ktB], sem=("sfd", 3))
        if st == NST - 1:
            P.dma("sp", lambda e: e.dma_start(out=O["p_rwkv_shift"][l], in_=tB[56:70, 0:128]), reads=[ktB], sem=("sfd", 3))
        P1a, Q1a, P1b, Q1b, Xt, AqbT, AqkT, BTm, K12, K1t, K2t, Vt, Rb, Un, Sb = [SB(24 + i) for i in range(15)]
        Sst, kSst = SF(3)
        units = [(64 * u, 64, u, None) for u in range(8)] + [(TP + 8 * s_, 8, 8 + s_, s_) for s_ in range(NSQ)]
        st_map = "(a b c) v k -> b v a c k"

        def state_out(S_src, kS_src, dst4):
            for a in range(2):
                tr(bank(3)[:, 128 * a:128 * a + 128], S_src[:, 2 * a:2 * a + 2, :].rearrange("p h v -> p (h v)"), ident,
                   [kS_src, "cst"], [("ps", 3)])
            cp(tC[:, 0:256], bank(3)[:, 0:256], [("ps", 3)], [ktC])
            for b_ in range(2):
                for a_ in range(2):
                    P.dma("sp", lambda e, b_=b_, a_=a_: e.dma_start(
                        out=dst4.rearrange(st_map, a=2, b=2, c=2)[b_][:, a_],
                        in_=tC[64 * b_:64 * b_ + 64, 128 * a_:128 * a_ + 128].rearrange("p (c k) -> p c k", c=2)),
                        reads=[ktC], sem=("sfd", 14))

        for (t0, C, ui, sidx) in units:
            if sidx is None:
                S_ap = Srw[:, l, :, :]
                kS = ("Srw", l)
            else:
                S_ap = Sst[:, 256:512].rearrange("p (h v) -> p h v", h=4)
                kS = kSst
                for b_ in range(2):
                    for a_ in range(2):
                        P.dma("sp", lambda e, b_=b_, a_=a_, sidx=sidx: e.dma_start(
                            out=Sst[64 * b_:64 * b_ + 64, 128 * a_:128 * a_ + 128].rearrange("p (c k) -> p c k", c=2),
                            in_=I["state_rwkv"][l, b0 + sidx].rearrange(st_map, a=2, b=2, c=2)[b_][:, a_]),
                            writes=[kSst], sem=("sfd", 3))
                for a in range(2):
                    tr(bank(3)[:, 128 * a:128 * a + 128], Sst[:, 128 * a:128 * a + 128], ident, [kSst, "cst"], [("ps", 3)])
                cp(Sst[:, 256:512], bank(3)[:, 0:256], [("ps", 3)], [kSst])
            cp(Sb[0][:, 0:256], S_ap.rearrange("p h v -> p (h v)"), [kS], [Sb[1]], eng="act")
            W8 = 8 * C
            U = slice(t0, t0 + C)
            hs = lambda h: 4 * (h % 2) + h // 2
            pv4 = lambda k: pall[0:C, 1024 * k:1024 * k + 1024].rearrange("p (a x) -> p a x", a=2)[:, :, 0:4 * C].rearrange("p a (h c) -> p a h c", h=4)
            sv4 = lambda t_: t_[0:C, 0:W8].rearrange("p (a h c) -> p a h c", a=2, h=4)
            mk4 = lambda c0: cst[0:C, c0:c0 + C].unsqueeze(1).unsqueeze(1).to_broadcast([C, 2, 4, C])

            def gram(k, lt, rt):
                for h in range(8):
                    hp, hl = h // 2, h % 2
                    pr = slice(64 * hl, 64 * hl + 64)
                    mm(pall[0:C, 1024 * k + 512 * hl + C * hp:1024 * k + 512 * hl + C * hp + C], lt[hp][0][pr, U], rt[hp][0][pr, U],
                       True, True, [lt[hp][1], rt[hp][1]], [("ps", 2 * k + hl)])
            gram(0, BT, CT)
            gram(1, CT, BT)
            gram(2, KT, CT)
            gram(3, BT, QT)
            tt(sv4(P1a[0]), pv4(0), mk4(C_NMUS), ALU.mult, [("ps", 0), ("ps", 1), "cst"], [P1a[1]])
            tt(sv4(Q1a[0]), pv4(1), mk4(C_NMLS), ALU.mult, [("ps", 2), ("ps", 3), "cst"], [Q1a[1]])
            tt(sv4(BTm[0]), pv4(2), mk4(C_MUS), ALU.mult, [("ps", 4), ("ps", 5), "cst"], [BTm[1]])
            tt(sv4(AqbT[0]), pv4(3), mk4(C_MUI), ALU.mult, [("ps", 6), ("ps", 7), "cst"], [AqbT[1]])
            gram(0, KT, QT)
            tt(sv4(AqkT[0]), pv4(0), mk4(C_MUI), ALU.mult, [("ps", 0), ("ps", 1), "cst"], [AqkT[1]])
            inv_chain(C, 8, P1a, Q1a, Xt, P1b, Q1b)
            K12v = K12[0][:, 0:8 * C].rearrange("p (h x c) -> p h x c", h=4, x=2)
            for hp in range(4):
                act(K12v[:, hp, 0, :], BT[hp][0][:, U], AF.Copy, [BT[hp][1], kmisc], [K12[1]], scale=WC[:, hp, ui:ui + 1])
                act(K12v[:, hp, 1, :], KT[hp][0][:, U], AF.Copy, [KT[hp][1], kmisc], [K12[1]], scale=WC[:, hp, ui:ui + 1])
            for hp in range(4):
                tr(pallb[0:C, 5 * 1024 + 128 * hp:5 * 1024 + 128 * hp + 128], K12v[:, hp, 0, :], identb, [K12[1], "cstb"], [("ps", 5)])
                tr(pallb[0:C, 6 * 1024 + 128 * hp:6 * 1024 + 128 * hp + 128], K12v[:, hp, 1, :], identb, [K12[1], "cstb"], [("ps", 6)])
                tr(pallb[0:C, 7 * 1024 + 128 * hp:7 * 1024 + 128 * hp + 128], vT[hp][0][:, U], identb, [vT[hp][1], "cstb"], [("ps", 7)])
            cp(K1t[0][0:C, 0:512], pallb[0:C, 5 * 1024:5 * 1024 + 512], [("ps", 5)], [K1t[1]], eng="act")
            cp(K2t[0][0:C, 0:512], pallb[0:C, 6 * 1024:6 * 1024 + 512], [("ps", 6)], [K2t[1]], eng="act")
            cp(Vt[0][0:C, 0:512], pallb[0:C, 7 * 1024:7 * 1024 + 512], [("ps", 7)], [Vt[1]], eng="act")
            for h in range(8):
                hp, hl = h // 2, h % 2
                pr = slice(64 * hl, 64 * hl + 64)
                mm(pall[0:C, 64 * hs(h):64 * hs(h) + 64], BTm[0][0:C, C * hs(h):C * hs(h) + C], Vt[0][0:C, 64 * h:64 * h + 64], True, True,
                   [BTm[1], Vt[1]], [("ps", 0)])
                mm(pall[0:C, 512 + 512 * hl + 64 * hp:512 + 512 * hl + 64 * hp + 64], CT[hp][0][pr, U], Sb[0][pr, 64 * hp:64 * hp + 64], True, True,
                   [CT[hp][1], Sb[1]], [("ps", 1 + hl)])
            act(tC[0:C, 0:512], pall[0:C, 0:512], AF.Copy, [("ps", 0)], [ktC], scale=-1.0)
            tt(Rb[0][0:C, 0:512].rearrange("p (a x) -> p a x", a=2), tC[0:C, 0:512].rearrange("p (a x) -> p a x", a=2),
               pall[0:C, 512:1536].rearrange("p (a x) -> p a x", a=2)[:, :, 0:256], ALU.subtract, [ktC, ("ps", 1), ("ps", 2)], [Rb[1]])
            for h in range(8):
                mm(pall[0:C, 1536 + 64 * hs(h):1536 + 64 * hs(h) + 64], Xt[0][0:C, C * hs(h):C * hs(h) + C],
                   Rb[0][0:C, 64 * hs(h):64 * hs(h) + 64], True, True, [Xt[1], Rb[1]], [("ps", 3)])
            cp(Un[0][0:C, 0:512], pall[0:C, 1536:2048], [("ps", 3)], [Un[1]], eng="act")
            for h in range(8):
                hp, hl = h // 2, h % 2
                pr = slice(64 * hl, 64 * hl + 64)
                mm(pall[pr, 2048 + 512 * hl + C * hp:2048 + 512 * hl + C * hp + C], Sb[0][pr, 64 * hp:64 * hp + 64], QT[hp][0][pr, U],
                   True, True, [Sb[1], QT[hp][1]], [("ps", 4 + hl)])
                oo = pall[pr, 3072 + C * hp:3072 + C * hp + C]
                mm(oo, Un[0][0:C, 64 * hs(h):64 * hs(h) + 64], AqbT[0][0:C, C * hs(h):C * hs(h) + C], True, False, [Un[1], AqbT[1]], [("ps", 6)])
                mm(oo, Vt[0][0:C, 64 * h:64 * h + 64], AqkT[0][0:C, C * hs(h):C * hs(h) + C], False, True, [Vt[1], AqkT[1]], [("ps", 6)])
                so = pall[pr, 3584 + 64 * hp:3584 + 64 * hp + 64]
                mm(so, K1t[0][0:C, 64 * h:64 * h + 64], Un[0][0:C, 64 * hs(h):64 * hs(h) + 64], True, False, [K1t[1], Un[1]], [("ps", 7)])
                mm(so, K2t[0][0:C, 64 * h:64 * h + 64], Vt[0][0:C, 64 * h:64 * h + 64], False, True, [K2t[1], Vt[1]], [("ps", 7)])
            cp(tC[:, 0:4 * C], pall[:, 3072:3072 + 4 * C], [("ps", 6)], [ktC], eng="act")
            for hp in range(4):
                for hl in range(2):
                    pr = slice(64 * hl, 64 * hl + 64)
                    tt(yT[hp][0][pr, U], tC[pr, C * hp:C * hp + C], pall[pr, 2048 + 512 * hl + C * hp:2048 + 512 * hl + C * hp + C], ALU.add,
                       [ktC, ("ps", 4 + hl)], [yT[hp][1]])
                stt(S_ap[:, hp, :], S_ap[:, hp, :], WC[:, hp, ui:ui + 1], pall[:, 3584 + 64 * hp:3584 + 64 * hp + 64], ALU.mult, ALU.add,
                    [kS, kmisc, ("ps", 7)], [kS])
            if sidx is not None:
                state_out(S_ap, kS, O["s_rwkv"][l, b0 + sidx])
        if st == NST - 1:
            state_out(Srw[:, l, :, :], ("Srw", l), O["p_rwkv"][l])
        oaT = [SB(24 + hp) for hp in range(4)]
        for hp in range(4):
            yb, kyb = SB(28)
            ysq_, kysq = SB(29)
            cp(yb[:], yT[hp][0][:, 0:T], [yT[hp][1]], [kyb], eng="act")
            act(ysq_[:], yT[hp][0][:, 0:T], AF.Square, [yT[hp][1]], [kysq])
            for (o0, n, bk) in ((0, TP, 0), (TP, TS, 1)):
                mm(pall[:, 2048 + o0:2048 + o0 + n], blkb, yb[:, o0:o0 + n], True, True, ["cstb", kyb], [("ps", 4 + bk)])
                mm(pall[:, 3072 + o0:3072 + o0 + n], blkb, ysq_[:, o0:o0 + n], True, True, ["cstb", kysq], [("ps", 6 + bk)])
                mm(pall[:, 0 + o0:0 + o0 + n], gup[:, 128 * hp:128 * hp + 128], sgb[:, o0:o0 + n], True, True, [kgup, ksgb],
                   [("ps", 0 + bk)])
            mean = tA[:, 0:T]
            var = tB[:, 0:T]
            act(mean, dbv(2), AF.Copy, dbk(2), [ktA], scale=1.0 / 64)
            tt(var, mean, mean, ALU.mult, [ktA], [ktB])
            stt(var, dbv(3), 1.0 / 64, var, ALU.mult, ALU.subtract, dbk(3) + [ktB], [ktB])
            act(var, var, AF.Ln, [ktB], [ktB], bias=64e-5)
            act(var, var, AF.Exp, [ktB], [ktB], scale=-0.5)
            tt(mean, yT[hp][0][:, 0:T], mean, ALU.subtract, [yT[hp][1], ktA], [ktA])
            tt(mean, mean, var, ALU.mult, [ktA, ktB], [ktA])
            act(mean, mean, AF.Identity, [ktA, ("pcol", l)], [ktA], bias=pc_(l, "gn_b", hp), scale=pc_(l, "gn_w", hp))
            tt(mean, mean, coef[hp][0][:, 0:T], ALU.add, [ktA, coef[hp][1]], [ktA])
            tt(oaT[hp][0][:], mean, dbv(0), ALU.mult, [ktA] + dbk(0), [oaT[hp][1]])
        merge_branch(l, 0, lambda c: oaT[c][0][:], [oaT[c][1] for c in range(4)], 4, 0)

    def mixing(l, st):
        for m in range(KC):
            memset(mg[:, m, :], 0.0, [("mg", m)], eng="pool" if False else "dve")
        branch_pool(l, st)
        if "m" in ACTIVE:
            branch_attn(l, st)
        if "c" in ACTIVE:
            branch_gdn(l, st)
        if "a" in ACTIVE:
            branch_rwkv(l, st)
        mgb = [SB(m) for m in range(KC)]
        for m in range(KC):
            cp(mgb[m][0][:], mg[:, m, :], [("mg", m)], [mgb[m][1]], eng="act")
        out_proj_ln(l, 1, I["w_out"][l], D, lambda k: mgb[k][0][:], [mgb[k][1] for k in range(KC)], 1.0 / ALPHA)

    attn_prep()
    nst = 1 if "one" in DBG else NST
    for st in range(nst):
        for half in range(2):
            for i in range(2):
                ti = 2 * half + i
                for hh in range(2):
                    P.dma("sp", lambda e, i=i, ti=ti, st=st, hh=hh: e.dma_start(
                        out=xin.cols(i, 512 * hh, 512 * hh + 512),
                        in_=I["x_prompt"][st * TP + 128 * ti:st * TP + 128 * ti + 128, 512 * hh:512 * hh + 512]),
                        writes=[xin.keys(i)[hh]], sem=("xin", i, hh))
            for k in range(KC):
                b = k % 2
                for i in range(2):
                    tr(bank(b)[:, 128 * i:128 * i + 128], xin.cols(i, 128 * k, 128 * k + 128), ident,
                       xin.keys(i) + ["cst"], [("ps", b)])
                cp(xf[:, k, 256 * half:256 * half + 256], bank(b)[:, 0:256], [("ps", b)], [("xf", k)])
                cp(xb[:, k, 256 * half:256 * half + 256], xf[:, k, 256 * half:256 * half + 256], [("xf", k)], [("xb", k)],
                   eng="act")
        for hh in range(2):
            P.dma("sp", lambda e, st=st, hh=hh: e.dma_start(
                out=xin_s.cols(0, 512 * hh, 512 * hh + 512, slice(0, 96)),
                in_=I["x_prompt"][st * TP + 416:st * TP + 512, 512 * hh:512 * hh + 512]),
                writes=[xin_s.keys(0)[hh]], sem=("xin_s", hh))
            P.dma("sp", lambda e, st=st, hh=hh: e.dma_start(
                out=xin_s.cols(0, 512 * hh, 512 * hh + 512, slice(96, 128)),
                in_=I["x_sample"][st * TS:st * TS + TS, 512 * hh:512 * hh + 512]),
                writes=[xin_s.keys(0)[hh]], sem=("xin_s", hh))
        for k in range(KC):
            b = 2 + k % 2
            tr(bank(b)[:, 0:128], xin_s.cols(0, 128 * k, 128 * k + 128), ident, xin_s.keys(0) + ["cst"], [("ps", b)])
            cp(xf[:, k, TP:T], bank(b)[:, 96:128], [("ps", b)], [("xf", k)])
            cp(xb[:, k, TP:T], xf[:, k, TP:T], [("xf", k)], [("xb", k)], eng="act")
        for l in range(NLAYERS):
            ffn_ln(l, I["ffn1_w_in"], I["ffn1_w_out"], 0)
            if "nomix" not in DBG:
                mixing(l, st)
            if "noffn2" not in DBG:
                ffn_ln(l, I["ffn2_w_in"], I["ffn2_w_out"], 2)
        for half in range(2):
            for i in range(2):
                ti = 2 * half + i
                for hh in range(2):
                    b = hh
                    for kk in range(4):
                        k = 4 * hh + kk
                        tr(bank(b)[:, 128 * kk:128 * kk + 128], xf[:, k, 128 * ti:128 * ti + 128], ident,
                           [("xf", k), "cst"], [("ps", b)])
                    cp(xin.cols(i, 512 * hh, 512 * hh + 512), bank(b), [("ps", b)], [xin.keys(i)[hh]], eng="dve" if hh == 0 else "act")
                    P.dma("sp", lambda e, i=i, ti=ti, st=st, hh=hh: e.dma_start(
                        out=O["y_prompt"][st * TP + 128 * ti:st * TP + 128 * ti + 128, 512 * hh:512 * hh + 512],
                        in_=xin.cols(i, 512 * hh, 512 * hh + 512)),
                        reads=[xin.keys(i)[hh]], sem=("xin", i, hh))
        for hh in range(2):
            b = 2 + hh
            for kk in range(4):
                k = 4 * hh + kk
                tr(bank(b)[:, 128 * kk:128 * kk + 128], xf[:, k, T - 128:T], ident, [("xf", k), "cst"], [("ps", b)])
            cp(xin_s.cols(0, 512 * hh, 512 * hh + 512, slice(96, 128)), bank(b)[96:128, :], [("ps", b)], [xin_s.keys(0)[hh]])
            P.dma("sp", lambda e, st=st, hh=hh: e.dma_start(
                out=O["y_sample"][st * TS:st * TS + TS, 512 * hh:512 * hh + 512],
                in_=xin_s.cols(0, 512 * hh, 512 * hh + 512, slice(96, 128))),
                reads=[xin_s.keys(0)[hh]], sem=("xin_s", hh))

    P.final_wait_all("sp")
    P.emit(es)
    es.close()
    return nc, P


_CACHE = {}


def make_in_maps(inputs, cores):
    consts = make_consts()
    f = lambda a: np.ascontiguousarray(a, dtype=np.float32)
    shared = {}
    for name, shp in IN_SPECS:
        if name in ("x_prompt", "x_sample", "mem_prompt", "consts") or name.startswith("cache_") or name.startswith("state_"):
            continue
        shared[name] = f(np.asarray(inputs[name]).reshape(shp))
    in_maps = []
    for c in cores:
        d = dict(shared)
        sl = slice(16 * c, 16 * c + 16)
        d["x_prompt"] = f(inputs["x_prompt"][c])
        d["x_sample"] = f(np.asarray(inputs["x_sample"][sl]).reshape(128, D))
        d["mem_prompt"] = f(inputs["mem_prompt"][c])
        d["cache_mem_k"] = f(np.asarray(inputs["cache_mem_k"][:, sl]).reshape(DEPTH, 16, 256, 256))
        d["cache_mem_v"] = f(np.asarray(inputs["cache_mem_v"][:, sl]).reshape(DEPTH, 16, 256, 256))
        d["state_rwkv"] = f(inputs["state_rwkv"][:, sl])
        d["state_rwkv_shift"] = f(np.asarray(inputs["state_rwkv_shift"][:, sl]).reshape(DEPTH, 16, 14, 128))
        d["state_pool"] = f(inputs["state_pool"][:, sl])
        d["state_delta"] = f(inputs["state_delta"][:, sl])
        d["state_delta_conv"] = f(inputs["state_delta_conv"][:, sl])
        d["consts"] = consts
        in_maps.append(d)
    return in_maps


def kernel(**inputs):
    n = 8
    if "nc" not in _CACHE:
        _CACHE["nc"] = build()[0]
    nc = _CACHE["nc"]
    in_maps = make_in_maps(inputs, range(n))
    res = run_bass_kernel_spmd(nc, in_maps, core_ids=list(range(n)))
    R_ = res.results
    cat = lambda name, shp: np.concatenate([R_[c][name].reshape(shp) for c in range(n)], axis=1)
    out = (
        np.stack([R_[c]["y_prompt"] for c in range(n)], 0),
        np.concatenate([R_[c]["y_sample"].reshape(16, 8, D) for c in range(n)], 0),
        cat("p_rwkv", (DEPTH, 1, 8, 64, 64)), cat("p_rwkv_shift", (DEPTH, 1, 1792)), cat("p_pool", (DEPTH, 1, 15, 512)),
        cat("p_delta", (DEPTH, 1, 4, 128, 128)), cat("p_delta_conv", (DEPTH, 1, 3, 1536)),
        cat("p_mem_k", (DEPTH, 1, 256, 4, 64)), cat("p_mem_v", (DEPTH, 1, 256, 4, 64)),
        cat("s_rwkv", (DEPTH, 16, 8, 64, 64)), cat("s_rwkv_shift", (DEPTH, 16, 1792)), cat("s_pool", (DEPTH, 16, 15, 512)),
        cat("s_delta", (DEPTH, 16, 4, 128, 128)), cat("s_delta_conv", (DEPTH, 16, 3, 1536)),
    )
    return out
```
